# Optimizing a Trainium2 kernel written in Bass

```python
import math
import jax, jax.numpy as jnp
from jax import lax
import numpy as np

D_MODEL = 1024
BATCH = 4
SEQ = 8192
DEPTH = 2

GRID_W = 64
CTX_LEN = 256
EPS = 1e-6
N_MOD = 6
N_BRANCH = 3
CONV_DIM = 1024
CONV_K = 31
DN_HEADS = 8
DN_DK = 128
DN_DV = 128
DN_DIM = DN_HEADS * DN_DK
DN_VDIM = DN_HEADS * DN_DV
DN_CONV_K = 5
DN_CHUNK = 64
NA_HEADS = 16
NA_HD = 64
NA_DIM = NA_HEADS * NA_HD
WIN_R = 8
WIN_C = 16
ROPE_THETA = 10000.0
D_FF = (((8 * D_MODEL + 2) // 3 + 255) // 256) * 256

IN_SEGMENTS = (
    ('conv', 2 * CONV_DIM),
    ('dn_q', DN_DIM),
    ('dn_k', DN_DIM),
    ('dn_v', DN_VDIM),
    ('dn_z', DN_VDIM),
    ('dn_beta', 2 * DN_HEADS),
    ('dn_alpha', 2 * DN_HEADS),
    ('na_q', NA_DIM),
    ('na_k', NA_DIM),
    ('na_v', NA_DIM),
    ('gate', N_BRANCH * D_MODEL),
)
IN_DIM = 2 * CONV_DIM + 2 * DN_DIM + 2 * DN_VDIM + 4 * DN_HEADS + 3 * NA_DIM + N_BRANCH * D_MODEL

kernel_name = 'hybrid_conv_deltanet_natten_dit'


def rmsnorm(x, g):
    xf = x.astype(jnp.float32)
    y = xf * lax.rsqrt(jnp.mean(xf * xf, axis=-1, keepdims=True) + EPS)
    return (y * g.astype(jnp.float32)).astype(x.dtype)


def layernorm(x, g, b):
    xf = x.astype(jnp.float32)
    mu = jnp.mean(xf, axis=-1, keepdims=True)
    var = jnp.mean(jnp.square(xf - mu), axis=-1, keepdims=True)
    return ((xf - mu) * lax.rsqrt(var + EPS) * g.astype(jnp.float32) + b.astype(jnp.float32)).astype(x.dtype)


def l2norm(x):
    return x * lax.rsqrt(jnp.sum(x * x, axis=-1, keepdims=True) + EPS)


def modulate(h, shift, scale):
    return h * (1 + scale) + shift


def depthwise_conv(x, w):
    k = w.shape[0]
    return lax.conv_general_dilated(
        x, w[:, None, :].astype(x.dtype), (1,), ((k // 2, k // 2),),
        dimension_numbers=('NWC', 'WIO', 'NWC'), feature_group_count=x.shape[-1])


def split_projection(p):
    out = {}
    off = 0
    for name, width in IN_SEGMENTS:
        out[name] = p[..., off:off + width]
        off += width
    return out


def project_segments(h, w_in, names):
    out = {}
    off = 0
    for name, width in IN_SEGMENTS:
        if name in names:
            out[name] = h @ w_in[:, off:off + width]
        off += width
    return out


def axial_rope_tables(n_tok, head_dim):
    t = jnp.arange(n_tok)
    axis_dim = head_dim // 2
    inv = ROPE_THETA ** (-jnp.arange(0, axis_dim, 2, dtype=jnp.float32) / axis_dim)
    ang_r = (t // GRID_W).astype(jnp.float32)[:, None] * inv
    ang_c = (t % GRID_W).astype(jnp.float32)[:, None] * inv
    return jnp.cos(ang_r), jnp.sin(ang_r), jnp.cos(ang_c), jnp.sin(ang_c)


def rotate(x, cos, sin):
    x1, x2 = jnp.split(x, 2, axis=-1)
    cos = cos[:, None, :]
    sin = sin[:, None, :]
    return jnp.concatenate([x1 * cos - x2 * sin, x1 * sin + x2 * cos], axis=-1)


def apply_axial_rope(x, tables):
    cr, sr, cc, sc = tables
    xr, xc = jnp.split(x, 2, axis=-1)
    return jnp.concatenate([rotate(xr, cr, sr), rotate(xc, cc, sc)], axis=-1)


def conformer_conv(u, dw, db, ln_g, ln_b, w_o):
    a, b = jnp.split(u, 2, axis=-1)
    h = a * jax.nn.sigmoid(b)
    h = depthwise_conv(h, dw) + db
    h = jax.nn.silu(layernorm(h, ln_g, ln_b))
    return h @ w_o


def dn_short_conv(t, w, head_dim):
    B_, L, C = t.shape
    return jax.nn.silu(depthwise_conv(t, w)).astype(jnp.float32).reshape(B_, L, C // head_dim, head_dim)


def dn_decay_and_beta(beta_raw, alpha_raw, a_log, dt_bias):
    B_, L, _ = beta_raw.shape
    beta = jax.nn.sigmoid(beta_raw.astype(jnp.float32)).reshape(B_, L, 2, DN_HEADS)
    g = -jnp.exp(a_log.astype(jnp.float32)) * jax.nn.softplus(
        alpha_raw.astype(jnp.float32).reshape(B_, L, 2, DN_HEADS) + dt_bias.astype(jnp.float32))
    return g, beta


def gated_delta_chunked(q, k, v, g, beta, s0):
    B_, L, H, _ = k.shape
    DV = v.shape[-1]
    C = DN_CHUNK
    N = L // C
    with_out = q is not None

    def to_chunks(t):
        t = t.reshape((B_, N, C, H) + t.shape[3:])
        return jnp.moveaxis(jnp.moveaxis(t, 1, 0), 3, 2)

    kc, vc, gc, bc = to_chunks(k), to_chunks(v), to_chunks(g), to_chunks(beta)
    gcum = jnp.cumsum(gc, axis=-1)
    causal = jnp.tril(jnp.ones((C, C), dtype=bool))
    strict = jnp.tril(jnp.ones((C, C), dtype=bool), -1)
    decay = jnp.exp(jnp.where(causal, gcum[..., :, None] - gcum[..., None, :], -jnp.inf))
    kb = kc * bc[..., None]
    lmat = jnp.where(strict, jnp.einsum('nbhid,nbhjd->nbhij', kb, kc) * decay, 0.0)
    eye = jnp.eye(C, dtype=lmat.dtype)
    tinv = lax.linalg.triangular_solve(lmat + eye, jnp.broadcast_to(eye, lmat.shape),
                                       left_side=True, lower=True, unit_diagonal=True)
    u = tinv @ (vc * bc[..., None])
    w = tinv @ (kb * jnp.exp(gcum)[..., None])
    kdec = kc * jnp.exp(gcum[..., -1:] - gcum)[..., None]
    glast = jnp.exp(gcum[..., -1])
    xs = (u, w, kdec, glast)
    if with_out:
        qc = to_chunks(q)
        a_qk = jnp.where(causal, jnp.einsum('nbhid,nbhjd->nbhij', qc, kc) * decay, 0.0)
        qdec = qc * jnp.exp(gcum)[..., None]
        xs = xs + (qdec, a_qk)

    def step(S, inp):
        u_i, w_i, kdec_i, glast_i = inp[:4]
        v_new = u_i - jnp.einsum('bhck,bhkv->bhcv', w_i, S)
        S_next = S * glast_i[..., None, None] + jnp.einsum('bhck,bhcv->bhkv', kdec_i, v_new)
        if not with_out:
            return S_next, None
        qdec_i, a_i = inp[4:]
        o = jnp.einsum('bhck,bhkv->bhcv', qdec_i, S) + jnp.einsum('bhij,bhjv->bhiv', a_i, v_new)
        return S_next, o

    S_fin, o = lax.scan(step, s0, xs)
    if not with_out:
        return None, S_fin
    o = jnp.moveaxis(jnp.moveaxis(o, 2, 3), 0, 1).reshape(B_, L, H, DV)
    return o, S_fin


def flip_if(t, d):
    return t if d == 0 else jnp.flip(t, axis=1)


def dn_output(o, z, norm_g, w_o, dtype):
    B_, L = o.shape[:2]
    zf = z.astype(jnp.float32).reshape(B_, L, DN_HEADS, DN_DV)
    y = rmsnorm(o, norm_g) * jax.nn.silu(zf)
    return y.reshape(B_, L, DN_VDIM).astype(dtype) @ w_o


def neighbourhood_attention(q, k, v, k_ctx, v_ctx, rpb):
    B_, S, H, Dh = q.shape
    rows = S // GRID_W
    kr = min(WIN_R, rows)
    n_loc = kr * WIN_C
    scale = Dh ** -0.5
    qg = q.reshape(B_, rows, GRID_W, H, Dh)
    kg = k.reshape(B_, rows, GRID_W, H, Dh)
    vg = v.reshape(B_, rows, GRID_W, H, Dh)
    cols = jnp.arange(GRID_W)
    col_start = jnp.clip(cols - WIN_C // 2, 0, GRID_W - WIN_C)
    col_idx = col_start[:, None] + jnp.arange(WIN_C)[None, :]
    bias_cols = rpb[:, :, col_idx - cols[:, None] + WIN_C - 1]

    def row_block(r):
        r0 = jnp.clip(r - kr // 2, 0, rows - kr)
        k_win = lax.dynamic_slice_in_dim(kg, r0, kr, axis=1)[:, :, col_idx]
        v_win = lax.dynamic_slice_in_dim(vg, r0, kr, axis=1)[:, :, col_idx]
        q_r = lax.dynamic_index_in_dim(qg, r, axis=1, keepdims=False)
        bias = bias_cols[:, r0 + jnp.arange(kr) - r + WIN_R - 1]
        s_loc = (jnp.einsum('bwhd,brwjhd->bhwrj', q_r, k_win).astype(jnp.float32) * scale
                 + jnp.transpose(bias, (0, 2, 1, 3)).astype(jnp.float32)[None])
        s_ctx = jnp.einsum('bwhd,bchd->bhwc', q_r, k_ctx).astype(jnp.float32) * scale
        p = jax.nn.softmax(jnp.concatenate([s_loc.reshape(B_, H, GRID_W, n_loc), s_ctx], axis=-1), axis=-1)
        p_loc = p[..., :n_loc].reshape(B_, H, GRID_W, kr, WIN_C).astype(v.dtype)
        p_ctx = p[..., n_loc:].astype(v.dtype)
        return (jnp.einsum('bhwrj,brwjhd->bwhd', p_loc, v_win)
                + jnp.einsum('bhwc,bchd->bwhd', p_ctx, v_ctx))

    out = lax.map(row_block, jnp.arange(rows))
    return jnp.moveaxis(out, 0, 1).reshape(B_, S, H * Dh)


def context_attention(q, k, v):
    B_, L, H, Dh = q.shape
    s = jnp.einsum('bqhd,bkhd->bhqk', q, k).astype(jnp.float32) * Dh ** -0.5
    p = jax.nn.softmax(s, axis=-1).astype(v.dtype)
    return jnp.einsum('bhqk,bkhd->bqhd', p, v).reshape(B_, L, H * Dh)


def gated_merge(gate_raw, a, b, c, w_out):
    B_, L, _ = gate_raw.shape
    g = jax.nn.sigmoid(gate_raw.astype(jnp.float32)).reshape(B_, L, N_BRANCH, D_MODEL).astype(a.dtype)
    return (g[:, :, 0] * a + g[:, :, 1] * b + g[:, :, 2] * c) @ w_out


def hybrid_mixer(hl, hc, w_in, conv_dw, conv_db, conv_ln_g, conv_ln_b, w_conv_out, dn_conv, dn_a_log,
                 dn_dt_bias, dn_norm_g, w_dn_out, na_rpb, w_na_out, w_out, tables, need_ctx):
    B_, S, _ = hl.shape
    Lc = hc.shape[1]
    pl = split_projection(hl @ w_in)
    if need_ctx:
        pc = split_projection(hc @ w_in)
    else:
        pc = project_segments(hc, w_in, ('dn_k', 'dn_v', 'dn_beta', 'dn_alpha', 'na_k', 'na_v'))

    a_l = conformer_conv(pl['conv'], conv_dw, conv_db, conv_ln_g, conv_ln_b, w_conv_out)

    cw_q, cw_k, cw_v = dn_conv[:, :DN_DIM], dn_conv[:, DN_DIM:2 * DN_DIM], dn_conv[:, 2 * DN_DIM:]
    q_scale = DN_DK ** -0.5
    k_c = l2norm(dn_short_conv(pc['dn_k'], cw_k, DN_DK))
    v_c = dn_short_conv(pc['dn_v'], cw_v, DN_DV)
    g_c, b_c = dn_decay_and_beta(pc['dn_beta'], pc['dn_alpha'], dn_a_log, dn_dt_bias)
    q_c = l2norm(dn_short_conv(pc['dn_q'], cw_q, DN_DK)) * q_scale if need_ctx else None
    q_l = apply_axial_rope(l2norm(dn_short_conv(pl['dn_q'], cw_q, DN_DK)), tables) * q_scale
    k_l = apply_axial_rope(l2norm(dn_short_conv(pl['dn_k'], cw_k, DN_DK)), tables)
    v_l = dn_short_conv(pl['dn_v'], cw_v, DN_DV)
    g_l, b_l = dn_decay_and_beta(pl['dn_beta'], pl['dn_alpha'], dn_a_log, dn_dt_bias)
    s0 = jnp.zeros((B_, DN_HEADS, DN_DK, DN_DV), jnp.float32)
    outs_l, outs_c = [], []
    for d in range(2):
        o_cd, s_ctx = gated_delta_chunked(flip_if(q_c, d) if need_ctx else None, flip_if(k_c, d), flip_if(v_c, d),
                                          flip_if(g_c[:, :, d], d), flip_if(b_c[:, :, d], d), s0)
        o_ld, _ = gated_delta_chunked(flip_if(q_l, d), flip_if(k_l, d), flip_if(v_l, d),
                                      flip_if(g_l[:, :, d], d), flip_if(b_l[:, :, d], d), s_ctx)
        outs_l.append(flip_if(o_ld, d))
        if need_ctx:
            outs_c.append(flip_if(o_cd, d))
    b_lat = dn_output(outs_l[0] + outs_l[1], pl['dn_z'], dn_norm_g, w_dn_out, hl.dtype)

    na_k_c = pc['na_k'].reshape(B_, Lc, NA_HEADS, NA_HD)
    na_v_c = pc['na_v'].reshape(B_, Lc, NA_HEADS, NA_HD)
    c_lat = neighbourhood_attention(pl['na_q'].reshape(B_, S, NA_HEADS, NA_HD),
                                    pl['na_k'].reshape(B_, S, NA_HEADS, NA_HD),
                                    pl['na_v'].reshape(B_, S, NA_HEADS, NA_HD),
                                    na_k_c, na_v_c, na_rpb).astype(hl.dtype) @ w_na_out

    y_l = gated_merge(pl['gate'], a_l, b_lat, c_lat, w_out)
    if not need_ctx:
        return y_l, None
    a_c = conformer_conv(pc['conv'], conv_dw, conv_db, conv_ln_g, conv_ln_b, w_conv_out)
    b_ctx = dn_output(outs_c[0] + outs_c[1], pc['dn_z'], dn_norm_g, w_dn_out, hc.dtype)
    c_ctx_out = context_attention(pc['na_q'].reshape(B_, Lc, NA_HEADS, NA_HD), na_k_c, na_v_c).astype(hc.dtype) @ w_na_out
    y_c = gated_merge(pc['gate'], a_c, b_ctx, c_ctx_out, w_out)
    return y_l, y_c


def swiglu(h, w_gu, w_down):
    gate, up = jnp.split(h @ w_gu, 2, axis=-1)
    return (jax.nn.silu(gate) * up) @ w_down


def setup_inputs(seed: int = 0) -> dict:
    key = jax.random.key(seed)
    ks = jax.random.split(key, 26)
    f32 = jnp.float32

    def dense(k, shape, fan_in, gain=1.0):
        return gain * fan_in ** -0.5 * jax.random.normal(k, shape, f32)

    def ones_noise(k, shape):
        return 1.0 + 0.02 * jax.random.normal(k, shape, f32)

    def small(k, shape):
        return 0.02 * jax.random.normal(k, shape, f32)

    dt = jnp.exp(jax.random.uniform(ks[16], (DEPTH, 2, DN_HEADS), f32, math.log(1e-3), math.log(1e-1)))
    return {
        'x': jax.random.normal(ks[0], (BATCH, SEQ, D_MODEL), f32),
        'c': jax.random.normal(ks[1], (BATCH, D_MODEL), f32),
        'ctx': jax.random.normal(ks[2], (BATCH, CTX_LEN, D_MODEL), f32),
        'c_ctx': jax.random.normal(ks[3], (D_MODEL,), f32),
        'w_mod': dense(ks[4], (DEPTH, D_MODEL, N_MOD * D_MODEL), D_MODEL, 0.5),
        'b_mod': small(ks[5], (DEPTH, N_MOD * D_MODEL)),
        'norm1_g': ones_noise(ks[6], (DEPTH, D_MODEL)),
        'norm2_g': ones_noise(ks[7], (DEPTH, D_MODEL)),
        'w_in': dense(ks[8], (DEPTH, D_MODEL, IN_DIM), D_MODEL),
        'conv_dw': dense(ks[9], (DEPTH, CONV_K, CONV_DIM), CONV_K),
        'conv_db': small(ks[10], (DEPTH, CONV_DIM)),
        'conv_ln_g': ones_noise(ks[11], (DEPTH, CONV_DIM)),
        'conv_ln_b': small(ks[12], (DEPTH, CONV_DIM)),
        'w_conv_out': dense(ks[13], (DEPTH, CONV_DIM, D_MODEL), CONV_DIM),
        'dn_conv': dense(ks[14], (DEPTH, DN_CONV_K, 2 * DN_DIM + DN_VDIM), DN_CONV_K),
        'dn_a_log': jnp.log(jax.random.uniform(ks[15], (DEPTH, 2, DN_HEADS), f32, 1.0, 16.0)),
        'dn_dt_bias': dt + jnp.log(-jnp.expm1(-dt)),
        'dn_norm_g': ones_noise(ks[17], (DEPTH, DN_DV)),
        'w_dn_out': dense(ks[18], (DEPTH, DN_VDIM, D_MODEL), DN_VDIM),
        'na_rpb': small(ks[19], (DEPTH, NA_HEADS, 2 * WIN_R - 1, 2 * WIN_C - 1)),
        'w_na_out': dense(ks[20], (DEPTH, NA_DIM, D_MODEL), NA_DIM),
        'w_out': dense(ks[21], (DEPTH, D_MODEL, D_MODEL), D_MODEL),
        'w_gu': dense(ks[22], (DEPTH, D_MODEL, 2 * D_FF), D_MODEL),
        'w_down': dense(ks[23], (DEPTH, D_FF, D_MODEL), D_FF),
        'final_norm_g': ones_noise(ks[24], (D_MODEL,)),
    }


def reference(x, c, ctx, c_ctx, w_mod, b_mod, norm1_g, norm2_g, w_in, conv_dw, conv_db, conv_ln_g, conv_ln_b,
              w_conv_out, dn_conv, dn_a_log, dn_dt_bias, dn_norm_g, w_dn_out, na_rpb, w_na_out, w_out, w_gu,
              w_down, final_norm_g):
    tables = axial_rope_tables(x.shape[1], DN_DK)
    xc = ctx
    act_lat = jax.nn.silu(c)
    act_ctx = jax.nn.silu(c_ctx)
    for i in range(DEPTH):
        need_ctx = i < DEPTH - 1
        ml = jnp.split((act_lat @ w_mod[i] + b_mod[i])[:, None, :], N_MOD, axis=-1)
        mc = jnp.split(act_ctx @ w_mod[i] + b_mod[i], N_MOD, axis=-1)
        hl = modulate(rmsnorm(x, norm1_g[i]), ml[0], ml[1])
        hc = modulate(rmsnorm(xc, norm1_g[i]), mc[0], mc[1])
        y_l, y_c = hybrid_mixer(hl, hc, w_in[i], conv_dw[i], conv_db[i], conv_ln_g[i], conv_ln_b[i],
                                w_conv_out[i], dn_conv[i], dn_a_log[i], dn_dt_bias[i], dn_norm_g[i],
                                w_dn_out[i], na_rpb[i], w_na_out[i], w_out[i], tables, need_ctx)
        x = x + ml[2] * y_l
        x = x + ml[5] * swiglu(modulate(rmsnorm(x, norm2_g[i]), ml[3], ml[4]), w_gu[i], w_down[i])
        if need_ctx:
            xc = xc + mc[2] * y_c
            xc = xc + mc[5] * swiglu(modulate(rmsnorm(xc, norm2_g[i]), mc[3], mc[4]), w_gu[i], w_down[i])
    return rmsnorm(x, final_norm_g)
```

```python
import contextlib
import numpy as np
import concourse.bass as bass
import concourse.mybir as mybir
from concourse.bass_utils import run_bass_kernel_spmd

F32 = mybir.dt.float32
BF16 = mybir.dt.bfloat16
ALU = mybir.AluOpType
AF = mybir.ActivationFunctionType

EPOCH = 12000
NDMA_SEM = 12

D = 1024
SEQ = 8192
LC = 256
NT = SEQ + LC
DEPTH = 2
IN_DIM = 12320
DFF = 2816
GRID_W = 64
ROWS = SEQ // GRID_W
EPS = 1e-6
OFF = dict(conv=0, dn_q=2048, dn_k=3072, dn_v=4096, dn_z=5120, beta=6144, alpha=6160,
           na_q=6176, na_k=7200, na_v=8224, gate=9248)
TILES = [(0, 256, True)] + [(LC + 512 * i, 512, False) for i in range(SEQ // 512)]
NEG = -30000.0


class Buf:
    __slots__ = ("name", "w", "r")

    def __init__(self, name=""):
        self.name = name
        self.w = None
        self.r = []


class Sched:
    ENG = ("pe", "act", "dve", "pool", "sp")

    def __init__(self, nc, stack):
        self.nc = nc
        self.stack = stack
        self.ops = {e: [] for e in self.ENG}
        self.count = {e: 0 for e in self.ENG}
        self.esems = {e: [] for e in self.ENG}
        self.waited = {e: {} for e in self.ENG}
        self.dsems = {}
        self.dcount = {}
        self.dnext = {}
        for q in ("sp", "act", "pool"):
            self.dsems[q] = [stack.enter_context(nc.semaphore(f"d_{q}_{i}")) for i in range(NDMA_SEM)]
            self.dcount[q] = [0] * NDMA_SEM
            self.dnext[q] = 0
        self.same_engine_sync = {"pe": False, "act": True, "dve": True, "pool": True, "sp": False}

    def _esem(self, eng, idx):
        ep = idx // EPOCH
        while len(self.esems[eng]) <= ep:
            self.esems[eng].append(self.stack.enter_context(
                self.nc.semaphore(f"e_{eng}_{len(self.esems[eng])}")))
        return self.esems[eng][ep], idx % EPOCH + 1

    def _need(self, eng, tok, waits, force=False):
        if tok is None:
            return
        if tok[0] == "e":
            _, src, idx = tok
            if src == eng and not self.same_engine_sync[eng] and not force:
                return
            key = ("e", src)
            if self.waited[eng].get(key, -1) >= idx:
                return
            self.waited[eng][key] = idx
            waits.append(self._esem(src, idx))
        else:
            _, q, si, val = tok
            key = ("d", q, si)
            if self.waited[eng].get(key, -1) >= val:
                return
            self.waited[eng][key] = val
            waits.append((self.dsems[q][si], val))

    def _deps(self, eng, reads, writes):
        waits = []
        for b in reads:
            self._need(eng, b.w, waits)
        for b in writes:
            self._need(eng, b.w, waits)
            for t in b.r:
                self._need(eng, t, waits)
        return waits

    def _commit(self, tok, reads, writes):
        for b in reads:
            b.r.append(tok)
        for b in writes:
            b.w = tok
            b.r = []

    def op(self, eng, fn, reads=(), writes=()):
        waits = self._deps(eng, reads, writes)
        idx = self.count[eng]
        self.count[eng] += 1
        sem, _ = self._esem(eng, idx)
        self.ops[eng].append((waits, fn, (sem, 1)))
        self._commit(("e", eng, idx), reads, writes)

    def dma(self, q, out, in_, reads=(), writes=()):
        waits = self._deps(q, reads, writes)
        si = self.dnext[q]
        self.dnext[q] = (si + 1) % NDMA_SEM
        prev = self.dcount[q][si]
        if prev > 0:
            self._need(q, ("d", q, si, prev), waits)
        val = prev + 16
        self.dcount[q][si] = val
        sem = self.dsems[q][si]
        self.ops[q].append((waits, lambda e, o=out, i=in_: e.dma_start(out=o, in_=i), (sem, 16)))
        tok = ("d", q, si, val)
        self._commit(tok, reads, writes)
        return tok

    def barrier(self):
        for e in self.ENG:
            waits = []
            for s in self.ENG:
                if self.count[s] > 0:
                    self._need(e, ("e", s, self.count[s] - 1), waits, force=True)
            for q in self.dsems:
                for si in range(NDMA_SEM):
                    if self.dcount[q][si] > 0:
                        self._need(e, ("d", q, si, self.dcount[q][si]), waits)
            self.ops[e].append((waits, None, None))

    def emit(self):
        nc = self.nc
        with nc.Block() as block:
            def run(engname):
                def body(e):
                    for waits, fn, inc in self.ops[engname]:
                        for s, v in waits:
                            e.wait_ge(s, v)
                        if fn is not None:
                            ins = fn(e)
                            ins.then_inc(inc[0], inc[1])
                return body
            block.tensor(run("pe"))
            block.scalar(run("act"))
            block.vector(run("dve"))
            block.gpsimd(run("pool"))
            block.sync(run("sp"))


class RR:
    def __init__(self, items):
        self.items = items
        self.i = 0

    def next(self):
        it = self.items[self.i]
        self.i = (self.i + 1) % len(self.items)
        return it


class Phase:
    def __init__(self, K, name):
        self.K = K
        self.name = name
        self.st = contextlib.ExitStack()
        self.n = 0

    def sb(self, shape, dt=F32):
        self.n += 1
        t = self.st.enter_context(self.K.nc.sbuf_tensor(f"{self.name}_{self.n}", list(shape), dt))
        return t, Buf(f"{self.name}_{self.n}")

    def pool(self, n, shape, dt=F32):
        return RR([self.sb(shape, dt) for _ in range(n)])

    def psum(self, n, dt=F32, cols=512):
        out = []
        for _ in range(n):
            self.n += 1
            t = self.st.enter_context(self.K.nc.psum_tensor(f"{self.name}_ps{self.n}", [128, cols], dt))
            out.append((t, Buf(f"{self.name}_ps{self.n}")))
        return RR(out)

    def close(self):
        self.K.S.barrier()
        self.st.close()


def bc(ap, shape, axis):
    return ap.unsqueeze(axis).to_broadcast(list(shape))


NVEC = 89
V_BMOD, V_N1, V_N2, V_CDB, V_LNG, V_LNB, V_DNG = 0, 48, 56, 64, 72, 80, 88
CM_ID, CM_PERM, CM_UF, CM_UB, CM_MNF, CM_MNB, CM_MSF, CM_MSB, CM_NUF, CM_NUB = range(10)
NCM = 10


class K:
    def __init__(self, debug=(), stop_after=None, layers=DEPTH):
        self.debug = set(debug)
        self.stop_after = stop_after
        self.layers = layers
        self.nc = bass.Bass("TRN2", target_bir_lowering=False)
        self.st = contextlib.ExitStack()
        self.S = Sched(self.nc, self.st)
        self.dram = {}

    def mm(self, out, lhsT, rhs, start, stop, reads, writes):
        self.S.op("pe", lambda e: e.matmul(out, lhsT=lhsT, rhs=rhs, start=start, stop=stop), reads, writes)

    def tr(self, out, in_, ident, reads, writes):
        self.S.op("pe", lambda e: e.transpose(out, in_, ident), reads, writes)

    def act(self, out, in_, func, reads, writes, bias=None, scale=None):
        kw = {}
        if bias is not None:
            kw["bias"] = bias
        if scale is not None:
            kw["scale"] = scale
        self.S.op("act", lambda e: e.activation(out=out, in_=in_, func=func, **kw), reads, writes)

    def tt(self, eng, out, in0, in1, op, reads, writes):
        self.S.op(eng, lambda e: e.tensor_tensor(out=out, in0=in0, in1=in1, op=op), reads, writes)

    def ts(self, eng, out, in0, s1, s2, op0, op1, reads, writes):
        if op1 is None:
            self.S.op(eng, lambda e: e.tensor_scalar(out=out, in0=in0, scalar1=s1, scalar2=None, op0=op0), reads, writes)
        else:
            self.S.op(eng, lambda e: e.tensor_scalar(out=out, in0=in0, scalar1=s1, scalar2=s2, op0=op0, op1=op1), reads, writes)

    def stt(self, out, in0, scalar, in1, op0, op1, reads, writes):
        self.S.op("dve", lambda e: e.scalar_tensor_tensor(out=out, in0=in0, scalar=scalar, in1=in1, op0=op0, op1=op1),
                  reads, writes)

    def cp(self, eng, out, in_, reads, writes):
        if eng == "act":
            self.act(out, in_, AF.Copy, reads, writes)
        else:
            self.S.op(eng, lambda e: e.tensor_copy(out=out, in_=in_), reads, writes)

    def ms(self, eng, out, val, writes):
        self.S.op(eng, lambda e: e.memset(out, val), (), writes)

    def rsqrt_(self, out, in_, reads, writes, eps_ap):
        self.act(out, in_, AF.Ln, reads, writes, bias=eps_ap)
        self.act(out, out, AF.Exp, writes, writes, scale=-0.5)

    def din(self, name, shape, dt=F32):
        t = self.nc.dram_tensor(name, list(shape), dt, kind="ExternalInput").ap()
        self.dram[name] = t
        return t

    def dscr(self, name, shape, dt):
        kind = "ExternalOutput" if name in self.debug else "Internal"
        t = self.nc.dram_tensor(name, list(shape), dt, kind=kind).ap()
        self.dram[name] = t
        return t

    def build(self):
        nc, S = self.nc, self.S
        g = self.din
        self.xT0 = g("xT0", [D, NT])
        self.cvec = g("cvec", [128, 8, 2])
        self.w_mod = g("w_mod", [DEPTH, D, 6 * D])
        self.w_in = g("w_in", [DEPTH, D, IN_DIM])
        self.w_sq = {n: g(n, [DEPTH, D, D]) for n in ("w_conv_out", "w_dn_out", "w_na_out", "w_out")}
        self.w_gu = g("w_gu", [DEPTH, D, 2 * DFF])
        self.w_down = g("w_down", [DEPTH, DFF, D])
        self.vecs = g("vecs", [DEPTH, 128, NVEC])
        self.conv_dw = g("conv_dw", [DEPTH, 128, 8, 31])
        self.dn_conv = g("dn_conv", [DEPTH, 128, 24, 5])
        self.dnp = g("dnp", [DEPTH, 128, 32])
        self.rpbT = g("rpbT", [DEPTH, 64, 16, 15, 64])
        self.fin_g = g("fin_g", [128, 8])
        self.ropeC = g("ropeC", [128, SEQ])
        self.ropeS = g("ropeS", [128, SEQ])
        self.cmat = g("cmat", [128, NCM, 128])
        self.yT = nc.dram_tensor("yT", [D, SEQ], F32, kind="ExternalOutput").ap()
        s = self.dscr
        self.wb_in = s("wb_in", [DEPTH, D, IN_DIM], BF16)
        self.wb_sq = {n: s("wb_" + n, [DEPTH, D, D], BF16) for n in self.w_sq}
        self.wb_gu = s("wb_gu", [DEPTH, D, 2 * DFF], BF16)
        self.wb_down = s("wb_down", [DEPTH, DFF, D], BF16)
        self.x1 = s("x1", [D, NT], F32)
        self.hconv = s("hconv", [D, NT], BF16)
        self.dnpre = s("dnpre", [3 * D, NT], BF16)
        self.zs = s("zs", [D, NT], BF16)
        self.bg = s("bg", [NT, 32], F32)
        self.naq = s("naq", [D, NT], BF16)
        self.nak = s("nak", [D, NT], BF16)
        self.nav = s("nav", [NT, D], BF16)
        self.gates = s("gates", [3 * D, NT], BF16)
        self.brA = s("brA", [D, NT], BF16)
        self.brB = s("brB", [D, NT], BF16)
        self.brC = s("brC", [D, NT], BF16)
        self.dq = s("dq", [D, NT], BF16)
        self.dk = s("dk", [D, NT], BF16)
        self.dkt = s("dkt", [NT, D], BF16)
        self.dvt = s("dvt", [NT, D], BF16)
        self.ofw = s("ofw", [D, NT], F32)
        self.obw = s("obw", [D, NT], F32)
        self.modv_d = s("modv_d", [128, 96], F32)

        self.consts()
        self.phase0()
        done = False
        for l in range(self.layers):
            last = (l == DEPTH - 1)
            xsrc = self.xT0 if l == 0 else self.x1
            steps = [("mod", lambda: self.phase_mod(l)),
                     ("p1", lambda: self.phase1(l, xsrc)),
                     ("p2", lambda: self.phase2(l, last)),
                     ("p3a", lambda: self.phase3a(l)),
                     ("p3c", lambda: self.phase3c(l)),
                     ("p3d", lambda: self.phase3d(l, last)),
                     ("p4", lambda: self.phase4(l, last)),
                     ("p5", lambda: self.phase5(l, xsrc, last))]
            for name, fn in steps:
                fn()
                if self.stop_after == (l, name):
                    done = True
                    break
            if done:
                break
        S.barrier()
        S.emit()
        self.st.close()
        return nc

    def consts(self):
        nc, S, st = self.nc, self.S, self.st
        sb = lambda name, shape, dt=F32: st.enter_context(nc.sbuf_tensor(name, list(shape), dt))
        self.cbuf = Buf("consts")
        cb = [self.cbuf]
        self.cm32 = sb("cm32", [128, NCM, 128])
        S.dma("sp", self.cm32[:], self.cmat, writes=cb)
        self.cmb = sb("cmb", [128, 2, 128], BF16)
        self.cp("dve", self.cmb[:], self.cm32[:, 0:2, :], cb, cb)
        self.ones_b = sb("ones_b", [128, 4, 128], BF16)
        for i, v in enumerate((1.0 / 1024, 1.0 / 128, 128.0, 1.0)):
            self.ms("pool", self.ones_b[:, i, :], v, cb)
        self.ones_f = sb("ones_f", [128, 128], F32)
        self.ms("pool", self.ones_f[:], 1.0, cb)
        self.cst = sb("cst", [128, 4], F32)
        for i, v in enumerate((EPS, 128 * EPS, 1.0, 0.0)):
            self.ms("pool", self.cst[:, i:i + 1], v, cb)
        self.eps_t = self.cst[:, 0:1]
        self.eps128_t = self.cst[:, 1:2]
        self.one_t = self.cst[:, 2:3]
        self.vec_t = sb("vec_t", [128, DEPTH, NVEC])
        S.dma("sp", self.vec_t[:], self.vecs.rearrange("l p n -> p l n"), writes=cb)
        self.fing_t = sb("fing_t", [128, 8])
        S.dma("sp", self.fing_t[:], self.fin_g, writes=cb)
        self.modv = sb("modv", [128, 48, 2])
        self.modA = sb("modA", [128, 2, 8, 2])
        self.mbuf = Buf("mod")
        S.barrier()

    def vcol(self, l, off, c):
        return self.vec_t[:, l, off + c:off + c + 1]

    def phase0(self):
        S = self.S
        for l in range(self.layers):
            pairs = [(self.w_in[l], self.wb_in[l], D), (self.w_gu[l], self.wb_gu[l], D),
                     (self.w_down[l], self.wb_down[l], DFF)]
            pairs += [(self.w_sq[n][l], self.wb_sq[n][l], D) for n in self.w_sq]
            for src, dst, rows in pairs:
                for r in range(0, rows, 128):
                    S.dma("pool", dst[r:r + 128, :], src[r:r + 128, :])
        S.barrier()

    def phase_mod(self, l):
        S = self.S
        ph = Phase(self, f"mod{l}")
        cb = [self.cbuf]
        cv, bcv = ph.sb([128, 8, 2])
        S.dma("sp", cv[:], self.cvec, writes=[bcv])
        self.act(cv[:], cv[:], AF.Silu, [bcv], [bcv])
        wp = ph.pool(2, [128, 8, 768])
        ps, bps = ph.psum(1).next()
        wsrc = self.w_mod[l].rearrange("(k p) n -> p k n", p=128)
        for gi in range(8):
            w, bw = wp.next()
            S.dma("sp", w[:], wsrc[:, :, gi * 768:(gi + 1) * 768], writes=[bw])
            for j in range(6):
                jj = gi * 6 + j
                for k in range(8):
                    self.mm(ps[:, jj * 2:jj * 2 + 2], w[:, k, j * 128:(j + 1) * 128], cv[:, k, :],
                            k == 0, k == 7, [bw, bcv], [bps])
        mb = [self.mbuf]
        self.tt("dve", self.modv[:], ps[:, 0:96].rearrange("p (j c) -> p j c", c=2),
                bc(self.vec_t[:, l, V_BMOD:V_BMOD + 48], [128, 48, 2], 2), ALU.add, [bps] + cb, mb)
        for n, (voff, sc) in enumerate(((V_N1, 8), (V_N2, 32))):
            self.ts("dve", self.modA[:, n], self.modv[:, sc:sc + 8, :], 1.0, None, ALU.add, None, mb, mb)
            self.tt("dve", self.modA[:, n], self.modA[:, n],
                    bc(self.vec_t[:, l, voff:voff + 8], [128, 8, 2], 2), ALU.mult, mb + cb, mb)
        if "modv_d" in self.debug:
            S.dma("sp", self.modv_d, self.modv[:].rearrange("p j c -> p (j c)"), reads=mb)
        ph.close()

    def norm_mod(self, ph, x, bx, T, n, col, xn, bxn, sq, bsq, h, bh, rs, brs, psb):
        mb = [self.mbuf]
        cb = [self.cbuf]
        ps, bps = psb
        self.act(sq[:, :, :T], x[:, :, :T], AF.Square, [bx], [bsq])
        for c in range(8):
            self.mm(ps[:, :T], self.ones_b[:, 0, :], sq[:, c, :T], c == 0, c == 7, [bsq] + cb, [bps])
        self.rsqrt_(rs[:, :T], ps[:, :T], [bps] + cb, [brs], self.eps_t)
        self.tt("dve", xn[:, :, :T], x[:, :, :T], bc(rs[:, :T], [128, 8, T], 1), ALU.mult, [bx, brs], [bxn])
        shift = 0 if n == 0 else 24
        for c in range(8):
            if c % 2 == 0:
                self.act(h[:, c, :T], xn[:, c, :T], AF.Identity, [bxn] + mb, [bh],
                         bias=self.modv[:, shift + c, col:col + 1], scale=self.modA[:, n, c, col:col + 1])
            else:
                self.ts("dve", h[:, c, :T], xn[:, c, :T], self.modA[:, n, c, col:col + 1],
                        self.modv[:, shift + c, col:col + 1], ALU.mult, ALU.add, [bxn] + mb, [bh])

    def phase1(self, l, xsrc):
        S = self.S
        ph = Phase(self, f"p1_{l}")
        cb = [self.cbuf]
        xp = ph.pool(2, [128, 8, 512])
        xnp = ph.pool(1, [128, 8, 512])
        sqp = ph.pool(1, [128, 8, 512], BF16)
        hp = ph.pool(2, [128, 8, 512], BF16)
        rsp = ph.pool(1, [128, 512])
        wp = ph.pool(3, [128, 8, 512], BF16)
        stp = ph.pool(3, [128, 4, 512], BF16)
        tmpp = ph.pool(2, [128, 512])
        bgp = ph.pool(2, [128, 4, 32])
        banks = ph.psum(7)
        psn = ph.psum(1).next()
        dnp_t, bdnp = ph.sb([128, 32])
        S.dma("sp", dnp_t[:], self.dnp[l], writes=[bdnp])
        self.act(dnp_t[:, 0:16], dnp_t[:, 0:16], AF.Exp, [bdnp], [bdnp])
        self.ts("dve", dnp_t[:, 0:16], dnp_t[:, 0:16], -1.0, None, ALU.mult, None, [bdnp], [bdnp])

        wsrc = self.wb_in[l].rearrange("(k p) n -> p k n", p=128)
        xs = xsrc.rearrange("(c p) t -> p c t", p=128)
        groups = []
        for j in range(0, 8, 2):
            groups.append(("glu", [(j * 128, 256), (1024 + j * 128, 256)], self.hconv, j * 128))
        for i in range(6):
            groups.append(("copy", [(2048 + i * 512, 512)], self.dnpre, i * 512))
        for i in range(2):
            groups.append(("silu", [(OFF["dn_z"] + i * 512, 512)], self.zs, i * 512))
        groups.append(("ba", [(OFF["beta"], 32)], self.bg, 0))
        for i in range(2):
            groups.append(("copy", [(OFF["na_q"] + i * 512, 512)], self.naq, i * 512))
        for i in range(2):
            groups.append(("copy", [(OFF["na_k"] + i * 512, 512)], self.nak, i * 512))
        for i in range(2):
            groups.append(("tok", [(OFF["na_v"] + i * 512, 512)], self.nav, i * 512))
        for i in range(6):
            groups.append(("sigmoid", [(OFF["gate"] + i * 512, 512)], self.gates, i * 512))

        for (t0, T, isctx) in TILES:
            col = 1 if isctx else 0
            x, bx = xp.next()
            S.dma("sp", x[:, :, :T], xs[:, :, t0:t0 + T], writes=[bx])
            xn, bxn = xnp.next()
            sq, bsq = sqp.next()
            h, bh = hp.next()
            rs, brs = rsp.next()
            self.norm_mod(ph, x, bx, T, 0, col, xn, bxn, sq, bsq, h, bh, rs, brs, psn)
            nsub = T // 128
            for gi, (kind, ranges, dst, drow) in enumerate(groups):
                w, bw = wp.next()
                o = 0
                for (c0, wd) in ranges:
                    S.dma("sp", w[:, :, o:o + wd], wsrc[:, :, c0:c0 + wd], writes=[bw])
                    o += wd
                if kind in ("glu", "copy", "silu", "sigmoid"):
                    pss = []
                    for j in range(4):
                        ps, bps = banks.next()
                        for k in range(8):
                            self.mm(ps[:, :T], w[:, k, j * 128:(j + 1) * 128], h[:, k, :T], k == 0, k == 7,
                                    [bw, bh], [bps])
                        pss.append((ps, bps))
                    stg, bst = stp.next()
                    if kind == "glu":
                        for j in range(2):
                            tmp, btmp = tmpp.next()
                            self.act(tmp[:, :T], pss[2 + j][0][:, :T], AF.Sigmoid, [pss[2 + j][1]], [btmp])
                            self.tt("dve", stg[:, j, :T], pss[j][0][:, :T], tmp[:, :T], ALU.mult,
                                    [pss[j][1], btmp], [bst])
                        nout = 2
                    else:
                        for j in range(4):
                            if kind == "copy":
                                self.cp("dve", stg[:, j, :T], pss[j][0][:, :T], [pss[j][1]], [bst])
                            else:
                                self.act(stg[:, j, :T], pss[j][0][:, :T], AF.Silu if kind == "silu" else AF.Sigmoid,
                                         [pss[j][1]], [bst])
                        nout = 4
                    S.dma("pool", dst[drow:drow + nout * 128, t0:t0 + T].rearrange("(j p) t -> p j t", p=128),
                          stg[:, 0:nout, :T], reads=[bst])
                elif kind == "tok":
                    for s in range(nsub):
                        ps, bps = banks.next()
                        for k in range(8):
                            self.mm(ps[:, :], h[:, k, s * 128:(s + 1) * 128], w[:, k, :], k == 0, k == 7,
                                    [bw, bh], [bps])
                        stg, bst = stp.next()
                        self.cp("act" if s % 2 == 0 else "dve", stg[:, 0, :], ps[:, :], [bps], [bst])
                        S.dma("pool", dst[t0 + s * 128:t0 + (s + 1) * 128, drow:drow + 512], stg[:, 0, :], reads=[bst])
                else:
                    ps, bps = banks.next()
                    for s in range(nsub):
                        for k in range(8):
                            self.mm(ps[:, s * 32:(s + 1) * 32], h[:, k, s * 128:(s + 1) * 128], w[:, k, 0:32],
                                    k == 0, k == 7, [bw, bh], [bps])
                    bgt, bbg = bgp.next()
                    pv = ps[:, 0:nsub * 32].rearrange("p (s c) -> p s c", c=32)
                    self.act(bgt[:, :nsub, 0:16], pv[:, :, 0:16], AF.Sigmoid, [bps], [bbg])
                    self.tt("dve", bgt[:, :nsub, 16:32], pv[:, :, 16:32], bc(dnp_t[:, 16:32], [128, nsub, 16], 1),
                            ALU.add, [bps, bdnp], [bbg])
                    self.ts("dve", bgt[:, :nsub, 16:32], bgt[:, :nsub, 16:32], 60.0, None, ALU.min, None, [bbg], [bbg])
                    self.act(bgt[:, :nsub, 16:32], bgt[:, :nsub, 16:32], AF.Exp, [bbg], [bbg])
                    self.act(bgt[:, :nsub, 16:32], bgt[:, :nsub, 16:32], AF.Ln, [bbg] + cb, [bbg], bias=self.one_t)
                    self.tt("dve", bgt[:, :nsub, 16:32], bgt[:, :nsub, 16:32], bc(dnp_t[:, 0:16], [128, nsub, 16], 1),
                            ALU.mult, [bbg, bdnp], [bbg])
                    S.dma("pool", dst[t0:t0 + T, :].rearrange("(s p) c -> p s c", p=128), bgt[:, :nsub, :], reads=[bbg])
        ph.close()

    def load_w_sq(self, ph, name, l):
        w, bw = ph.sb([128, 8, D], BF16)
        self.S.dma("sp", w[:], self.wb_sq[name][l].rearrange("(k p) n -> p k n", p=128), writes=[bw])
        return w, bw

    def proj_store(self, y, by, T, w, bw, banks, stp, dst, t0):
        for half in range(2):
            stg, bst = stp.next()
            for j in range(4):
                jj = half * 4 + j
                ps, bps = banks.next()
                for k in range(8):
                    self.mm(ps[:, :T], w[:, k, jj * 128:(jj + 1) * 128], y[:, k, :T], k == 0, k == 7, [bw, by], [bps])
                self.cp("act" if j % 2 == 0 else "dve", stg[:, j, :T], ps[:, :T], [bps], [bst])
            self.S.dma("pool", dst[half * 512:(half + 1) * 512, t0:t0 + T].rearrange("(j p) t -> p j t", p=128),
                       stg[:, :, :T], reads=[bst])

    def phase2(self, l, last):
        S = self.S
        ph = Phase(self, f"p2_{l}")
        cb = [self.cbuf]
        w, bw = self.load_w_sq(ph, "w_conv_out", l)
        dw, bdw = ph.sb([128, 8, 31])
        S.dma("sp", dw[:], self.conv_dw[l], writes=[bdw])
        diag, bdiag = ph.sb([128, 248, 128], BF16)
        for c in range(8):
            for k in range(31):
                self.ts("dve" if (c * 31 + k) % 2 == 0 else "pool", diag[:, c * 31 + k, :], self.cm32[:, CM_ID, :],
                        dw[:, c, k:k + 1], None, ALU.mult, None, [bdw] + cb, [bdiag])
        hinp = ph.pool(2, [128, 8, 512 + 30], BF16)
        accp = ph.pool(2, [128, 8, 512])
        hbp = ph.pool(1, [128, 8, 512], BF16)
        sqp = ph.pool(1, [128, 8, 512], BF16)
        yp = ph.pool(2, [128, 8, 512], BF16)
        stp = ph.pool(2, [128, 4, 512], BF16)
        mp = ph.pool(1, [128, 512])
        m2p = ph.pool(1, [128, 512])
        rsp = ph.pool(1, [128, 512])
        banks = ph.psum(6)
        pstat = ph.psum(2)
        hsrc = self.hconv.rearrange("(c p) t -> p c t", p=128)
        for (t0, T, isctx) in TILES:
            if isctx and last:
                continue
            s0, s1 = (0, LC) if isctx else (LC, NT)
            lo, hi = max(t0 - 15, s0), min(t0 + T + 15, s1)
            hin, bhin = hinp.next()
            if lo > t0 - 15:
                self.ms("pool", hin[:, :, 0:15], 0.0, [bhin])
            if hi < t0 + T + 15:
                self.ms("pool", hin[:, :, T + 15:T + 30], 0.0, [bhin])
            S.dma("sp", hin[:, :, lo - (t0 - 15):hi - (t0 - 15)], hsrc[:, :, lo:hi], writes=[bhin])
            acc, bacc = accp.next()
            for c in range(8):
                ps, bps = banks.next()
                for k in range(31):
                    self.mm(ps[:, :T], diag[:, c * 31 + k, :], hin[:, c, k:k + T], k == 0, k == 30, [bhin, bdiag], [bps])
                if c % 2 == 0:
                    self.act(acc[:, c, :T], ps[:, :T], AF.Identity, [bps] + cb, [bacc], bias=self.vcol(l, V_CDB, c))
                else:
                    self.ts("dve", acc[:, c, :T], ps[:, :T], self.vcol(l, V_CDB, c), None, ALU.add, None, [bps] + cb, [bacc])
            hb, bhb = hbp.next()
            sq, bsq = sqp.next()
            self.cp("act", hb[:, :, :T], acc[:, :, :T], [bacc], [bhb])
            self.act(sq[:, :, :T], acc[:, :, :T], AF.Square, [bacc], [bsq])
            pm, bpm = pstat.next()
            pq, bpq = pstat.next()
            for c in range(8):
                self.mm(pm[:, :T], self.ones_b[:, 0, :], hb[:, c, :T], c == 0, c == 7, [bhb] + cb, [bpm])
            for c in range(8):
                self.mm(pq[:, :T], self.ones_b[:, 0, :], sq[:, c, :T], c == 0, c == 7, [bsq] + cb, [bpq])
            mean, bmean = mp.next()
            m2, bm2 = m2p.next()
            rs, brs = rsp.next()
            self.cp("act", mean[:, :T], pm[:, :T], [bpm], [bmean])
            self.tt("dve", m2[:, :T], mean[:, :T], mean[:, :T], ALU.mult, [bmean], [bm2])
            self.tt("dve", m2[:, :T], pq[:, :T], m2[:, :T], ALU.subtract, [bpq, bm2], [bm2])
            self.ts("dve", m2[:, :T], m2[:, :T], 0.0, None, ALU.max, None, [bm2], [bm2])
            self.rsqrt_(rs[:, :T], m2[:, :T], [bm2] + cb, [brs], self.eps_t)
            self.tt("dve", acc[:, :, :T], acc[:, :, :T], bc(mean[:, :T], [128, 8, T], 1), ALU.subtract,
                    [bacc, bmean], [bacc])
            self.tt("dve", acc[:, :, :T], acc[:, :, :T], bc(rs[:, :T], [128, 8, T], 1), ALU.mult, [bacc, brs], [bacc])
            y, by = yp.next()
            for c in range(8):
                self.act(y[:, c, :T], acc[:, c, :T], AF.Silu, [bacc] + cb, [by],
                         bias=self.vcol(l, V_LNB, c), scale=self.vcol(l, V_LNG, c))
            self.proj_store(y, by, T, w, bw, banks, stp, self.brA, t0)
        ph.close()

    def phase5(self, l, xsrc, last):
        S = self.S
        ph = Phase(self, f"p5_{l}")
        cb = [self.cbuf]
        mb = [self.mbuf]
        w, bw = self.load_w_sq(ph, "w_out", l)
        xp = ph.pool(1, [128, 8, 512])
        xnp = ph.pool(1, [128, 8, 512])
        sqp = ph.pool(1, [128, 8, 512], BF16)
        hp = ph.pool(1, [128, 8, 512], BF16)
        rsp = ph.pool(1, [128, 512])
        brp = ph.pool(3, [128, 8, 512], BF16)
        gp = ph.pool(1, [128, 24, 512], BF16)
        t1p = ph.pool(2, [128, 512])
        t2p = ph.pool(2, [128, 512])
        mgp = ph.pool(1, [128, 8, 512], BF16)
        wgp = ph.pool(2, [128, 8, 512], BF16)
        wdp = ph.pool(1, [128, 22, 512], BF16)
        fp = ph.pool(1, [128, 22, 512], BF16)
        tmpp = ph.pool(2, [128, 512])
        banks = ph.psum(7)
        psn = ph.psum(1).next()
        xs = xsrc.rearrange("(c p) t -> p c t", p=128)
        wgs = self.wb_gu[l].rearrange("(k p) n -> p k n", p=128)
        wds = self.wb_down[l].rearrange("(k p) n -> p k n", p=128)
        fm3 = lambda t: t.rearrange("(c p) t -> p c t", p=128)
        for (t0, T, isctx) in TILES:
            if isctx and last:
                continue
            col = 1 if isctx else 0
            x, bx = xp.next()
            S.dma("sp", x[:, :, :T], xs[:, :, t0:t0 + T], writes=[bx])
            g, bg_ = gp.next()
            S.dma("sp", g[:, :, :T], fm3(self.gates)[:, :, t0:t0 + T], writes=[bg_])
            brs_ = []
            for src in (self.brA, self.brB, self.brC):
                b_, bb_ = brp.next()
                S.dma("sp", b_[:, :, :T], fm3(src)[:, :, t0:t0 + T], writes=[bb_])
                brs_.append((b_, bb_))
            mg, bmg = mgp.next()
            for c in range(8):
                t1, bt1 = t1p.next()
                t2, bt2 = t2p.next()
                self.tt("dve", t1[:, :T], g[:, c, :T], brs_[0][0][:, c, :T], ALU.mult, [bg_, brs_[0][1]], [bt1])
                self.tt("pool", t2[:, :T], g[:, 8 + c, :T], brs_[1][0][:, c, :T], ALU.mult, [bg_, brs_[1][1]], [bt2])
                self.tt("dve", t1[:, :T], t1[:, :T], t2[:, :T], ALU.add, [bt1, bt2], [bt1])
                self.tt("pool", t2[:, :T], g[:, 16 + c, :T], brs_[2][0][:, c, :T], ALU.mult, [bg_, brs_[2][1], bt1], [bt2])
                self.tt("dve", mg[:, c, :T], t1[:, :T], t2[:, :T], ALU.add, [bt1, bt2], [bmg])
            for j in range(8):
                ps, bps = banks.next()
                for k in range(8):
                    self.mm(ps[:, :T], w[:, k, j * 128:(j + 1) * 128], mg[:, k, :T], k == 0, k == 7, [bw, bmg], [bps])
                self.stt(x[:, j, :T], ps[:, :T], self.modv[:, 16 + j, col:col + 1], x[:, j, :T], ALU.mult, ALU.add,
                         [bps, bx] + mb, [bx])
            xn, bxn = xnp.next()
            sq, bsq = sqp.next()
            h, bh = hp.next()
            rs, brs = rsp.next()
            self.norm_mod(ph, x, bx, T, 1, col, xn, bxn, sq, bsq, h, bh, rs, brs, psn)
            f, bf = fp.next()
            for gi in range(11):
                wg, bwg = wgp.next()
                S.dma("sp", wg[:, :, 0:256], wgs[:, :, gi * 256:(gi + 1) * 256], writes=[bwg])
                S.dma("sp", wg[:, :, 256:512], wgs[:, :, DFF + gi * 256:DFF + (gi + 1) * 256], writes=[bwg])
                pss = []
                for j in range(4):
                    ps, bps = banks.next()
                    for k in range(8):
                        self.mm(ps[:, :T], wg[:, k, j * 128:(j + 1) * 128], h[:, k, :T], k == 0, k == 7, [bwg, bh], [bps])
                    pss.append((ps, bps))
                for j in range(2):
                    tmp, btmp = tmpp.next()
                    self.act(tmp[:, :T], pss[j][0][:, :T], AF.Silu, [pss[j][1]], [btmp])
                    self.tt("dve", f[:, gi * 2 + j, :T], tmp[:, :T], pss[2 + j][0][:, :T], ALU.mult,
                            [btmp, pss[2 + j][1]], [bf])
            for half in range(2):
                wd, bwd = wdp.next()
                S.dma("sp", wd[:], wds[:, :, half * 512:(half + 1) * 512], writes=[bwd])
                for j in range(4):
                    jj = half * 4 + j
                    ps, bps = banks.next()
                    for k in range(22):
                        self.mm(ps[:, :T], wd[:, k, j * 128:(j + 1) * 128], f[:, k, :T], k == 0, k == 21, [bwd, bf], [bps])
                    self.stt(x[:, jj, :T], ps[:, :T], self.modv[:, 40 + jj, col:col + 1], x[:, jj, :T], ALU.mult, ALU.add,
                             [bps, bx] + mb, [bx])
            if not last:
                S.dma("pool", fm3(self.x1)[:, :, t0:t0 + T], x[:, :, :T], reads=[bx])
            else:
                sq2, bsq2 = sqp.next()
                self.act(sq2[:, :, :T], x[:, :, :T], AF.Square, [bx], [bsq2])
                ps, bps = psn
                for c in range(8):
                    self.mm(ps[:, :T], self.ones_b[:, 0, :], sq2[:, c, :T], c == 0, c == 7, [bsq2] + cb, [bps])
                rs2, brs2 = rsp.next()
                self.rsqrt_(rs2[:, :T], ps[:, :T], [bps] + cb, [brs2], self.eps_t)
                xn2, bxn2 = xnp.next()
                self.tt("dve", xn2[:, :, :T], x[:, :, :T], bc(rs2[:, :T], [128, 8, T], 1), ALU.mult, [bx, brs2], [bxn2])
                self.tt("pool", xn2[:, :, :T], xn2[:, :, :T], bc(self.fing_t[:], [128, 8, T], 2), ALU.mult,
                        [bxn2] + cb, [bxn2])
                S.dma("pool", fm3(self.yT)[:, :, t0 - LC:t0 - LC + T], xn2[:, :, :T], reads=[bxn2])
        ph.close()

    def phase4(self, l, last):
        S = self.S
        ph = Phase(self, f"p4_{l}")
        cb = [self.cbuf]
        wna, bwna = ph.sb([64, 16, D], BF16)
        S.dma("sp", wna[:], self.wb_sq["w_na_out"][l].rearrange("(h p) n -> p h n", p=64), writes=[bwna])
        E, bE = ph.sb([64, 16, 15, 64], BF16)
        tmpph = Phase(self, f"p4e_{l}")
        ep = tmpph.pool(2, [64, 2, 15, 64])
        for i in range(8):
            e32, be32 = ep.next()
            S.dma("sp", e32[:], self.rpbT[l][:, i * 2:(i + 1) * 2], writes=[be32])
            self.act(E[:, i * 2:(i + 1) * 2], e32[:], AF.Exp, [be32], [bE])
        tmpph.close()
        kTc, bkTc = ph.sb([128, 8, LC], BF16)
        S.dma("sp", kTc[:], self.nak.rearrange("(c p) t -> p c t", p=128)[:, :, 0:LC], writes=[bkTc])
        Vc0, bVc0 = ph.sb([128, 2, D], BF16)
        S.dma("sp", Vc0[:], self.nav[0:LC, :].rearrange("(j p) f -> p j f", p=128), writes=[bVc0])
        Vc, bVc = ph.sb([128, 2, 16, 65], BF16)
        self.ms("pool", Vc[:, :, :, 64:65], 1.0, [bVc])
        self.cp("pool", Vc[:, :, :, 0:64], Vc0[:].rearrange("p j (h f) -> p j h f", f=64), [bVc0], [bVc])
        kTp = ph.pool(1, [128, 8, 960], BF16)
        V0p = ph.pool(1, [64, 8, D], BF16)
        Vp = ph.pool(1, [64, 15, 16, 65], BF16)
        for (V_, bV_) in Vp.items:
            self.ms("pool", V_[:, :, :, 64:65], 1.0, [bV_])
        srowp = ph.pool(2, [128, 512])
        rbp = ph.pool(2, [64, 512])
        bcbp = ph.psum(1)
        qp = ph.pool(1, [128, 8, 512], BF16)
        oTp = ph.pool(1, [64, 16, 512], BF16)
        Pp = ph.pool(3, [128, 512], BF16)
        Pfp = ph.pool(3, [64, 512])
        stp = ph.pool(2, [128, 4, 512], BF16)
        sbanks = ph.psum(3)
        accs = ph.psum(3)
        pbank = ph.psum(1)
        naqs = self.naq.rearrange("(c p) t -> p c t", p=128)
        naks = self.nak.rearrange("(c p) t -> p c t", p=128)
        r0f = lambda qr: min(max(qr - 4, 0), ROWS - 8)
        for (t0, T, isctx) in TILES:
            if isctx and last:
                continue
            q, bq = qp.next()
            S.dma("sp", q[:, :, :T], naqs[:, :, t0:t0 + T], writes=[bq])
            rows = []
            if not isctx:
                R0 = (t0 - LC) // GRID_W
                kmin, kmax = r0f(R0), r0f(R0 + 7) + 7
                nr = kmax - kmin + 1
                kT, bkT = kTp.next()
                S.dma("sp", kT[:, :, 0:nr * 64], naks[:, :, LC + kmin * 64:LC + (kmax + 1) * 64], writes=[bkT])
                V, bV = Vp.next()
                for r_lo in range(0, nr, 8):
                    r_n = min(8, nr - r_lo)
                    V0, bV0 = V0p.next()
                    S.dma("sp", V0[:, 0:r_n, :],
                          self.nav[LC + (kmin + r_lo) * 64:LC + (kmin + r_lo + r_n) * 64, :].rearrange("(r p) f -> p r f", p=64),
                          writes=[bV0])
                    self.cp("pool", V[:, r_lo:r_lo + r_n, :, 0:64], V0[:, 0:r_n, :].rearrange("p r (h f) -> p r h f", f=64),
                            [bV0], [bV])
                for kr in range(kmin, kmax + 1):
                    qs = [qr for qr in range(R0, R0 + 8) if r0f(qr) <= kr <= r0f(qr) + 7]
                    if qs:
                        rows.append((kr, qs[0], qs[-1] + 1))
            oT, boT = oTp.next()
            steps = []
            for h in range(16):
                steps.append(("ctx", h, 0, None))
                for ri, row in enumerate(rows):
                    steps.append(("row", h, ri, row))
                steps.append(("ctx", h, 1, None))
            state = {}

            def emit_qk(st_):
                kind, h, a, row = st_
                c, po = h // 2, (h % 2) * 64
                ps, bps = sbanks.next()
                P, bP = Pp.next()
                if kind == "ctx":
                    self.mm(ps[:, :T], kTc[po:po + 64, c, a * 128:(a + 1) * 128], q[po:po + 64, c, :T], True, True,
                            [bkTc, bq], [bps])
                    self.act(P[:, :T], ps[:, :T], AF.Exp, [bps], [bP], scale=0.125)
                else:
                    kr, qlo, qhi = row
                    cs, nq = (qlo - R0) * 64, qhi - qlo
                    n = nq * 64
                    self.mm(ps[0:64, 0:n], kT[po:po + 64, c, (kr - kmin) * 64:(kr - kmin + 1) * 64],
                            q[po:po + 64, c, cs:cs + n], True, True, [bkT, bq], [bps])
                    Pf, bPf = Pfp.next()
                    self.act(Pf[:, 0:n], ps[0:64, 0:n], AF.Exp, [bps], [bPf], scale=0.125)
                    elo = qlo - kr + 7
                    self.tt("dve" if a % 2 == 0 else "pool", P[0:64, 0:n].rearrange("p (a b) -> p a b", b=64),
                            Pf[:, 0:n].rearrange("p (a b) -> p a b", b=64), E[:, h, elo:elo + nq, :], ALU.mult,
                            [bPf, bE], [bP])
                return (P, bP)

            def emit_pv(st_, pt):
                kind, h, a, row = st_
                P, bP = pt
                if kind == "ctx" and a == 0:
                    state["acc"] = accs.next()
                acc, bacc = state["acc"]
                if kind == "ctx":
                    first = (a == 0)
                    self.mm(acc[0:65, :T], Vc[:, a, h, :], P[:, :T], first, not first, [bVc, bP], [bacc])
                    if not first:
                        srow, bsrow = srowp.next()
                        self.cp("act", srow[64:65, :T], acc[64:65, :T], [bacc], [bsrow])
                        self.S.op("dve", lambda e, o=srow[64:65, :T]: e.reciprocal(out=o, in_=o), [bsrow], [bsrow])

                        def fin(h=h, acc=acc, bacc=bacc, srow=srow, bsrow=bsrow):
                            bcb, bbcb = bcbp.next()
                            self.mm(bcb[0:64, :T], self.ones_f[64:65, 0:64], srow[64:65, :T], True, True, [bsrow] + cb, [bbcb])
                            rb, brb = rbp.next()
                            self.cp("act", rb[:, :T], bcb[0:64, :T], [bbcb], [brb])
                            self.tt("dve", oT[:, h, :T], acc[0:64, :T], rb[:, :T], ALU.mult, [bacc, brb], [boT])
                        deferred.append([min(6, len(rows) + 2), fin])
                else:
                    kr, qlo, qhi = row
                    cs, n = (qlo - R0) * 64, (qhi - qlo) * 64
                    self.mm(acc[0:65, cs:cs + n], V[:, kr - kmin, h, :], P[0:64, 0:n], False, False,
                            [bV, bP], [bacc])

            LA = 2
            pts = {}
            deferred = []
            for i in range(len(steps) + LA):
                if i < len(steps):
                    pts[i] = emit_qk(steps[i])
                if i - LA >= 0:
                    emit_pv(steps[i - LA], pts.pop(i - LA))
                for dfr in list(deferred):
                    dfr[0] -= 1
                    if dfr[0] <= 0:
                        dfr[1]()
                        deferred.remove(dfr)
            for dfr in deferred:
                dfr[1]()
            for half in range(2):
                stg, bst = stp.next()
                for j in range(4):
                    jj = half * 4 + j
                    ps, bps = pbank.next()
                    for h in range(16):
                        self.mm(ps[:, :T], wna[:, h, jj * 128:(jj + 1) * 128], oT[:, h, :T], h == 0, h == 15,
                                [bwna, boT], [bps])
                    self.cp("act" if j % 2 == 0 else "dve", stg[:, j, :T], ps[:, :T], [bps], [bst])
                S.dma("pool", self.brC[half * 512:(half + 1) * 512, t0:t0 + T].rearrange("(j p) t -> p j t", p=128),
                      stg[:, :, :T], reads=[bst])
        ph.close()

    def phase3a(self, l):
        S = self.S
        ph = Phase(self, f"p3a_{l}")
        cb = [self.cbuf]
        dcw, bdcw = ph.sb([128, 24, 5])
        S.dma("sp", dcw[:], self.dn_conv[l], writes=[bdcw])
        diag, bdiag = ph.sb([128, 120, 128], BF16)
        for cc in range(24):
            for k in range(5):
                self.ts("dve" if (cc * 5 + k) % 2 == 0 else "pool", diag[:, cc * 5 + k, :], self.cm32[:, CM_ID, :],
                        dcw[:, cc, k:k + 1], None, ALU.mult, None, [bdcw] + cb, [bdiag])
        pinp = ph.pool(2, [128, 24, 512 + 4], BF16)
        accp = ph.pool(2, [128, 8, 512])
        sqp = ph.pool(1, [128, 8, 512], BF16)
        rsp = ph.pool(1, [128, 8, 512])
        sbp = ph.pool(1, [128, 8, 512], BF16)
        outp = ph.pool(2, [128, 8, 512], BF16)
        ropep = ph.pool(2, [128, 2, 512])
        t1p = ph.pool(2, [128, 512])
        t2p = ph.pool(2, [128, 512])
        stp = ph.pool(2, [128, 1024], BF16)
        pst = ph.psum(2, BF16, 1024)
        pso = ph.psum(3)
        psp = ph.psum(3)
        src = self.dnpre.rearrange("(c p) t -> p c t", p=128)
        ident = self.cmb[:, 0, :]
        perm = self.cmb[:, 1, :]

        def transposes(t, bt, T, dst, t0):
            for sidx in range(T // 128):
                ps, bps = pst.next()
                for hh in range(8):
                    self.tr(ps[:, hh * 128:(hh + 1) * 128], t[:, hh, sidx * 128:(sidx + 1) * 128], ident, [bt] + cb, [bps])
                stg, bst = stp.next()
                self.cp("act" if sidx % 2 == 0 else "dve", stg[:], ps[:], [bps], [bst])
                S.dma("pool", dst[t0 + sidx * 128:t0 + (sidx + 1) * 128, :], stg[:], reads=[bst])

        for (t0, T, isctx) in TILES:
            s0, s1 = (0, LC) if isctx else (LC, NT)
            lo, hi = max(t0 - 2, s0), min(t0 + T + 2, s1)
            pin, bpin = pinp.next()
            if lo > t0 - 2:
                self.ms("pool", pin[:, :, 0:2], 0.0, [bpin])
            if hi < t0 + T + 2:
                self.ms("pool", pin[:, :, T + 2:T + 4], 0.0, [bpin])
            S.dma("sp", pin[:, :, lo - (t0 - 2):hi - (t0 - 2)], src[:, :, lo:hi], writes=[bpin])
            if not isctx:
                rope, brope = ropep.next()
                S.dma("sp", rope[:, 0, :T], self.ropeC[:, t0 - LC:t0 - LC + T], writes=[brope])
                S.dma("sp", rope[:, 1, :T], self.ropeS[:, t0 - LC:t0 - LC + T], writes=[brope])
            for part in range(3):
                acc, bacc = accp.next()
                out, bout = outp.next()
                for c in range(8):
                    cc = part * 8 + c
                    ps, bps = pso.next()
                    for k in range(5):
                        self.mm(ps[:, :T], diag[:, cc * 5 + k, :], pin[:, cc, k:k + T], k == 0, k == 4, [bpin, bdiag], [bps])
                    if part == 2:
                        self.act(out[:, c, :T], ps[:, :T], AF.Silu, [bps], [bout])
                    else:
                        self.act(acc[:, c, :T], ps[:, :T], AF.Silu, [bps], [bacc])
                if part == 2:
                    transposes(out, bout, T, self.dvt, t0)
                    continue
                sq, bsq = sqp.next()
                self.act(sq[:, :, :T], acc[:, :, :T], AF.Square, [bacc], [bsq])
                rs, brs = rsp.next()
                for c in range(8):
                    ps, bps = pso.next()
                    self.mm(ps[:, :T], self.ones_b[:, 2 if part == 0 else 3, :], sq[:, c, :T], True, True, [bsq] + cb, [bps])
                    self.rsqrt_(rs[:, c, :T], ps[:, :T], [bps] + cb, [brs], self.eps128_t if part == 0 else self.eps_t)
                if isctx:
                    self.tt("dve", out[:, :, :T], acc[:, :, :T], rs[:, :, :T], ALU.mult, [bacc, brs], [bout])
                else:
                    sb_, bsb = sbp.next()
                    self.cp("pool", sb_[:, :, :T], acc[:, :, :T], [bacc], [bsb])
                    for c in range(8):
                        ps, bps = psp.next()
                        self.mm(ps[:, :T], perm, sb_[:, c, :T], True, True, [bsb] + cb, [bps])
                        t1, bt1 = t1p.next()
                        t2, bt2 = t2p.next()
                        self.tt("pool", t1[:, :T], acc[:, c, :T], rope[:, 0, :T], ALU.mult, [bacc, brope], [bt1])
                        self.tt("dve", t2[:, :T], ps[:, :T], rope[:, 1, :T], ALU.mult, [bps, brope], [bt2])
                        self.tt("dve", t1[:, :T], t1[:, :T], t2[:, :T], ALU.add, [bt1, bt2], [bt1])
                        self.tt("dve", out[:, c, :T], t1[:, :T], rs[:, c, :T], ALU.mult, [bt1, brs], [bout])
                dst = self.dq if part == 0 else self.dk
                S.dma("pool", dst.rearrange("(c p) t -> p c t", p=128)[:, :, t0:t0 + T], out[:, :, :T], reads=[bout])
                if part == 1:
                    transposes(out, bout, T, self.dkt, t0)
        ph.close()

    def phase3d(self, l, last):
        S = self.S
        ph = Phase(self, f"p3d_{l}")
        cb = [self.cbuf]
        w, bw = self.load_w_sq(ph, "w_dn_out", l)
        ofp = ph.pool(2, [128, 8, 512])
        obp = ph.pool(2, [128, 8, 512])
        zp = ph.pool(2, [128, 8, 512], BF16)
        sqp = ph.pool(1, [128, 8, 512], BF16)
        rsp = ph.pool(2, [128, 512])
        tp = ph.pool(2, [128, 512])
        yp = ph.pool(2, [128, 8, 512], BF16)
        stp = ph.pool(2, [128, 4, 512], BF16)
        banks = ph.psum(5)
        pso = ph.psum(3)
        fm3 = lambda t: t.rearrange("(c p) t -> p c t", p=128)
        for (t0, T, isctx) in TILES:
            if isctx and last:
                continue
            of, bof = ofp.next()
            ob, bob = obp.next()
            z, bz = zp.next()
            S.dma("sp", of[:, :, :T], fm3(self.ofw)[:, :, t0:t0 + T], writes=[bof])
            S.dma("sp", ob[:, :, :T], fm3(self.obw)[:, :, t0:t0 + T], writes=[bob])
            S.dma("sp", z[:, :, :T], fm3(self.zs)[:, :, t0:t0 + T], writes=[bz])
            self.tt("pool", of[:, :, :T], of[:, :, :T], ob[:, :, :T], ALU.add, [bof, bob], [bof])
            sq, bsq = sqp.next()
            self.act(sq[:, :, :T], of[:, :, :T], AF.Square, [bof], [bsq])
            y, by = yp.next()
            for c in range(8):
                ps, bps = pso.next()
                self.mm(ps[:, :T], self.ones_b[:, 1, :], sq[:, c, :T], True, True, [bsq] + cb, [bps])
                rs, brs = rsp.next()
                self.rsqrt_(rs[:, :T], ps[:, :T], [bps] + cb, [brs], self.eps_t)
                t, bt = tp.next()
                self.tt("dve", t[:, :T], of[:, c, :T], rs[:, :T], ALU.mult, [bof, brs], [bt])
                self.stt(y[:, c, :T], t[:, :T], self.vcol(l, V_DNG, 0), z[:, c, :T], ALU.mult, ALU.mult,
                         [bt, bz] + cb, [by])
            self.proj_store(y, by, T, w, bw, banks, stp, self.brB, t0)
        ph.close()

    def phase3c(self, l):
        S = self.S
        ph = Phase(self, f"p3c_{l}")
        cb = [self.cbuf]
        cm = self.cm32
        I64b = self.cmb[0:64, 0, 0:64]
        rrs = ph.psum(2)
        dks = self.dk.rearrange("(c p) t -> p c t", p=128)
        dqs = self.dq.rearrange("(c p) t -> p c t", p=128)
        f3 = [64, 8, 64]
        flat = lambda t: t.rearrange("p h f -> p (h f)")
        nchunks = NT // 64
        pre_done = [0, 0]
        seq_done = [0, 0]
        hand = [None, None]
        orders = [list(range(nchunks)), list(range(3, -1, -1)) + list(range(nchunks - 1, 3, -1))]

        def mk_hand():
            hs = []
            for _ in range(2):
                hs.append(dict(u=ph.sb([64, 8, 128]), wT=ph.sb([128, 8, 64], BF16), kdec=ph.sb([64, 8, 128], BF16),
                               qdT=ph.sb([128, 8, 64], BF16), Aqk=ph.sb(f3, BF16), sm=ph.sb([128, 16])))
            return hs

        def pre(d):
            U = cm[0:64, CM_UF if d == 0 else CM_UB, 0:64]
            negU = cm[0:64, CM_NUF if d == 0 else CM_NUB, 0:64]
            Mneg = cm[0:64, CM_MNF if d == 0 else CM_MNB, 0:64]
            MnegT = cm[0:64, CM_MNB if d == 0 else CM_MNF, 0:64]
            Ms = cm[0:64, CM_MSF if d == 0 else CM_MSB, 0:64]
            lastc = 63 if d == 0 else 0
            rr = ph.psum(2)
            psX, bpsX = ph.psum(1).next()
            kTbp = ph.pool(2, [128, 8, 256], BF16)
            qTbp = ph.pool(2, [128, 8, 256], BF16)
            ktbp = ph.pool(2, [64, 4, D], BF16)
            vtbp = ph.pool(2, [64, 4, D], BF16)
            bgbp = ph.pool(2, [64, 4, 32])
            Gm, bGm = ph.sb(f3)
            gB, bgB = ph.sb(f3)
            Dm, bDm = ph.sb(f3)
            DT, bDT = ph.sb(f3)
            Eg, bEg = ph.sb([128, 8, 64])
            ed, bed = ph.sb([64, 8])
            be, bbe = ph.sb([64, 8])
            L0, bL0 = ph.sb(f3)
            Lp = ph.pool(2, f3, BF16)
            Np = ph.pool(2, f3, BF16)
            ImN, bImN = ph.sb(f3, BF16)
            Xbp = ph.pool(2, f3, BF16)
            M1T, bM1T = ph.sb(f3, BF16)
            M2T, bM2T = ph.sb(f3, BF16)
            curb = None
            for idx, n in enumerate(orders[d]):
                while idx - seq_done[d] >= 2:
                    yield
                H = hand[d][idx % 2]
                u, bu = H["u"]
                wT, bwT = H["wT"]
                kdec, bkdec = H["kdec"]
                qdT, bqdT = H["qdT"]
                Aqk, bAqk = H["Aqk"]
                sm, bsm = H["sm"]
                b, nn = n // 4, n % 4
                if b != curb:
                    curb = b
                    kTb, bkTb = kTbp.next()
                    qTb, bqTb = qTbp.next()
                    ktb, bktb = ktbp.next()
                    vtb, bvtb = vtbp.next()
                    bgb, bbgb = bgbp.next()
                    tk = slice(b * 256, (b + 1) * 256)
                    S.dma("sp", kTb[:], dks[:, :, tk], writes=[bkTb])
                    S.dma("sp", qTb[:], dqs[:, :, tk], writes=[bqTb])
                    S.dma("sp", ktb[:], self.dkt[tk, :].rearrange("(n p) f -> p n f", p=64), writes=[bktb])
                    S.dma("sp", vtb[:], self.dvt[tk, :].rearrange("(n p) f -> p n f", p=64), writes=[bvtb])
                    S.dma("sp", bgb[:], self.bg[tk, :].rearrange("(n p) c -> p n c", p=64), writes=[bbgb])
                kT_c = kTb[:, :, nn * 64:(nn + 1) * 64]
                qT_c = qTb[:, :, nn * 64:(nn + 1) * 64]
                kt_c = ktb[:, nn, :].rearrange("p (h f) -> p h f", f=128)
                vt_c = vtb[:, nn, :].rearrange("p (h f) -> p h f", f=128)
                beta = bgb[:, nn, 8 * d:8 * d + 8]
                g = bgb[:, nn, 16 + 8 * d:24 + 8 * d]
                self.tt("pool", Gm[:], bc(U, f3, 1), bc(g, f3, 2), ALU.mult, cb + [bbgb], [bGm])
                self.cp("pool", gB[:], bc(g, f3, 2), [bbgb], [bgB])
                psR, bpsR = rr.next()
                self.mm(psR[0:64, :], self.ones_f[0:64, 0:64], flat(Gm[:]), True, False, cb + [bGm], [bpsR])
                self.mm(psR[0:64, :], negU, flat(gB[:]), False, True, cb + [bgB], [bpsR])
                psR3 = psR[0:64, :].rearrange("p (h f) -> p h f", f=64)
                self.tt("dve", Dm[:], bc(Mneg, f3, 1), psR3, ALU.subtract, cb + [bpsR], [bDm])
                self.act(Dm[:], Dm[:], AF.Exp, [bDm], [bDm])
                self.tt("dve", DT[:], psR3, bc(MnegT, f3, 1), ALU.add, cb + [bpsR], [bDT])
                self.act(DT[:], DT[:], AF.Exp, [bDT], [bDT])
                self.act(ed[:], psR3[:, :, lastc], AF.Exp, [bpsR], [bed])
                psE, bpsE = rr.next()
                self.mm(psE[:, :], self.ones_f[0:64, :], flat(Gm[:]), True, True, cb + [bGm], [bpsE])
                self.act(flat(Eg[:]), psE[:, :], AF.Exp, [bpsE], [bEg])
                psSm, bpsSm = rr.next()
                self.mm(psSm[:, 0:8], self.ones_f[0:64, :], g, True, True, cb + [bbgb], [bpsSm])
                self.mm(psSm[0:64, 8:16], U, g, True, True, cb + [bbgb], [bpsSm])
                self.act(sm[:], psSm[:, 0:16], AF.Exp, [bpsSm], [bsm])
                self.tt("dve", be[:], beta, sm[0:64, 8:16], ALU.mult, [bbgb, bsm], [bbe])
                yield
                psKK, bpsKK = rr.next()
                psQK, bpsQK = rr.next()
                for h in range(8):
                    self.mm(psKK[0:64, h * 64:(h + 1) * 64], kT_c[:, h, :], kT_c[:, h, :], True, True, [bkTb], [bpsKK])
                for h in range(8):
                    self.mm(psQK[0:64, h * 64:(h + 1) * 64], kT_c[:, h, :], qT_c[:, h, :], True, True, [bkTb, bqTb], [bpsQK])
                self.tt("dve", flat(L0[:]), psKK[0:64, :], flat(Dm[:]), ALU.mult, [bpsKK, bDm], [bL0])
                self.tt("dve", flat(Aqk[:]), psQK[0:64, :], flat(DT[:]), ALU.mult, [bpsQK, bDT], [bAqk])
                self.tt("pool", L0[:], L0[:], bc(Ms, f3, 1), ALU.mult, [bL0] + cb, [bL0])
                Lc, bLc = Lp.next()
                self.tt("dve", Lc[:], L0[:], bc(beta, f3, 2), ALU.mult, [bL0, bbgb], [bLc])
                psT32, bpsT = rr.next()
                psT = psT32[0:64, :].bitcast(BF16)
                for h in range(8):
                    self.tr(psT[:, h * 64:(h + 1) * 64], Lc[:, h, :], I64b, [bLc] + cb, [bpsT])
                Nc, bNc = Np.next()
                self.cp("act", flat(Nc[:]), psT[:, 0:512], [bpsT], [bNc])
                self.tt("pool", ImN[:], bc(I64b, f3, 1), Nc[:], ALU.subtract, cb + [bNc], [bImN])
                yield
                self.mm(psX[0:64, :], I64b, flat(ImN[:]), True, True, cb + [bImN], [bpsX])
                Xb, bXb = Xbp.next()
                self.cp("act", flat(Xb[:]), psX[0:64, :], [bpsX], [bXb])
                for k in range(1, 6):
                    psL, bpsL = rr.next()
                    for h in range(8):
                        self.mm(psL[0:64, h * 64:(h + 1) * 64], Nc[:, h, :], Lc[:, h, :], True, True, [bNc, bLc], [bpsL])
                    if k < 5:
                        psN, bpsN = rr.next()
                        for h in range(8):
                            self.mm(psN[0:64, h * 64:(h + 1) * 64], Lc[:, h, :], Nc[:, h, :], True, True, [bNc, bLc], [bpsN])
                    Lc, bLc = Lp.next()
                    self.cp("act", flat(Lc[:]), psL[0:64, :], [bpsL], [bLc])
                    if k < 5:
                        Nc, bNc = Np.next()
                        self.cp("dve", flat(Nc[:]), psN[0:64, :], [bpsN], [bNc])
                    for h in range(8):
                        self.mm(psX[0:64, h * 64:(h + 1) * 64], Lc[:, h, :], Xb[:, h, :], False, True, [bLc, bXb], [bpsX])
                    if k < 5:
                        Xb, bXb = Xbp.next()
                        self.cp("act", flat(Xb[:]), psX[0:64, :], [bpsX], [bXb])
                    yield
                psX3 = psX[0:64, :].rearrange("p (h f) -> p h f", f=64)
                self.tt("dve", M1T[:], psX3, bc(beta, f3, 2), ALU.mult, [bpsX, bbgb], [bM1T])
                self.tt("dve", M2T[:], psX3, bc(be[:], f3, 2), ALU.mult, [bpsX, bbe], [bM2T])
                pu = [rr.next(), rr.next()]
                for h in range(8):
                    p_, bp_ = pu[h // 4]
                    self.mm(p_[0:64, (h % 4) * 128:(h % 4 + 1) * 128], M1T[:, h, :], vt_c[:, h, :], True, True,
                            [bM1T, bvtb], [bp_])
                for i in range(2):
                    self.cp("act", flat(u[:, i * 4:(i + 1) * 4, :]), pu[i][0][0:64, :], [pu[i][1]], [bu])
                psW, bpsW = rr.next()
                for h in range(8):
                    self.mm(psW[:, h * 64:(h + 1) * 64], kt_c[:, h, :], M2T[:, h, :], True, True, [bktb, bM2T], [bpsW])
                self.cp("dve", flat(wT[:]), psW[:, :], [bpsW], [bwT])
                self.tt("pool", kdec[:], kt_c, bc(ed[:], [64, 8, 128], 2), ALU.mult, [bktb, bed], [bkdec])
                self.tt("pool", qdT[:], qT_c, Eg[:], ALU.mult, [bqTb, bEg], [bqdT])
                pre_done[d] = idx + 1
                yield

        def seq(d):
            odst = self.ofw if d == 0 else self.obw
            vnew, bvnew = ph.sb([64, 8, 128], BF16)
            ost, bost = ph.sb([128, 8, 64])
            St, bSt = ph.sb([128, 8, 128])
            Sb, bSb = ph.sb([128, 8, 128], BF16)
            self.ms("pool", St[:], 0.0, [bSt])
            self.ms("pool", Sb[:], 0.0, [bSb])
            for idx, n in enumerate(orders[d]):
                while pre_done[d] <= idx:
                    yield
                H = hand[d][idx % 2]
                u, bu = H["u"]
                wT, bwT = H["wT"]
                kdec, bkdec = H["kdec"]
                qdT, bqdT = H["qdT"]
                Aqk, bAqk = H["Aqk"]
                sm, bsm = H["sm"]
                pv = [rrs.next(), rrs.next()]
                for h in range(8):
                    p_, bp_ = pv[h // 4]
                    self.mm(p_[0:64, (h % 4) * 128:(h % 4 + 1) * 128], wT[:, h, :], Sb[:, h, :], True, True,
                            [bwT, bSb], [bp_])
                for i in range(2):
                    self.tt("dve", flat(vnew[:, i * 4:(i + 1) * 4, :]), flat(u[:, i * 4:(i + 1) * 4, :]), pv[i][0][0:64, :],
                            ALU.subtract, [bu, pv[i][1]], [bvnew])
                psO, bpsO = rrs.next()
                for h in range(8):
                    self.mm(psO[:, h * 64:(h + 1) * 64], Sb[:, h, :], qdT[:, h, :], True, False, [bSb, bqdT], [bpsO])
                    self.mm(psO[:, h * 64:(h + 1) * 64], vnew[:, h, :], Aqk[:, h, :], False, True, [bvnew, bAqk], [bpsO])
                self.cp("act", flat(ost[:]), psO[:, :], [bpsO], [bost])
                S.dma("sp", odst[:, n * 64:(n + 1) * 64].rearrange("(h p) t -> p h t", p=128), ost[:], reads=[bost])
                pS = [rrs.next(), rrs.next()]
                for h in range(8):
                    p_, bp_ = pS[h // 4]
                    self.mm(p_[:, (h % 4) * 128:(h % 4 + 1) * 128], kdec[:, h, :], vnew[:, h, :], True, True,
                            [bkdec, bvnew], [bp_])
                self.tt("dve", St[:], St[:], bc(sm[:, 0:8], [128, 8, 128], 2), ALU.mult, [bSt, bsm], [bSt])
                for i in range(2):
                    self.tt("dve", flat(St[:, i * 4:(i + 1) * 4, :]), flat(St[:, i * 4:(i + 1) * 4, :]), pS[i][0][:, :],
                            ALU.add, [bSt, pS[i][1]], [bSt])
                self.cp("act", Sb[:], St[:], [bSt], [bSb])
                seq_done[d] = idx + 1
                yield

        hand[0] = mk_hand()
        hand[1] = mk_hand()
        gens = [pre(0), pre(1), seq(0), seq(1)]
        alive = [True] * 4
        while any(alive):
            for i, gnr in enumerate(gens):
                if alive[i]:
                    try:
                        next(gnr)
                    except StopIteration:
                        alive[i] = False
        ph.close()


def host_consts():
    f32 = np.float32
    t = np.arange(SEQ)
    inv = (np.float32(10000.0) ** (-np.arange(0, 64, 2, dtype=f32) / f32(64))).astype(f32)
    ang_r = ((t // GRID_W).astype(f32)[:, None] * inv).astype(f32)
    ang_c = ((t % GRID_W).astype(f32)[:, None] * inv).astype(f32)
    C = np.zeros((128, SEQ), f32)
    Sg = np.zeros((128, SEQ), f32)
    for f in range(128):
        ang = ang_r if f < 64 else ang_c
        fi = f % 32
        C[f] = np.cos(ang[:, fi])
        Sg[f] = np.sin(ang[:, fi]) * (-1.0 if (f % 64) < 32 else 1.0)
    cm = np.zeros((128, NCM, 128), f32)
    cm[:, CM_ID, :] = np.eye(128, dtype=f32)
    for m in range(128):
        partner = m + 32 if (m % 64) < 32 else m - 32
        cm[partner, CM_PERM, m] = 1.0
    i = np.arange(64)
    le = (i[:, None] <= i[None, :]).astype(f32)
    ge = (i[:, None] >= i[None, :]).astype(f32)
    cm[:64, CM_UF, :64] = le
    cm[:64, CM_UB, :64] = ge
    cm[:64, CM_MNF, :64] = np.where(i[:, None] >= i[None, :], 0.0, NEG)
    cm[:64, CM_MNB, :64] = np.where(i[:, None] <= i[None, :], 0.0, NEG)
    cm[:64, CM_MSF, :64] = (i[:, None] > i[None, :]).astype(f32)
    cm[:64, CM_MSB, :64] = (i[:, None] < i[None, :]).astype(f32)
    cm[:64, CM_NUF, :64] = -le
    cm[:64, CM_NUB, :64] = -ge
    return dict(ropeC=C, ropeS=Sg, cmat=cm)


def fm(v, nchunk):
    sh = v.shape[:-1]
    return np.ascontiguousarray(np.moveaxis(v.reshape(sh + (nchunk, 128)), -1, -2))


def host_shared(inp):
    f32 = np.float32
    out = dict(host_consts())
    for n in ("w_mod", "w_in", "w_conv_out", "w_dn_out", "w_na_out", "w_out", "w_gu", "w_down"):
        out[n] = np.ascontiguousarray(inp[n], dtype=f32)
    vecs = np.zeros((DEPTH, 128, NVEC), f32)
    vecs[:, :, V_BMOD:V_BMOD + 48] = fm(inp["b_mod"], 48)
    vecs[:, :, V_N1:V_N1 + 8] = fm(inp["norm1_g"], 8)
    vecs[:, :, V_N2:V_N2 + 8] = fm(inp["norm2_g"], 8)
    vecs[:, :, V_CDB:V_CDB + 8] = fm(inp["conv_db"], 8)
    vecs[:, :, V_LNG:V_LNG + 8] = fm(inp["conv_ln_g"], 8)
    vecs[:, :, V_LNB:V_LNB + 8] = fm(inp["conv_ln_b"], 8)
    vecs[:, :, V_DNG] = inp["dn_norm_g"]
    out["vecs"] = vecs
    out["conv_dw"] = np.ascontiguousarray(np.transpose(inp["conv_dw"].reshape(DEPTH, 31, 8, 128), (0, 3, 2, 1)), dtype=f32)
    out["dn_conv"] = np.ascontiguousarray(np.transpose(inp["dn_conv"].reshape(DEPTH, 5, 24, 128), (0, 3, 2, 1)), dtype=f32)
    dnp = np.concatenate([inp["dn_a_log"].reshape(DEPTH, 16), inp["dn_dt_bias"].reshape(DEPTH, 16)], -1)
    out["dnp"] = np.ascontiguousarray(np.broadcast_to(dnp[:, None, :], (DEPTH, 128, 32)), dtype=f32)
    rpb = inp["na_rpb"]
    kc = np.arange(64)[:, None]
    qc = np.arange(64)[None, :]
    c0 = np.clip(qc - 8, 0, 48)
    valid = (kc >= c0) & (kc < c0 + 16)
    idx = np.clip(kc - qc + 15, 0, 30)
    tb = rpb[:, :, ::-1, :][:, :, :, idx]
    tb = np.where(valid[None, None, None], tb, f32(NEG))
    out["rpbT"] = np.ascontiguousarray(np.transpose(tb, (0, 3, 1, 2, 4)), dtype=f32)
    out["fin_g"] = fm(inp["final_norm_g"], 8).astype(f32)
    return out


def host_core(inp, b):
    f32 = np.float32
    xT0 = np.concatenate([inp["ctx"][b].T, inp["x"][b].T], axis=1)
    cvec = np.stack([fm(inp["c"][b], 8), fm(inp["c_ctx"], 8)], axis=-1)
    return dict(xT0=np.ascontiguousarray(xT0, dtype=f32), cvec=np.ascontiguousarray(cvec, dtype=f32))


_NC_CACHE = {}


def kernel(**inputs):
    inp = {k: np.asarray(v) for k, v in inputs.items()}
    if "nc" not in _NC_CACHE:
        _NC_CACHE["nc"] = K().build()
    nc = _NC_CACHE["nc"]
    shared = host_shared(inp)
    n = 8
    in_maps = []
    for core in range(n):
        m = dict(shared)
        m.update(host_core(inp, core % 4))
        in_maps.append(m)
    res = run_bass_kernel_spmd(nc, in_maps, core_ids=list(range(n)))
    out = np.stack([np.ascontiguousarray(res.results[b]["yT"].T) for b in range(4)], axis=0)
    return out.astype(np.float32)
```

```python
import contextlib
import numpy as np
import concourse.bass as bass
import concourse.mybir as mybir
from concourse.bass_utils import run_bass_kernel_spmd

F32 = mybir.dt.float32
BF16 = mybir.dt.bfloat16
ALU = mybir.AluOpType
AF = mybir.ActivationFunctionType

EPOCH = 12000
NDMA_SEM = 12

D = 1024
SEQ = 8192
LC = 256
NT = SEQ + LC
DEPTH = 2
IN_DIM = 12320
DFF = 2816
GRID_W = 64
ROWS = SEQ // GRID_W
EPS = 1e-6
OFF = dict(conv=0, dn_q=2048, dn_k=3072, dn_v=4096, dn_z=5120, beta=6144, alpha=6160,
           na_q=6176, na_k=7200, na_v=8224, gate=9248)
TILES = [(0, 256, True)] + [(LC + 512 * i, 512, False) for i in range(SEQ // 512)]
NEG = -30000.0


class Buf:
    __slots__ = ("name", "w", "r")

    def __init__(self, name=""):
        self.name = name
        self.w = None
        self.r = []


class Sched:
    ENG = ("pe", "act", "dve", "pool", "sp")

    def __init__(self, nc, stack):
        self.nc = nc
        self.stack = stack
        self.ops = {e: [] for e in self.ENG}
        self.count = {e: 0 for e in self.ENG}
        self.esems = {e: [] for e in self.ENG}
        self.waited = {e: {} for e in self.ENG}
        self.dsems = {}
        self.dcount = {}
        self.dnext = {}
        for q in ("sp", "act", "pool"):
            self.dsems[q] = [stack.enter_context(nc.semaphore(f"d_{q}_{i}")) for i in range(NDMA_SEM)]
            self.dcount[q] = [0] * NDMA_SEM
            self.dnext[q] = 0
        self.same_engine_sync = {"pe": False, "act": True, "dve": True, "pool": True, "sp": False}

    def _esem(self, eng, idx):
        ep = idx // EPOCH
        while len(self.esems[eng]) <= ep:
            self.esems[eng].append(self.stack.enter_context(
                self.nc.semaphore(f"e_{eng}_{len(self.esems[eng])}")))
        return self.esems[eng][ep], idx % EPOCH + 1

    def _need(self, eng, tok, waits, force=False):
        if tok is None:
            return
        if tok[0] == "e":
            _, src, idx = tok
            if src == eng and not self.same_engine_sync[eng] and not force:
                return
            key = ("e", src)
            if self.waited[eng].get(key, -1) >= idx:
                return
            self.waited[eng][key] = idx
            waits.append(self._esem(src, idx))
        else:
            _, q, si, val = tok
            key = ("d", q, si)
            if self.waited[eng].get(key, -1) >= val:
                return
            self.waited[eng][key] = val
            waits.append((self.dsems[q][si], val))

    def _deps(self, eng, reads, writes):
        waits = []
        for b in reads:
            self._need(eng, b.w, waits)
        for b in writes:
            self._need(eng, b.w, waits)
            for t in b.r:
                self._need(eng, t, waits)
        return waits

    def _commit(self, tok, reads, writes):
        for b in reads:
            b.r.append(tok)
        for b in writes:
            b.w = tok
            b.r = []

    def op(self, eng, fn, reads=(), writes=()):
        waits = self._deps(eng, reads, writes)
        idx = self.count[eng]
        self.count[eng] += 1
        sem, _ = self._esem(eng, idx)
        self.ops[eng].append((waits, fn, (sem, 1)))
        self._commit(("e", eng, idx), reads, writes)

    def dma(self, q, out, in_, reads=(), writes=()):
        waits = self._deps(q, reads, writes)
        si = self.dnext[q]
        self.dnext[q] = (si + 1) % NDMA_SEM
        prev = self.dcount[q][si]
        if prev > 0:
            self._need(q, ("d", q, si, prev), waits)
        val = prev + 16
        self.dcount[q][si] = val
        sem = self.dsems[q][si]
        self.ops[q].append((waits, lambda e, o=out, i=in_: e.dma_start(out=o, in_=i), (sem, 16)))
        tok = ("d", q, si, val)
        self._commit(tok, reads, writes)
        return tok

    def barrier(self):
        for e in self.ENG:
            waits = []
            for s in self.ENG:
                if self.count[s] > 0:
                    self._need(e, ("e", s, self.count[s] - 1), waits, force=True)
            for q in self.dsems:
                for si in range(NDMA_SEM):
                    if self.dcount[q][si] > 0:
                        self._need(e, ("d", q, si, self.dcount[q][si]), waits)
            self.ops[e].append((waits, None, None))

    def emit(self):
        nc = self.nc
        with nc.Block() as block:
            def run(engname):
                def body(e):
                    for waits, fn, inc in self.ops[engname]:
                        for s, v in waits:
                            e.wait_ge(s, v)
                        if fn is not None:
                            ins = fn(e)
                            ins.then_inc(inc[0], inc[1])
                return body
            block.tensor(run("pe"))
            block.scalar(run("act"))
            block.vector(run("dve"))
            block.gpsimd(run("pool"))
            block.sync(run("sp"))


class RR:
    def __init__(self, items):
        self.items = items
        self.i = 0

    def next(self):
        it = self.items[self.i]
        self.i = (self.i + 1) % len(self.items)
        return it


class Phase:
    def __init__(self, K, name):
        self.K = K
        self.name = name
        self.st = contextlib.ExitStack()
        self.n = 0

    def sb(self, shape, dt=F32):
        self.n += 1
        t = self.st.enter_context(self.K.nc.sbuf_tensor(f"{self.name}_{self.n}", list(shape), dt))
        return t, Buf(f"{self.name}_{self.n}")

    def pool(self, n, shape, dt=F32):
        return RR([self.sb(shape, dt) for _ in range(n)])

    def psum(self, n, dt=F32, cols=512):
        out = []
        for _ in range(n):
            self.n += 1
            t = self.st.enter_context(self.K.nc.psum_tensor(f"{self.name}_ps{self.n}", [128, cols], dt))
            out.append((t, Buf(f"{self.name}_ps{self.n}")))
        return RR(out)

    def close(self):
        self.K.S.barrier()
        self.st.close()


def bc(ap, shape, axis):
    return ap.unsqueeze(axis).to_broadcast(list(shape))


NVEC = 89
V_BMOD, V_N1, V_N2, V_CDB, V_LNG, V_LNB, V_DNG = 0, 48, 56, 64, 72, 80, 88
CM_ID, CM_PERM, CM_UF, CM_UB, CM_MNF, CM_MNB, CM_MSF, CM_MSB, CM_NUF, CM_NUB = range(10)
NCM = 10


class K:
    def __init__(self, debug=(), stop_after=None, layers=DEPTH):
        self.debug = set(debug)
        self.stop_after = stop_after
        self.layers = layers
        self.nc = bass.Bass("TRN2", target_bir_lowering=False)
        self.st = contextlib.ExitStack()
        self.S = Sched(self.nc, self.st)
        self.dram = {}

    def mm(self, out, lhsT, rhs, start, stop, reads, writes):
        self.S.op("pe", lambda e: e.matmul(out, lhsT=lhsT, rhs=rhs, start=start, stop=stop), reads, writes)

    def tr(self, out, in_, ident, reads, writes):
        self.S.op("pe", lambda e: e.transpose(out, in_, ident), reads, writes)

    def act(self, out, in_, func, reads, writes, bias=None, scale=None):
        kw = {}
        if bias is not None:
            kw["bias"] = bias
        if scale is not None:
            kw["scale"] = scale
        self.S.op("act", lambda e: e.activation(out=out, in_=in_, func=func, **kw), reads, writes)

    def tt(self, eng, out, in0, in1, op, reads, writes):
        self.S.op(eng, lambda e: e.tensor_tensor(out=out, in0=in0, in1=in1, op=op), reads, writes)

    def ts(self, eng, out, in0, s1, s2, op0, op1, reads, writes):
        if op1 is None:
            self.S.op(eng, lambda e: e.tensor_scalar(out=out, in0=in0, scalar1=s1, scalar2=None, op0=op0), reads, writes)
        else:
            self.S.op(eng, lambda e: e.tensor_scalar(out=out, in0=in0, scalar1=s1, scalar2=s2, op0=op0, op1=op1), reads, writes)

    def stt(self, out, in0, scalar, in1, op0, op1, reads, writes):
        self.S.op("dve", lambda e: e.scalar_tensor_tensor(out=out, in0=in0, scalar=scalar, in1=in1, op0=op0, op1=op1),
                  reads, writes)

    def cp(self, eng, out, in_, reads, writes):
        if eng == "act":
            self.act(out, in_, AF.Copy, reads, writes)
        else:
            self.S.op(eng, lambda e: e.tensor_copy(out=out, in_=in_), reads, writes)

    def ms(self, eng, out, val, writes):
        self.S.op(eng, lambda e: e.memset(out, val), (), writes)

    def rsqrt_(self, out, in_, reads, writes, eps_ap):
        self.act(out, in_, AF.Ln, reads, writes, bias=eps_ap)
        self.act(out, out, AF.Exp, writes, writes, scale=-0.5)

    def din(self, name, shape, dt=F32):
        t = self.nc.dram_tensor(name, list(shape), dt, kind="ExternalInput").ap()
        self.dram[name] = t
        return t

    def dscr(self, name, shape, dt):
        kind = "ExternalOutput" if name in self.debug else "Internal"
        t = self.nc.dram_tensor(name, list(shape), dt, kind=kind).ap()
        self.dram[name] = t
        return t

    def build(self):
        nc, S = self.nc, self.S
        g = self.din
        self.xT0 = g("xT0", [D, NT])
        self.cvec = g("cvec", [128, 8, 2])
        self.w_mod = g("w_mod", [DEPTH, D, 6 * D])
        self.w_in = g("w_in", [DEPTH, D, IN_DIM])
        self.w_sq = {n: g(n, [DEPTH, D, D]) for n in ("w_conv_out", "w_dn_out", "w_na_out", "w_out")}
        self.w_gu = g("w_gu", [DEPTH, D, 2 * DFF])
        self.w_down = g("w_down", [DEPTH, DFF, D])
        self.vecs = g("vecs", [DEPTH, 128, NVEC])
        self.conv_dw = g("conv_dw", [DEPTH, 128, 8, 31])
        self.dn_conv = g("dn_conv", [DEPTH, 128, 24, 5])
        self.dnp = g("dnp", [DEPTH, 128, 32])
        self.rpbT = g("rpbT", [DEPTH, 64, 16, 15, 64])
        self.fin_g = g("fin_g", [128, 8])
        self.ropeC = g("ropeC", [128, SEQ])
        self.ropeS = g("ropeS", [128, SEQ])
        self.cmat = g("cmat", [128, NCM, 128])
        self.yT = nc.dram_tensor("yT", [D, SEQ], F32, kind="ExternalOutput").ap()
        s = self.dscr
        self.wb_in = s("wb_in", [DEPTH, D, IN_DIM], BF16)
        self.wb_sq = {n: s("wb_" + n, [DEPTH, D, D], BF16) for n in self.w_sq}
        self.wb_gu = s("wb_gu", [DEPTH, D, 2 * DFF], BF16)
        self.wb_down = s("wb_down", [DEPTH, DFF, D], BF16)
        self.x1 = s("x1", [D, NT], F32)
        self.hconv = s("hconv", [D, NT], BF16)
        self.dnpre = s("dnpre", [3 * D, NT], BF16)
        self.zs = s("zs", [D, NT], BF16)
        self.bg = s("bg", [NT, 32], F32)
        self.naq = s("naq", [D, NT], BF16)
        self.nak = s("nak", [D, NT], BF16)
        self.nav = s("nav", [NT, D], BF16)
        self.gates = s("gates", [3 * D, NT], BF16)
        self.brA = s("brA", [D, NT], BF16)
        self.brB = s("brB", [D, NT], BF16)
        self.brC = s("brC", [D, NT], BF16)
        self.dq = s("dq", [D, NT], BF16)
        self.dk = s("dk", [D, NT], BF16)
        self.dkt = s("dkt", [NT, D], BF16)
        self.dvt = s("dvt", [NT, D], BF16)
        self.ofw = s("ofw", [D, NT], F32)
        self.obw = s("obw", [D, NT], F32)
        self.modv_d = s("modv_d", [128, 96], F32)

        self.consts()
        self.phase0()
        done = False
        for l in range(self.layers):
            last = (l == DEPTH - 1)
            xsrc = self.xT0 if l == 0 else self.x1
            steps = [("mod", lambda: self.phase_mod(l)),
                     ("p1", lambda: self.phase1(l, xsrc)),
                     ("p2", lambda: self.phase2(l, last)),
                     ("p3a", lambda: self.phase3a(l)),
                     ("p3c", lambda: self.phase3c(l)),
                     ("p3d", lambda: self.phase3d(l, last)),
                     ("p4", lambda: self.phase4(l, last)),
                     ("p5", lambda: self.phase5(l, xsrc, last))]
            for name, fn in steps:
                fn()
                if self.stop_after == (l, name):
                    done = True
                    break
            if done:
                break
        S.barrier()
        S.emit()
        self.st.close()
        return nc

    def consts(self):
        nc, S, st = self.nc, self.S, self.st
        sb = lambda name, shape, dt=F32: st.enter_context(nc.sbuf_tensor(name, list(shape), dt))
        self.cbuf = Buf("consts")
        cb = [self.cbuf]
        self.cm32 = sb("cm32", [128, NCM, 128])
        S.dma("sp", self.cm32[:], self.cmat, writes=cb)
        self.cmb = sb("cmb", [128, 2, 128], BF16)
        self.cp("dve", self.cmb[:], self.cm32[:, 0:2, :], cb, cb)
        self.ones_b = sb("ones_b", [128, 4, 128], BF16)
        for i, v in enumerate((1.0 / 1024, 1.0 / 128, 128.0, 1.0)):
            self.ms("pool", self.ones_b[:, i, :], v, cb)
        self.ones_f = sb("ones_f", [128, 128], F32)
        self.ms("pool", self.ones_f[:], 1.0, cb)
        self.cst = sb("cst", [128, 4], F32)
        for i, v in enumerate((EPS, 128 * EPS, 1.0, 0.0)):
            self.ms("pool", self.cst[:, i:i + 1], v, cb)
        self.eps_t = self.cst[:, 0:1]
        self.eps128_t = self.cst[:, 1:2]
        self.one_t = self.cst[:, 2:3]
        self.vec_t = sb("vec_t", [128, DEPTH, NVEC])
        S.dma("sp", self.vec_t[:], self.vecs.rearrange("l p n -> p l n"), writes=cb)
        self.fing_t = sb("fing_t", [128, 8])
        S.dma("sp", self.fing_t[:], self.fin_g, writes=cb)
        self.modv = sb("modv", [128, 48, 2])
        self.modA = sb("modA", [128, 2, 8, 2])
        self.mbuf = Buf("mod")
        S.barrier()

    def vcol(self, l, off, c):
        return self.vec_t[:, l, off + c:off + c + 1]

    def phase0(self):
        S = self.S
        for l in range(self.layers):
            pairs = [(self.w_in[l], self.wb_in[l], D), (self.w_gu[l], self.wb_gu[l], D),
                     (self.w_down[l], self.wb_down[l], DFF)]
            pairs += [(self.w_sq[n][l], self.wb_sq[n][l], D) for n in self.w_sq]
            for src, dst, rows in pairs:
                for r in range(0, rows, 128):
                    S.dma("pool", dst[r:r + 128, :], src[r:r + 128, :])
        S.barrier()

    def phase_mod(self, l):
        S = self.S
        ph = Phase(self, f"mod{l}")
        cb = [self.cbuf]
        cv, bcv = ph.sb([128, 8, 2])
        S.dma("sp", cv[:], self.cvec, writes=[bcv])
        self.act(cv[:], cv[:], AF.Silu, [bcv], [bcv])
        wp = ph.pool(2, [128, 8, 768])
        ps, bps = ph.psum(1).next()
        wsrc = self.w_mod[l].rearrange("(k p) n -> p k n", p=128)
        for gi in range(8):
            w, bw = wp.next()
            S.dma("sp", w[:], wsrc[:, :, gi * 768:(gi + 1) * 768], writes=[bw])
            for j in range(6):
                jj = gi * 6 + j
                for k in range(8):
                    self.mm(ps[:, jj * 2:jj * 2 + 2], w[:, k, j * 128:(j + 1) * 128], cv[:, k, :],
                            k == 0, k == 7, [bw, bcv], [bps])
        mb = [self.mbuf]
        self.tt("dve", self.modv[:], ps[:, 0:96].rearrange("p (j c) -> p j c", c=2),
                bc(self.vec_t[:, l, V_BMOD:V_BMOD + 48], [128, 48, 2], 2), ALU.add, [bps] + cb, mb)
        for n, (voff, sc) in enumerate(((V_N1, 8), (V_N2, 32))):
            self.ts("dve", self.modA[:, n], self.modv[:, sc:sc + 8, :], 1.0, None, ALU.add, None, mb, mb)
            self.tt("dve", self.modA[:, n], self.modA[:, n],
                    bc(self.vec_t[:, l, voff:voff + 8], [128, 8, 2], 2), ALU.mult, mb + cb, mb)
        if "modv_d" in self.debug:
            S.dma("sp", self.modv_d, self.modv[:].rearrange("p j c -> p (j c)"), reads=mb)
        ph.close()

    def norm_mod(self, ph, x, bx, T, n, col, xn, bxn, sq, bsq, h, bh, rs, brs, psb):
        mb = [self.mbuf]
        cb = [self.cbuf]
        ps, bps = psb
        self.act(sq[:, :, :T], x[:, :, :T], AF.Square, [bx], [bsq])
        for c in range(8):
            self.mm(ps[:, :T], self.ones_b[:, 0, :], sq[:, c, :T], c == 0, c == 7, [bsq] + cb, [bps])
        self.rsqrt_(rs[:, :T], ps[:, :T], [bps] + cb, [brs], self.eps_t)
        self.tt("dve", xn[:, :, :T], x[:, :, :T], bc(rs[:, :T], [128, 8, T], 1), ALU.mult, [bx, brs], [bxn])
        shift = 0 if n == 0 else 24
        for c in range(8):
            if c % 2 == 0:
                self.act(h[:, c, :T], xn[:, c, :T], AF.Identity, [bxn] + mb, [bh],
                         bias=self.modv[:, shift + c, col:col + 1], scale=self.modA[:, n, c, col:col + 1])
            else:
                self.ts("dve", h[:, c, :T], xn[:, c, :T], self.modA[:, n, c, col:col + 1],
                        self.modv[:, shift + c, col:col + 1], ALU.mult, ALU.add, [bxn] + mb, [bh])

    def phase1(self, l, xsrc):
        S = self.S
        ph = Phase(self, f"p1_{l}")
        cb = [self.cbuf]
        xp = ph.pool(2, [128, 8, 512])
        xnp = ph.pool(1, [128, 8, 512])
        sqp = ph.pool(1, [128, 8, 512], BF16)
        hp = ph.pool(2, [128, 8, 512], BF16)
        rsp = ph.pool(1, [128, 512])
        wp = ph.pool(3, [128, 8, 512], BF16)
        stp = ph.pool(3, [128, 4, 512], BF16)
        tmpp = ph.pool(2, [128, 512])
        bgp = ph.pool(2, [128, 4, 32])
        banks = ph.psum(7)
        psn = ph.psum(1).next()
        dnp_t, bdnp = ph.sb([128, 32])
        S.dma("sp", dnp_t[:], self.dnp[l], writes=[bdnp])
        self.act(dnp_t[:, 0:16], dnp_t[:, 0:16], AF.Exp, [bdnp], [bdnp])
        self.ts("dve", dnp_t[:, 0:16], dnp_t[:, 0:16], -1.0, None, ALU.mult, None, [bdnp], [bdnp])

        wsrc = self.wb_in[l].rearrange("(k p) n -> p k n", p=128)
        xs = xsrc.rearrange("(c p) t -> p c t", p=128)
        groups = []
        for j in range(0, 8, 2):
            groups.append(("glu", [(j * 128, 256), (1024 + j * 128, 256)], self.hconv, j * 128))
        for i in range(6):
            groups.append(("copy", [(2048 + i * 512, 512)], self.dnpre, i * 512))
        for i in range(2):
            groups.append(("silu", [(OFF["dn_z"] + i * 512, 512)], self.zs, i * 512))
        groups.append(("ba", [(OFF["beta"], 32)], self.bg, 0))
        for i in range(2):
            groups.append(("copy", [(OFF["na_q"] + i * 512, 512)], self.naq, i * 512))
        for i in range(2):
            groups.append(("copy", [(OFF["na_k"] + i * 512, 512)], self.nak, i * 512))
        for i in range(2):
            groups.append(("tok", [(OFF["na_v"] + i * 512, 512)], self.nav, i * 512))
        for i in range(6):
            groups.append(("sigmoid", [(OFF["gate"] + i * 512, 512)], self.gates, i * 512))

        for (t0, T, isctx) in TILES:
            col = 1 if isctx else 0
            x, bx = xp.next()
            S.dma("sp", x[:, :, :T], xs[:, :, t0:t0 + T], writes=[bx])
            xn, bxn = xnp.next()
            sq, bsq = sqp.next()
            h, bh = hp.next()
            rs, brs = rsp.next()
            self.norm_mod(ph, x, bx, T, 0, col, xn, bxn, sq, bsq, h, bh, rs, brs, psn)
            nsub = T // 128
            for gi, (kind, ranges, dst, drow) in enumerate(groups):
                w, bw = wp.next()
                o = 0
                for (c0, wd) in ranges:
                    S.dma("sp", w[:, :, o:o + wd], wsrc[:, :, c0:c0 + wd], writes=[bw])
                    o += wd
                if kind in ("glu", "copy", "silu", "sigmoid"):
                    pss = []
                    for j in range(4):
                        ps, bps = banks.next()
                        for k in range(8):
                            self.mm(ps[:, :T], w[:, k, j * 128:(j + 1) * 128], h[:, k, :T], k == 0, k == 7,
                                    [bw, bh], [bps])
                        pss.append((ps, bps))
                    stg, bst = stp.next()
                    if kind == "glu":
                        for j in range(2):
                            tmp, btmp = tmpp.next()
                            self.act(tmp[:, :T], pss[2 + j][0][:, :T], AF.Sigmoid, [pss[2 + j][1]], [btmp])
                            self.tt("dve", stg[:, j, :T], pss[j][0][:, :T], tmp[:, :T], ALU.mult,
                                    [pss[j][1], btmp], [bst])
                        nout = 2
                    else:
                        for j in range(4):
                            if kind == "copy":
                                self.cp("dve", stg[:, j, :T], pss[j][0][:, :T], [pss[j][1]], [bst])
                            else:
                                self.act(stg[:, j, :T], pss[j][0][:, :T], AF.Silu if kind == "silu" else AF.Sigmoid,
                                         [pss[j][1]], [bst])
                        nout = 4
                    S.dma("pool", dst[drow:drow + nout * 128, t0:t0 + T].rearrange("(j p) t -> p j t", p=128),
                          stg[:, 0:nout, :T], reads=[bst])
                elif kind == "tok":
                    for s in range(nsub):
                        ps, bps = banks.next()
                        for k in range(8):
                            self.mm(ps[:, :], h[:, k, s * 128:(s + 1) * 128], w[:, k, :], k == 0, k == 7,
                                    [bw, bh], [bps])
                        stg, bst = stp.next()
                        self.cp("act" if s % 2 == 0 else "dve", stg[:, 0, :], ps[:, :], [bps], [bst])
                        S.dma("pool", dst[t0 + s * 128:t0 + (s + 1) * 128, drow:drow + 512], stg[:, 0, :], reads=[bst])
                else:
                    ps, bps = banks.next()
                    for s in range(nsub):
                        for k in range(8):
                            self.mm(ps[:, s * 32:(s + 1) * 32], h[:, k, s * 128:(s + 1) * 128], w[:, k, 0:32],
                                    k == 0, k == 7, [bw, bh], [bps])
                    bgt, bbg = bgp.next()
                    pv = ps[:, 0:nsub * 32].rearrange("p (s c) -> p s c", c=32)
                    self.act(bgt[:, :nsub, 0:16], pv[:, :, 0:16], AF.Sigmoid, [bps], [bbg])
                    self.tt("dve", bgt[:, :nsub, 16:32], pv[:, :, 16:32], bc(dnp_t[:, 16:32], [128, nsub, 16], 1),
                            ALU.add, [bps, bdnp], [bbg])
                    self.ts("dve", bgt[:, :nsub, 16:32], bgt[:, :nsub, 16:32], 60.0, None, ALU.min, None, [bbg], [bbg])
                    self.act(bgt[:, :nsub, 16:32], bgt[:, :nsub, 16:32], AF.Exp, [bbg], [bbg])
                    self.act(bgt[:, :nsub, 16:32], bgt[:, :nsub, 16:32], AF.Ln, [bbg] + cb, [bbg], bias=self.one_t)
                    self.tt("dve", bgt[:, :nsub, 16:32], bgt[:, :nsub, 16:32], bc(dnp_t[:, 0:16], [128, nsub, 16], 1),
                            ALU.mult, [bbg, bdnp], [bbg])
                    S.dma("pool", dst[t0:t0 + T, :].rearrange("(s p) c -> p s c", p=128), bgt[:, :nsub, :], reads=[bbg])
        ph.close()

    def load_w_sq(self, ph, name, l):
        w, bw = ph.sb([128, 8, D], BF16)
        self.S.dma("sp", w[:], self.wb_sq[name][l].rearrange("(k p) n -> p k n", p=128), writes=[bw])
        return w, bw

    def proj_store(self, y, by, T, w, bw, banks, stp, dst, t0):
        for half in range(2):
            stg, bst = stp.next()
            for j in range(4):
                jj = half * 4 + j
                ps, bps = banks.next()
                for k in range(8):
                    self.mm(ps[:, :T], w[:, k, jj * 128:(jj + 1) * 128], y[:, k, :T], k == 0, k == 7, [bw, by], [bps])
                self.cp("act" if j % 2 == 0 else "dve", stg[:, j, :T], ps[:, :T], [bps], [bst])
            self.S.dma("pool", dst[half * 512:(half + 1) * 512, t0:t0 + T].rearrange("(j p) t -> p j t", p=128),
                       stg[:, :, :T], reads=[bst])

    def phase2(self, l, last):
        S = self.S
        ph = Phase(self, f"p2_{l}")
        cb = [self.cbuf]
        w, bw = self.load_w_sq(ph, "w_conv_out", l)
        dw, bdw = ph.sb([128, 8, 31])
        S.dma("sp", dw[:], self.conv_dw[l], writes=[bdw])
        diag, bdiag = ph.sb([128, 248, 128], BF16)
        for c in range(8):
            for k in range(31):
                self.ts("dve" if (c * 31 + k) % 2 == 0 else "pool", diag[:, c * 31 + k, :], self.cm32[:, CM_ID, :],
                        dw[:, c, k:k + 1], None, ALU.mult, None, [bdw] + cb, [bdiag])
        hinp = ph.pool(2, [128, 8, 512 + 30], BF16)
        accp = ph.pool(2, [128, 8, 512])
        hbp = ph.pool(1, [128, 8, 512], BF16)
        sqp = ph.pool(1, [128, 8, 512], BF16)
        yp = ph.pool(2, [128, 8, 512], BF16)
        stp = ph.pool(2, [128, 4, 512], BF16)
        mp = ph.pool(1, [128, 512])
        m2p = ph.pool(1, [128, 512])
        rsp = ph.pool(1, [128, 512])
        banks = ph.psum(6)
        pstat = ph.psum(2)
        hsrc = self.hconv.rearrange("(c p) t -> p c t", p=128)
        for (t0, T, isctx) in TILES:
            if isctx and last:
                continue
            s0, s1 = (0, LC) if isctx else (LC, NT)
            lo, hi = max(t0 - 15, s0), min(t0 + T + 15, s1)
            hin, bhin = hinp.next()
            if lo > t0 - 15:
                self.ms("pool", hin[:, :, 0:15], 0.0, [bhin])
            if hi < t0 + T + 15:
                self.ms("pool", hin[:, :, T + 15:T + 30], 0.0, [bhin])
            S.dma("sp", hin[:, :, lo - (t0 - 15):hi - (t0 - 15)], hsrc[:, :, lo:hi], writes=[bhin])
            acc, bacc = accp.next()
            for c in range(8):
                ps, bps = banks.next()
                for k in range(31):
                    self.mm(ps[:, :T], diag[:, c * 31 + k, :], hin[:, c, k:k + T], k == 0, k == 30, [bhin, bdiag], [bps])
                if c % 2 == 0:
                    self.act(acc[:, c, :T], ps[:, :T], AF.Identity, [bps] + cb, [bacc], bias=self.vcol(l, V_CDB, c))
                else:
                    self.ts("dve", acc[:, c, :T], ps[:, :T], self.vcol(l, V_CDB, c), None, ALU.add, None, [bps] + cb, [bacc])
            hb, bhb = hbp.next()
            sq, bsq = sqp.next()
            self.cp("act", hb[:, :, :T], acc[:, :, :T], [bacc], [bhb])
            self.act(sq[:, :, :T], acc[:, :, :T], AF.Square, [bacc], [bsq])
            pm, bpm = pstat.next()
            pq, bpq = pstat.next()
            for c in range(8):
                self.mm(pm[:, :T], self.ones_b[:, 0, :], hb[:, c, :T], c == 0, c == 7, [bhb] + cb, [bpm])
            for c in range(8):
                self.mm(pq[:, :T], self.ones_b[:, 0, :], sq[:, c, :T], c == 0, c == 7, [bsq] + cb, [bpq])
            mean, bmean = mp.next()
            m2, bm2 = m2p.next()
            rs, brs = rsp.next()
            self.cp("act", mean[:, :T], pm[:, :T], [bpm], [bmean])
            self.tt("dve", m2[:, :T], mean[:, :T], mean[:, :T], ALU.mult, [bmean], [bm2])
            self.tt("dve", m2[:, :T], pq[:, :T], m2[:, :T], ALU.subtract, [bpq, bm2], [bm2])
            self.ts("dve", m2[:, :T], m2[:, :T], 0.0, None, ALU.max, None, [bm2], [bm2])
            self.rsqrt_(rs[:, :T], m2[:, :T], [bm2] + cb, [brs], self.eps_t)
            self.tt("dve", acc[:, :, :T], acc[:, :, :T], bc(mean[:, :T], [128, 8, T], 1), ALU.subtract,
                    [bacc, bmean], [bacc])
            self.tt("dve", acc[:, :, :T], acc[:, :, :T], bc(rs[:, :T], [128, 8, T], 1), ALU.mult, [bacc, brs], [bacc])
            y, by = yp.next()
            for c in range(8):
                self.act(y[:, c, :T], acc[:, c, :T], AF.Silu, [bacc] + cb, [by],
                         bias=self.vcol(l, V_LNB, c), scale=self.vcol(l, V_LNG, c))
            self.proj_store(y, by, T, w, bw, banks, stp, self.brA, t0)
        ph.close()

    def phase5(self, l, xsrc, last):
        S = self.S
        ph = Phase(self, f"p5_{l}")
        cb = [self.cbuf]
        mb = [self.mbuf]
        w, bw = self.load_w_sq(ph, "w_out", l)
        xp = ph.pool(1, [128, 8, 512])
        xnp = ph.pool(1, [128, 8, 512])
        sqp = ph.pool(1, [128, 8, 512], BF16)
        hp = ph.pool(1, [128, 8, 512], BF16)
        rsp = ph.pool(1, [128, 512])
        brp = ph.pool(3, [128, 8, 512], BF16)
        gp = ph.pool(1, [128, 24, 512], BF16)
        t1p = ph.pool(2, [128, 512])
        t2p = ph.pool(2, [128, 512])
        mgp = ph.pool(1, [128, 8, 512], BF16)
        wgp = ph.pool(2, [128, 8, 512], BF16)
        wdp = ph.pool(1, [128, 22, 512], BF16)
        fp = ph.pool(1, [128, 22, 512], BF16)
        tmpp = ph.pool(2, [128, 512])
        banks = ph.psum(7)
        psn = ph.psum(1).next()
        xs = xsrc.rearrange("(c p) t -> p c t", p=128)
        wgs = self.wb_gu[l].rearrange("(k p) n -> p k n", p=128)
        wds = self.wb_down[l].rearrange("(k p) n -> p k n", p=128)
        fm3 = lambda t: t.rearrange("(c p) t -> p c t", p=128)
        for (t0, T, isctx) in TILES:
            if isctx and last:
                continue
            col = 1 if isctx else 0
            x, bx = xp.next()
            S.dma("sp", x[:, :, :T], xs[:, :, t0:t0 + T], writes=[bx])
            g, bg_ = gp.next()
            S.dma("sp", g[:, :, :T], fm3(self.gates)[:, :, t0:t0 + T], writes=[bg_])
            brs_ = []
            for src in (self.brA, self.brB, self.brC):
                b_, bb_ = brp.next()
                S.dma("sp", b_[:, :, :T], fm3(src)[:, :, t0:t0 + T], writes=[bb_])
                brs_.append((b_, bb_))
            mg, bmg = mgp.next()
            for c in range(8):
                t1, bt1 = t1p.next()
                t2, bt2 = t2p.next()
                self.tt("dve", t1[:, :T], g[:, c, :T], brs_[0][0][:, c, :T], ALU.mult, [bg_, brs_[0][1]], [bt1])
                self.tt("pool", t2[:, :T], g[:, 8 + c, :T], brs_[1][0][:, c, :T], ALU.mult, [bg_, brs_[1][1]], [bt2])
                self.tt("dve", t1[:, :T], t1[:, :T], t2[:, :T], ALU.add, [bt1, bt2], [bt1])
                self.tt("pool", t2[:, :T], g[:, 16 + c, :T], brs_[2][0][:, c, :T], ALU.mult, [bg_, brs_[2][1], bt1], [bt2])
                self.tt("dve", mg[:, c, :T], t1[:, :T], t2[:, :T], ALU.add, [bt1, bt2], [bmg])
            for j in range(8):
                ps, bps = banks.next()
                for k in range(8):
                    self.mm(ps[:, :T], w[:, k, j * 128:(j + 1) * 128], mg[:, k, :T], k == 0, k == 7, [bw, bmg], [bps])
                self.stt(x[:, j, :T], ps[:, :T], self.modv[:, 16 + j, col:col + 1], x[:, j, :T], ALU.mult, ALU.add,
                         [bps, bx] + mb, [bx])
            xn, bxn = xnp.next()
            sq, bsq = sqp.next()
            h, bh = hp.next()
            rs, brs = rsp.next()
            self.norm_mod(ph, x, bx, T, 1, col, xn, bxn, sq, bsq, h, bh, rs, brs, psn)
            f, bf = fp.next()
            for gi in range(11):
                wg, bwg = wgp.next()
                S.dma("sp", wg[:, :, 0:256], wgs[:, :, gi * 256:(gi + 1) * 256], writes=[bwg])
                S.dma("sp", wg[:, :, 256:512], wgs[:, :, DFF + gi * 256:DFF + (gi + 1) * 256], writes=[bwg])
                pss = []
                for j in range(4):
                    ps, bps = banks.next()
                    for k in range(8):
                        self.mm(ps[:, :T], wg[:, k, j * 128:(j + 1) * 128], h[:, k, :T], k == 0, k == 7, [bwg, bh], [bps])
                    pss.append((ps, bps))
                for j in range(2):
                    tmp, btmp = tmpp.next()
                    self.act(tmp[:, :T], pss[j][0][:, :T], AF.Silu, [pss[j][1]], [btmp])
                    self.tt("dve", f[:, gi * 2 + j, :T], tmp[:, :T], pss[2 + j][0][:, :T], ALU.mult,
                            [btmp, pss[2 + j][1]], [bf])
            for half in range(2):
                wd, bwd = wdp.next()
                S.dma("sp", wd[:], wds[:, :, half * 512:(half + 1) * 512], writes=[bwd])
                for j in range(4):
                    jj = half * 4 + j
                    ps, bps = banks.next()
                    for k in range(22):
                        self.mm(ps[:, :T], wd[:, k, j * 128:(j + 1) * 128], f[:, k, :T], k == 0, k == 21, [bwd, bf], [bps])
                    self.stt(x[:, jj, :T], ps[:, :T], self.modv[:, 40 + jj, col:col + 1], x[:, jj, :T], ALU.mult, ALU.add,
                             [bps, bx] + mb, [bx])
            if not last:
                S.dma("pool", fm3(self.x1)[:, :, t0:t0 + T], x[:, :, :T], reads=[bx])
            else:
                sq2, bsq2 = sqp.next()
                self.act(sq2[:, :, :T], x[:, :, :T], AF.Square, [bx], [bsq2])
                ps, bps = psn
                for c in range(8):
                    self.mm(ps[:, :T], self.ones_b[:, 0, :], sq2[:, c, :T], c == 0, c == 7, [bsq2] + cb, [bps])
                rs2, brs2 = rsp.next()
                self.rsqrt_(rs2[:, :T], ps[:, :T], [bps] + cb, [brs2], self.eps_t)
                xn2, bxn2 = xnp.next()
                self.tt("dve", xn2[:, :, :T], x[:, :, :T], bc(rs2[:, :T], [128, 8, T], 1), ALU.mult, [bx, brs2], [bxn2])
                self.tt("pool", xn2[:, :, :T], xn2[:, :, :T], bc(self.fing_t[:], [128, 8, T], 2), ALU.mult,
                        [bxn2] + cb, [bxn2])
                S.dma("pool", fm3(self.yT)[:, :, t0 - LC:t0 - LC + T], xn2[:, :, :T], reads=[bxn2])
        ph.close()

    def phase4(self, l, last):
        S = self.S
        ph = Phase(self, f"p4_{l}")
        cb = [self.cbuf]
        wna, bwna = ph.sb([64, 16, D], BF16)
        S.dma("sp", wna[:], self.wb_sq["w_na_out"][l].rearrange("(h p) n -> p h n", p=64), writes=[bwna])
        E, bE = ph.sb([64, 16, 15, 64], BF16)
        tmpph = Phase(self, f"p4e_{l}")
        ep = tmpph.pool(2, [64, 2, 15, 64])
        for i in range(8):
            e32, be32 = ep.next()
            S.dma("sp", e32[:], self.rpbT[l][:, i * 2:(i + 1) * 2], writes=[be32])
            self.act(E[:, i * 2:(i + 1) * 2], e32[:], AF.Exp, [be32], [bE])
        tmpph.close()
        kTc, bkTc = ph.sb([128, 8, LC], BF16)
        S.dma("sp", kTc[:], self.nak.rearrange("(c p) t -> p c t", p=128)[:, :, 0:LC], writes=[bkTc])
        Vc0, bVc0 = ph.sb([128, 2, D], BF16)
        S.dma("sp", Vc0[:], self.nav[0:LC, :].rearrange("(j p) f -> p j f", p=128), writes=[bVc0])
        Vc, bVc = ph.sb([128, 2, 16, 65], BF16)
        self.ms("pool", Vc[:, :, :, 64:65], 1.0, [bVc])
        self.cp("pool", Vc[:, :, :, 0:64], Vc0[:].rearrange("p j (h f) -> p j h f", f=64), [bVc0], [bVc])
        kTp = ph.pool(1, [128, 8, 960], BF16)
        V0p = ph.pool(1, [64, 8, D], BF16)
        Vp = ph.pool(1, [64, 15, 16, 65], BF16)
        for (V_, bV_) in Vp.items:
            self.ms("pool", V_[:, :, :, 64:65], 1.0, [bV_])
        srowp = ph.pool(2, [128, 512])
        rbp = ph.pool(2, [64, 512])
        bcbp = ph.psum(1)
        qp = ph.pool(1, [128, 8, 512], BF16)
        oTp = ph.pool(1, [64, 16, 512], BF16)
        Pp = ph.pool(3, [128, 512], BF16)
        Pfp = ph.pool(3, [64, 512])
        stp = ph.pool(2, [128, 4, 512], BF16)
        sbanks = ph.psum(3)
        accs = ph.psum(3)
        pbank = ph.psum(1)
        naqs = self.naq.rearrange("(c p) t -> p c t", p=128)
        naks = self.nak.rearrange("(c p) t -> p c t", p=128)
        r0f = lambda qr: min(max(qr - 4, 0), ROWS - 8)
        for (t0, T, isctx) in TILES:
            if isctx and last:
                continue
            q, bq = qp.next()
            S.dma("sp", q[:, :, :T], naqs[:, :, t0:t0 + T], writes=[bq])
            rows = []
            if not isctx:
                R0 = (t0 - LC) // GRID_W
                kmin, kmax = r0f(R0), r0f(R0 + 7) + 7
                nr = kmax - kmin + 1
                kT, bkT = kTp.next()
                S.dma("sp", kT[:, :, 0:nr * 64], naks[:, :, LC + kmin * 64:LC + (kmax + 1) * 64], writes=[bkT])
                V, bV = Vp.next()
                for r_lo in range(0, nr, 8):
                    r_n = min(8, nr - r_lo)
                    V0, bV0 = V0p.next()
                    S.dma("sp", V0[:, 0:r_n, :],
                          self.nav[LC + (kmin + r_lo) * 64:LC + (kmin + r_lo + r_n) * 64, :].rearrange("(r p) f -> p r f", p=64),
                          writes=[bV0])
                    self.cp("pool", V[:, r_lo:r_lo + r_n, :, 0:64], V0[:, 0:r_n, :].rearrange("p r (h f) -> p r h f", f=64),
                            [bV0], [bV])
                for kr in range(kmin, kmax + 1):
                    qs = [qr for qr in range(R0, R0 + 8) if r0f(qr) <= kr <= r0f(qr) + 7]
                    if qs:
                        rows.append((kr, qs[0], qs[-1] + 1))
            oT, boT = oTp.next()
            steps = []
            for h in range(16):
                steps.append(("ctx", h, 0, None))
                for ri, row in enumerate(rows):
                    steps.append(("row", h, ri, row))
                steps.append(("ctx", h, 1, None))
            state = {}

            def emit_qk(st_):
                kind, h, a, row = st_
                c, po = h // 2, (h % 2) * 64
                ps, bps = sbanks.next()
                P, bP = Pp.next()
                if kind == "ctx":
                    self.mm(ps[:, :T], kTc[po:po + 64, c, a * 128:(a + 1) * 128], q[po:po + 64, c, :T], True, True,
                            [bkTc, bq], [bps])
                    self.act(P[:, :T], ps[:, :T], AF.Exp, [bps], [bP], scale=0.125)
                else:
                    kr, qlo, qhi = row
                    cs, nq = (qlo - R0) * 64, qhi - qlo
                    n = nq * 64
                    self.mm(ps[0:64, 0:n], kT[po:po + 64, c, (kr - kmin) * 64:(kr - kmin + 1) * 64],
                            q[po:po + 64, c, cs:cs + n], True, True, [bkT, bq], [bps])
                    Pf, bPf = Pfp.next()
                    self.act(Pf[:, 0:n], ps[0:64, 0:n], AF.Exp, [bps], [bPf], scale=0.125)
                    elo = qlo - kr + 7
                    self.tt("dve" if a % 2 == 0 else "pool", P[0:64, 0:n].rearrange("p (a b) -> p a b", b=64),
                            Pf[:, 0:n].rearrange("p (a b) -> p a b", b=64), E[:, h, elo:elo + nq, :], ALU.mult,
                            [bPf, bE], [bP])
                return (P, bP)

            def emit_pv(st_, pt):
                kind, h, a, row = st_
                P, bP = pt
                if kind == "ctx" and a == 0:
                    state["acc"] = accs.next()
                acc, bacc = state["acc"]
                if kind == "ctx":
                    first = (a == 0)
                    self.mm(acc[0:65, :T], Vc[:, a, h, :], P[:, :T], first, not first, [bVc, bP], [bacc])
                    if not first:
                        srow, bsrow = srowp.next()
                        self.cp("act", srow[64:65, :T], acc[64:65, :T], [bacc], [bsrow])
                        self.S.op("dve", lambda e, o=srow[64:65, :T]: e.reciprocal(out=o, in_=o), [bsrow], [bsrow])

                        def fin(h=h, acc=acc, bacc=bacc, srow=srow, bsrow=bsrow):
                            bcb, bbcb = bcbp.next()
                            self.mm(bcb[0:64, :T], self.ones_f[64:65, 0:64], srow[64:65, :T], True, True, [bsrow] + cb, [bbcb])
                            rb, brb = rbp.next()
                            self.cp("act", rb[:, :T], bcb[0:64, :T], [bbcb], [brb])
                            self.tt("dve", oT[:, h, :T], acc[0:64, :T], rb[:, :T], ALU.mult, [bacc, brb], [boT])
                        deferred.append([min(6, len(rows) + 2), fin])
                else:
                    kr, qlo, qhi = row
                    cs, n = (qlo - R0) * 64, (qhi - qlo) * 64
                    self.mm(acc[0:65, cs:cs + n], V[:, kr - kmin, h, :], P[0:64, 0:n], False, False,
                            [bV, bP], [bacc])

            LA = 2
            pts = {}
            deferred = []
            for i in range(len(steps) + LA):
                if i < len(steps):
                    pts[i] = emit_qk(steps[i])
                if i - LA >= 0:
                    emit_pv(steps[i - LA], pts.pop(i - LA))
                for dfr in list(deferred):
                    dfr[0] -= 1
                    if dfr[0] <= 0:
                        dfr[1]()
                        deferred.remove(dfr)
            for dfr in deferred:
                dfr[1]()
            for half in range(2):
                stg, bst = stp.next()
                for j in range(4):
                    jj = half * 4 + j
                    ps, bps = pbank.next()
                    for h in range(16):
                        self.mm(ps[:, :T], wna[:, h, jj * 128:(jj + 1) * 128], oT[:, h, :T], h == 0, h == 15,
                                [bwna, boT], [bps])
                    self.cp("act" if j % 2 == 0 else "dve", stg[:, j, :T], ps[:, :T], [bps], [bst])
                S.dma("pool", self.brC[half * 512:(half + 1) * 512, t0:t0 + T].rearrange("(j p) t -> p j t", p=128),
                      stg[:, :, :T], reads=[bst])
        ph.close()

    def phase3a(self, l):
        S = self.S
        ph = Phase(self, f"p3a_{l}")
        cb = [self.cbuf]
        dcw, bdcw = ph.sb([128, 24, 5])
        S.dma("sp", dcw[:], self.dn_conv[l], writes=[bdcw])
        diag, bdiag = ph.sb([128, 120, 128], BF16)
        for cc in range(24):
            for k in range(5):
                self.ts("dve" if (cc * 5 + k) % 2 == 0 else "pool", diag[:, cc * 5 + k, :], self.cm32[:, CM_ID, :],
                        dcw[:, cc, k:k + 1], None, ALU.mult, None, [bdcw] + cb, [bdiag])
        pinp = ph.pool(2, [128, 24, 512 + 4], BF16)
        accp = ph.pool(2, [128, 8, 512])
        sqp = ph.pool(1, [128, 8, 512], BF16)
        rsp = ph.pool(1, [128, 8, 512])
        sbp = ph.pool(1, [128, 8, 512], BF16)
        outp = ph.pool(2, [128, 8, 512], BF16)
        ropep = ph.pool(2, [128, 2, 512])
        t1p = ph.pool(2, [128, 512])
        t2p = ph.pool(2, [128, 512])
        stp = ph.pool(2, [128, 1024], BF16)
        pst = ph.psum(2, BF16, 1024)
        pso = ph.psum(3)
        psp = ph.psum(3)
        src = self.dnpre.rearrange("(c p) t -> p c t", p=128)
        ident = self.cmb[:, 0, :]
        perm = self.cmb[:, 1, :]

        def transposes(t, bt, T, dst, t0):
            for sidx in range(T // 128):
                ps, bps = pst.next()
                for hh in range(8):
                    self.tr(ps[:, hh * 128:(hh + 1) * 128], t[:, hh, sidx * 128:(sidx + 1) * 128], ident, [bt] + cb, [bps])
                stg, bst = stp.next()
                self.cp("act" if sidx % 2 == 0 else "dve", stg[:], ps[:], [bps], [bst])
                S.dma("pool", dst[t0 + sidx * 128:t0 + (sidx + 1) * 128, :], stg[:], reads=[bst])

        for (t0, T, isctx) in TILES:
            s0, s1 = (0, LC) if isctx else (LC, NT)
            lo, hi = max(t0 - 2, s0), min(t0 + T + 2, s1)
            pin, bpin = pinp.next()
            if lo > t0 - 2:
                self.ms("pool", pin[:, :, 0:2], 0.0, [bpin])
            if hi < t0 + T + 2:
                self.ms("pool", pin[:, :, T + 2:T + 4], 0.0, [bpin])
            S.dma("sp", pin[:, :, lo - (t0 - 2):hi - (t0 - 2)], src[:, :, lo:hi], writes=[bpin])
            if not isctx:
                rope, brope = ropep.next()
                S.dma("sp", rope[:, 0, :T], self.ropeC[:, t0 - LC:t0 - LC + T], writes=[brope])
                S.dma("sp", rope[:, 1, :T], self.ropeS[:, t0 - LC:t0 - LC + T], writes=[brope])
            for part in range(3):
                acc, bacc = accp.next()
                out, bout = outp.next()
                for c in range(8):
                    cc = part * 8 + c
                    ps, bps = pso.next()
                    for k in range(5):
                        self.mm(ps[:, :T], diag[:, cc * 5 + k, :], pin[:, cc, k:k + T], k == 0, k == 4, [bpin, bdiag], [bps])
                    if part == 2:
                        self.act(out[:, c, :T], ps[:, :T], AF.Silu, [bps], [bout])
                    else:
                        self.act(acc[:, c, :T], ps[:, :T], AF.Silu, [bps], [bacc])
                if part == 2:
                    transposes(out, bout, T, self.dvt, t0)
                    continue
                sq, bsq = sqp.next()
                self.act(sq[:, :, :T], acc[:, :, :T], AF.Square, [bacc], [bsq])
                rs, brs = rsp.next()
                for c in range(8):
                    ps, bps = pso.next()
                    self.mm(ps[:, :T], self.ones_b[:, 2 if part == 0 else 3, :], sq[:, c, :T], True, True, [bsq] + cb, [bps])
                    self.rsqrt_(rs[:, c, :T], ps[:, :T], [bps] + cb, [brs], self.eps128_t if part == 0 else self.eps_t)
                if isctx:
                    self.tt("dve", out[:, :, :T], acc[:, :, :T], rs[:, :, :T], ALU.mult, [bacc, brs], [bout])
                else:
                    sb_, bsb = sbp.next()
                    self.cp("pool", sb_[:, :, :T], acc[:, :, :T], [bacc], [bsb])
                    for c in range(8):
                        ps, bps = psp.next()
                        self.mm(ps[:, :T], perm, sb_[:, c, :T], True, True, [bsb] + cb, [bps])
                        t1, bt1 = t1p.next()
                        t2, bt2 = t2p.next()
                        self.tt("pool", t1[:, :T], acc[:, c, :T], rope[:, 0, :T], ALU.mult, [bacc, brope], [bt1])
                        self.tt("dve", t2[:, :T], ps[:, :T], rope[:, 1, :T], ALU.mult, [bps, brope], [bt2])
                        self.tt("dve", t1[:, :T], t1[:, :T], t2[:, :T], ALU.add, [bt1, bt2], [bt1])
                        self.tt("dve", out[:, c, :T], t1[:, :T], rs[:, c, :T], ALU.mult, [bt1, brs], [bout])
                dst = self.dq if part == 0 else self.dk
                S.dma("pool", dst.rearrange("(c p) t -> p c t", p=128)[:, :, t0:t0 + T], out[:, :, :T], reads=[bout])
                if part == 1:
                    transposes(out, bout, T, self.dkt, t0)
        ph.close()

    def phase3d(self, l, last):
        S = self.S
        ph = Phase(self, f"p3d_{l}")
        cb = [self.cbuf]
        w, bw = self.load_w_sq(ph, "w_dn_out", l)
        ofp = ph.pool(2, [128, 8, 512])
        obp = ph.pool(2, [128, 8, 512])
        zp = ph.pool(2, [128, 8, 512], BF16)
        sqp = ph.pool(1, [128, 8, 512], BF16)
        rsp = ph.pool(2, [128, 512])
        tp = ph.pool(2, [128, 512])
        yp = ph.pool(2, [128, 8, 512], BF16)
        stp = ph.pool(2, [128, 4, 512], BF16)
        banks = ph.psum(5)
        pso = ph.psum(3)
        fm3 = lambda t: t.rearrange("(c p) t -> p c t", p=128)
        for (t0, T, isctx) in TILES:
            if isctx and last:
                continue
            of, bof = ofp.next()
            ob, bob = obp.next()
            z, bz = zp.next()
            S.dma("sp", of[:, :, :T], fm3(self.ofw)[:, :, t0:t0 + T], writes=[bof])
            S.dma("sp", ob[:, :, :T], fm3(self.obw)[:, :, t0:t0 + T], writes=[bob])
            S.dma("sp", z[:, :, :T], fm3(self.zs)[:, :, t0:t0 + T], writes=[bz])
            self.tt("pool", of[:, :, :T], of[:, :, :T], ob[:, :, :T], ALU.add, [bof, bob], [bof])
            sq, bsq = sqp.next()
            self.act(sq[:, :, :T], of[:, :, :T], AF.Square, [bof], [bsq])
            y, by = yp.next()
            for c in range(8):
                ps, bps = pso.next()
                self.mm(ps[:, :T], self.ones_b[:, 1, :], sq[:, c, :T], True, True, [bsq] + cb, [bps])
                rs, brs = rsp.next()
                self.rsqrt_(rs[:, :T], ps[:, :T], [bps] + cb, [brs], self.eps_t)
                t, bt = tp.next()
                self.tt("dve", t[:, :T], of[:, c, :T], rs[:, :T], ALU.mult, [bof, brs], [bt])
                self.stt(y[:, c, :T], t[:, :T], self.vcol(l, V_DNG, 0), z[:, c, :T], ALU.mult, ALU.mult,
                         [bt, bz] + cb, [by])
            self.proj_store(y, by, T, w, bw, banks, stp, self.brB, t0)
        ph.close()

    def phase3c(self, l):
        S = self.S
        ph = Phase(self, f"p3c_{l}")
        cb = [self.cbuf]
        cm = self.cm32
        I64b = self.cmb[0:64, 0, 0:64]
        psSm, bpsSm = ph.psum(1).next()
        psT, bpsT = ph.psum(1, BF16, 1024).next()
        dks = self.dk.rearrange("(c p) t -> p c t", p=128)
        dqs = self.dq.rearrange("(c p) t -> p c t", p=128)

        def chain(d):
            U = cm[0:64, CM_UF if d == 0 else CM_UB, 0:64]
            negU = cm[0:64, CM_NUF if d == 0 else CM_NUB, 0:64]
            Mneg = cm[0:64, CM_MNF if d == 0 else CM_MNB, 0:64]
            MnegT = cm[0:64, CM_MNB if d == 0 else CM_MNF, 0:64]
            Ms = cm[0:64, CM_MSF if d == 0 else CM_MSB, 0:64]
            lastc = 63 if d == 0 else 0
            odst = self.ofw if d == 0 else self.obw
            rr = ph.psum(2)
            psX, bpsX = ph.psum(1).next()
            kTbp = ph.pool(2, [128, 8, 256], BF16)
            qTbp = ph.pool(2, [128, 8, 256], BF16)
            ktbp = ph.pool(2, [64, 4, D], BF16)
            vtbp = ph.pool(2, [64, 4, D], BF16)
            bgbp = ph.pool(2, [64, 4, 32])
            f3 = [64, 8, 64]
            Gm, bGm = ph.sb(f3)
            gB, bgB = ph.sb(f3)
            Dm, bDm = ph.sb(f3)
            DT, bDT = ph.sb(f3)
            Eg, bEg = ph.sb([128, 8, 64])
            sm, bsm = ph.sb([128, 16])
            ed, bed = ph.sb([64, 8])
            be, bbe = ph.sb([64, 8])
            L0, bL0 = ph.sb(f3)
            Lp = ph.pool(2, f3, BF16)
            Np = ph.pool(2, f3, BF16)
            ImN, bImN = ph.sb(f3, BF16)
            Xbp = ph.pool(2, f3, BF16)
            Aqk, bAqk = ph.sb(f3, BF16)
            M1T, bM1T = ph.sb(f3, BF16)
            M2T, bM2T = ph.sb(f3, BF16)
            u, bu = ph.sb([64, 8, 128])
            wT, bwT = ph.sb([128, 8, 64], BF16)
            kdec, bkdec = ph.sb([64, 8, 128], BF16)
            qdT, bqdT = ph.sb([128, 8, 64], BF16)
            vnew, bvnew = ph.sb([64, 8, 128], BF16)
            ost, bost = ph.sb([128, 8, 64])
            St, bSt = ph.sb([128, 8, 128])
            Sb, bSb = ph.sb([128, 8, 128], BF16)
            self.ms("pool", St[:], 0.0, [bSt])
            self.ms("pool", Sb[:], 0.0, [bSb])
            order = list(range(4)) + list(range(4, NT // 64)) if d == 0 else \
                list(range(3, -1, -1)) + list(range(NT // 64 - 1, 3, -1))
            curb = None
            for n in order:
                b, nn = n // 4, n % 4
                if b != curb:
                    curb = b
                    kTb, bkTb = kTbp.next()
                    qTb, bqTb = qTbp.next()
                    ktb, bktb = ktbp.next()
                    vtb, bvtb = vtbp.next()
                    bgb, bbgb = bgbp.next()
                    tk = slice(b * 256, (b + 1) * 256)
                    S.dma("sp", kTb[:], dks[:, :, tk], writes=[bkTb])
                    S.dma("sp", qTb[:], dqs[:, :, tk], writes=[bqTb])
                    S.dma("sp", ktb[:], self.dkt[tk, :].rearrange("(n p) f -> p n f", p=64), writes=[bktb])
                    S.dma("sp", vtb[:], self.dvt[tk, :].rearrange("(n p) f -> p n f", p=64), writes=[bvtb])
                    S.dma("sp", bgb[:], self.bg[tk, :].rearrange("(n p) c -> p n c", p=64), writes=[bbgb])
                kT_c = kTb[:, :, nn * 64:(nn + 1) * 64]
                qT_c = qTb[:, :, nn * 64:(nn + 1) * 64]
                kt_c = ktb[:, nn, :].rearrange("p (h f) -> p h f", f=128)
                vt_c = vtb[:, nn, :].rearrange("p (h f) -> p h f", f=128)
                beta = bgb[:, nn, 8 * d:8 * d + 8]
                g = bgb[:, nn, 16 + 8 * d:24 + 8 * d]
                flat = lambda t: t.rearrange("p h f -> p (h f)")
                self.tt("pool", Gm[:], bc(U, f3, 1), bc(g, f3, 2), ALU.mult, cb + [bbgb], [bGm])
                self.cp("pool", gB[:], bc(g, f3, 2), [bbgb], [bgB])
                psR, bpsR = rr.next()
                psE, bpsE = rr.next()
                self.mm(psR[0:64, :], self.ones_f[0:64, 0:64], flat(Gm[:]), True, False, cb + [bGm], [bpsR])
                self.mm(psR[0:64, :], negU, flat(gB[:]), False, True, cb + [bgB], [bpsR])
                self.mm(psE[:, :], self.ones_f[0:64, :], flat(Gm[:]), True, True, cb + [bGm], [bpsE])
                self.mm(psSm[:, 0:8], self.ones_f[0:64, :], g, True, True, cb + [bbgb], [bpsSm])
                self.mm(psSm[0:64, 8:16], U, g, True, True, cb + [bbgb], [bpsSm])
                psR3 = psR[0:64, :].rearrange("p (h f) -> p h f", f=64)
                self.tt("dve", Dm[:], bc(Mneg, f3, 1), psR3, ALU.subtract, cb + [bpsR], [bDm])
                self.act(Dm[:], Dm[:], AF.Exp, [bDm], [bDm])
                self.tt("dve", DT[:], psR3, bc(MnegT, f3, 1), ALU.add, cb + [bpsR], [bDT])
                self.act(DT[:], DT[:], AF.Exp, [bDT], [bDT])
                self.act(flat(Eg[:]), psE[:, :], AF.Exp, [bpsE], [bEg])
                self.act(sm[:], psSm[:, 0:16], AF.Exp, [bpsSm], [bsm])
                self.act(ed[:], psR3[:, :, lastc], AF.Exp, [bpsR], [bed])
                self.tt("dve", be[:], beta, sm[0:64, 8:16], ALU.mult, [bbgb, bsm], [bbe])
                yield
                psKK, bpsKK = rr.next()
                psQK, bpsQK = rr.next()
                for h in range(8):
                    self.mm(psKK[0:64, h * 64:(h + 1) * 64], kT_c[:, h, :], kT_c[:, h, :], True, True, [bkTb], [bpsKK])
                for h in range(8):
                    self.mm(psQK[0:64, h * 64:(h + 1) * 64], kT_c[:, h, :], qT_c[:, h, :], True, True, [bkTb, bqTb], [bpsQK])
                self.tt("dve", flat(L0[:]), psKK[0:64, :], flat(Dm[:]), ALU.mult, [bpsKK, bDm], [bL0])
                self.tt("dve", L0[:], L0[:], bc(Ms, f3, 1), ALU.mult, [bL0] + cb, [bL0])
                Lc, bLc = Lp.next()
                self.tt("dve", Lc[:], L0[:], bc(beta, f3, 2), ALU.mult, [bL0, bbgb], [bLc])
                self.tt("dve", flat(Aqk[:]), psQK[0:64, :], flat(DT[:]), ALU.mult, [bpsQK, bDT], [bAqk])
                for h in range(8):
                    self.tr(psT[0:64, h * 64:(h + 1) * 64], Lc[:, h, :], I64b, [bLc] + cb, [bpsT])
                Nc, bNc = Np.next()
                self.cp("act", flat(Nc[:]), psT[0:64, 0:512], [bpsT], [bNc])
                self.tt("pool", ImN[:], bc(I64b, f3, 1), Nc[:], ALU.subtract, cb + [bNc], [bImN])
                yield
                self.mm(psX[0:64, :], I64b, flat(ImN[:]), True, True, cb + [bImN], [bpsX])
                Xb, bXb = Xbp.next()
                self.cp("act", flat(Xb[:]), psX[0:64, :], [bpsX], [bXb])
                for k in range(1, 6):
                    psL, bpsL = rr.next()
                    for h in range(8):
                        self.mm(psL[0:64, h * 64:(h + 1) * 64], Nc[:, h, :], Lc[:, h, :], True, True, [bNc, bLc], [bpsL])
                    if k < 5:
                        psN, bpsN = rr.next()
                        for h in range(8):
                            self.mm(psN[0:64, h * 64:(h + 1) * 64], Lc[:, h, :], Nc[:, h, :], True, True, [bNc, bLc], [bpsN])
                    Lc, bLc = Lp.next()
                    self.cp("act", flat(Lc[:]), psL[0:64, :], [bpsL], [bLc])
                    if k < 5:
                        Nc, bNc = Np.next()
                        self.cp("dve", flat(Nc[:]), psN[0:64, :], [bpsN], [bNc])
                    for h in range(8):
                        self.mm(psX[0:64, h * 64:(h + 1) * 64], Lc[:, h, :], Xb[:, h, :], False, True, [bLc, bXb], [bpsX])
                    if k < 5:
                        Xb, bXb = Xbp.next()
                        self.cp("act", flat(Xb[:]), psX[0:64, :], [bpsX], [bXb])
                    yield
                psX3 = psX[0:64, :].rearrange("p (h f) -> p h f", f=64)
                self.tt("dve", M1T[:], psX3, bc(beta, f3, 2), ALU.mult, [bpsX, bbgb], [bM1T])
                self.tt("dve", M2T[:], psX3, bc(be[:], f3, 2), ALU.mult, [bpsX, bbe], [bM2T])
                pu = [rr.next(), rr.next()]
                for h in range(8):
                    p_, bp_ = pu[h // 4]
                    self.mm(p_[0:64, (h % 4) * 128:(h % 4 + 1) * 128], M1T[:, h, :], vt_c[:, h, :], True, True,
                            [bM1T, bvtb], [bp_])
                for i in range(2):
                    self.cp("act", flat(u[:, i * 4:(i + 1) * 4, :]), pu[i][0][0:64, :], [pu[i][1]], [bu])
                psW, bpsW = rr.next()
                for h in range(8):
                    self.mm(psW[:, h * 64:(h + 1) * 64], kt_c[:, h, :], M2T[:, h, :], True, True, [bktb, bM2T], [bpsW])
                self.cp("dve", flat(wT[:]), psW[:, :], [bpsW], [bwT])
                self.tt("pool", kdec[:], kt_c, bc(ed[:], [64, 8, 128], 2), ALU.mult, [bktb, bed], [bkdec])
                self.tt("pool", qdT[:], qT_c, Eg[:], ALU.mult, [bqTb, bEg], [bqdT])
                yield
                pv = [rr.next(), rr.next()]
                for h in range(8):
                    p_, bp_ = pv[h // 4]
                    self.mm(p_[0:64, (h % 4) * 128:(h % 4 + 1) * 128], wT[:, h, :], Sb[:, h, :], True, True,
                            [bwT, bSb], [bp_])
                for i in range(2):
                    self.tt("dve", flat(vnew[:, i * 4:(i + 1) * 4, :]), flat(u[:, i * 4:(i + 1) * 4, :]), pv[i][0][0:64, :],
                            ALU.subtract, [bu, pv[i][1]], [bvnew])
                psO, bpsO = rr.next()
                for h in range(8):
                    self.mm(psO[:, h * 64:(h + 1) * 64], Sb[:, h, :], qdT[:, h, :], True, False, [bSb, bqdT], [bpsO])
                    self.mm(psO[:, h * 64:(h + 1) * 64], vnew[:, h, :], Aqk[:, h, :], False, True, [bvnew, bAqk], [bpsO])
                self.cp("act", flat(ost[:]), psO[:, :], [bpsO], [bost])
                S.dma("pool", odst[:, n * 64:(n + 1) * 64].rearrange("(h p) t -> p h t", p=128), ost[:], reads=[bost])
                pS = [rr.next(), rr.next()]
                for h in range(8):
                    p_, bp_ = pS[h // 4]
                    self.mm(p_[:, (h % 4) * 128:(h % 4 + 1) * 128], kdec[:, h, :], vnew[:, h, :], True, True,
                            [bkdec, bvnew], [bp_])
                self.tt("dve", St[:], St[:], bc(sm[:, 0:8], [128, 8, 128], 2), ALU.mult, [bSt, bsm], [bSt])
                for i in range(2):
                    self.tt("dve", flat(St[:, i * 4:(i + 1) * 4, :]), flat(St[:, i * 4:(i + 1) * 4, :]), pS[i][0][:, :],
                            ALU.add, [bSt, pS[i][1]], [bSt])
                self.cp("act", Sb[:], St[:], [bSt], [bSb])
                yield

        gens = [chain(0), chain(1)]
        alive = [True, True]
        while any(alive):
            for i, gnr in enumerate(gens):
                if alive[i]:
                    try:
                        next(gnr)
                    except StopIteration:
                        alive[i] = False
        ph.close()


def host_consts():
    f32 = np.float32
    t = np.arange(SEQ)
    inv = (np.float32(10000.0) ** (-np.arange(0, 64, 2, dtype=f32) / f32(64))).astype(f32)
    ang_r = ((t // GRID_W).astype(f32)[:, None] * inv).astype(f32)
    ang_c = ((t % GRID_W).astype(f32)[:, None] * inv).astype(f32)
    C = np.zeros((128, SEQ), f32)
    Sg = np.zeros((128, SEQ), f32)
    for f in range(128):
        ang = ang_r if f < 64 else ang_c
        fi = f % 32
        C[f] = np.cos(ang[:, fi])
        Sg[f] = np.sin(ang[:, fi]) * (-1.0 if (f % 64) < 32 else 1.0)
    cm = np.zeros((128, NCM, 128), f32)
    cm[:, CM_ID, :] = np.eye(128, dtype=f32)
    for m in range(128):
        partner = m + 32 if (m % 64) < 32 else m - 32
        cm[partner, CM_PERM, m] = 1.0
    i = np.arange(64)
    le = (i[:, None] <= i[None, :]).astype(f32)
    ge = (i[:, None] >= i[None, :]).astype(f32)
    cm[:64, CM_UF, :64] = le
    cm[:64, CM_UB, :64] = ge
    cm[:64, CM_MNF, :64] = np.where(i[:, None] >= i[None, :], 0.0, NEG)
    cm[:64, CM_MNB, :64] = np.where(i[:, None] <= i[None, :], 0.0, NEG)
    cm[:64, CM_MSF, :64] = (i[:, None] > i[None, :]).astype(f32)
    cm[:64, CM_MSB, :64] = (i[:, None] < i[None, :]).astype(f32)
    cm[:64, CM_NUF, :64] = -le
    cm[:64, CM_NUB, :64] = -ge
    return dict(ropeC=C, ropeS=Sg, cmat=cm)


def fm(v, nchunk):
    sh = v.shape[:-1]
    return np.ascontiguousarray(np.moveaxis(v.reshape(sh + (nchunk, 128)), -1, -2))


def host_shared(inp):
    f32 = np.float32
    out = dict(host_consts())
    for n in ("w_mod", "w_in", "w_conv_out", "w_dn_out", "w_na_out", "w_out", "w_gu", "w_down"):
        out[n] = np.ascontiguousarray(inp[n], dtype=f32)
    vecs = np.zeros((DEPTH, 128, NVEC), f32)
    vecs[:, :, V_BMOD:V_BMOD + 48] = fm(inp["b_mod"], 48)
    vecs[:, :, V_N1:V_N1 + 8] = fm(inp["norm1_g"], 8)
    vecs[:, :, V_N2:V_N2 + 8] = fm(inp["norm2_g"], 8)
    vecs[:, :, V_CDB:V_CDB + 8] = fm(inp["conv_db"], 8)
    vecs[:, :, V_LNG:V_LNG + 8] = fm(inp["conv_ln_g"], 8)
    vecs[:, :, V_LNB:V_LNB + 8] = fm(inp["conv_ln_b"], 8)
    vecs[:, :, V_DNG] = inp["dn_norm_g"]
    out["vecs"] = vecs
    out["conv_dw"] = np.ascontiguousarray(np.transpose(inp["conv_dw"].reshape(DEPTH, 31, 8, 128), (0, 3, 2, 1)), dtype=f32)
    out["dn_conv"] = np.ascontiguousarray(np.transpose(inp["dn_conv"].reshape(DEPTH, 5, 24, 128), (0, 3, 2, 1)), dtype=f32)
    dnp = np.concatenate([inp["dn_a_log"].reshape(DEPTH, 16), inp["dn_dt_bias"].reshape(DEPTH, 16)], -1)
    out["dnp"] = np.ascontiguousarray(np.broadcast_to(dnp[:, None, :], (DEPTH, 128, 32)), dtype=f32)
    rpb = inp["na_rpb"]
    kc = np.arange(64)[:, None]
    qc = np.arange(64)[None, :]
    c0 = np.clip(qc - 8, 0, 48)
    valid = (kc >= c0) & (kc < c0 + 16)
    idx = np.clip(kc - qc + 15, 0, 30)
    tb = rpb[:, :, ::-1, :][:, :, :, idx]
    tb = np.where(valid[None, None, None], tb, f32(NEG))
    out["rpbT"] = np.ascontiguousarray(np.transpose(tb, (0, 3, 1, 2, 4)), dtype=f32)
    out["fin_g"] = fm(inp["final_norm_g"], 8).astype(f32)
    return out


def host_core(inp, b):
    f32 = np.float32
    xT0 = np.concatenate([inp["ctx"][b].T, inp["x"][b].T], axis=1)
    cvec = np.stack([fm(inp["c"][b], 8), fm(inp["c_ctx"], 8)], axis=-1)
    return dict(xT0=np.ascontiguousarray(xT0, dtype=f32), cvec=np.ascontiguousarray(cvec, dtype=f32))


_NC_CACHE = {}


def kernel(**inputs):
    inp = {k: np.asarray(v) for k, v in inputs.items()}
    if "nc" not in _NC_CACHE:
        _NC_CACHE["nc"] = K().build()
    nc = _NC_CACHE["nc"]
    shared = host_shared(inp)
    n = 8
    in_maps = []
    for core in range(n):
        m = dict(shared)
        m.update(host_core(inp, core % 4))
        in_maps.append(m)
    res = run_bass_kernel_spmd(nc, in_maps, core_ids=list(range(n)))
    out = np.stack([np.ascontiguousarray(res.results[b]["yT"].T) for b in range(4)], axis=0)
    return out.astype(np.float32)
```

```python
import contextlib
import numpy as np
import concourse.bass as bass
import concourse.mybir as mybir
from concourse.bass_utils import run_bass_kernel_spmd

F32 = mybir.dt.float32
BF16 = mybir.dt.bfloat16
ALU = mybir.AluOpType
AF = mybir.ActivationFunctionType

EPOCH = 12000
NDMA_SEM = 12

D = 1024
SEQ = 8192
LC = 256
NT = SEQ + LC
DEPTH = 2
IN_DIM = 12320
DFF = 2816
GRID_W = 64
ROWS = SEQ // GRID_W
EPS = 1e-6
OFF = dict(conv=0, dn_q=2048, dn_k=3072, dn_v=4096, dn_z=5120, beta=6144, alpha=6160,
           na_q=6176, na_k=7200, na_v=8224, gate=9248)
TILES = [(0, 256, True)] + [(LC + 512 * i, 512, False) for i in range(SEQ // 512)]
NEG = -30000.0


class Buf:
    __slots__ = ("name", "w", "r")

    def __init__(self, name=""):
        self.name = name
        self.w = None
        self.r = []


class Sched:
    ENG = ("pe", "act", "dve", "pool", "sp")

    def __init__(self, nc, stack):
        self.nc = nc
        self.stack = stack
        self.ops = {e: [] for e in self.ENG}
        self.count = {e: 0 for e in self.ENG}
        self.esems = {e: [] for e in self.ENG}
        self.waited = {e: {} for e in self.ENG}
        self.dsems = {}
        self.dcount = {}
        self.dnext = {}
        for q in ("sp", "act", "pool"):
            self.dsems[q] = [stack.enter_context(nc.semaphore(f"d_{q}_{i}")) for i in range(NDMA_SEM)]
            self.dcount[q] = [0] * NDMA_SEM
            self.dnext[q] = 0
        self.same_engine_sync = {"pe": False, "act": True, "dve": True, "pool": True, "sp": False}

    def _esem(self, eng, idx):
        ep = idx // EPOCH
        while len(self.esems[eng]) <= ep:
            self.esems[eng].append(self.stack.enter_context(
                self.nc.semaphore(f"e_{eng}_{len(self.esems[eng])}")))
        return self.esems[eng][ep], idx % EPOCH + 1

    def _need(self, eng, tok, waits, force=False):
        if tok is None:
            return
        if tok[0] == "e":
            _, src, idx = tok
            if src == eng and not self.same_engine_sync[eng] and not force:
                return
            key = ("e", src)
            if self.waited[eng].get(key, -1) >= idx:
                return
            self.waited[eng][key] = idx
            waits.append(self._esem(src, idx))
        else:
            _, q, si, val = tok
            key = ("d", q, si)
            if self.waited[eng].get(key, -1) >= val:
                return
            self.waited[eng][key] = val
            waits.append((self.dsems[q][si], val))

    def _deps(self, eng, reads, writes):
        waits = []
        for b in reads:
            self._need(eng, b.w, waits)
        for b in writes:
            self._need(eng, b.w, waits)
            for t in b.r:
                self._need(eng, t, waits)
        return waits

    def _commit(self, tok, reads, writes):
        for b in reads:
            b.r.append(tok)
        for b in writes:
            b.w = tok
            b.r = []

    def op(self, eng, fn, reads=(), writes=()):
        waits = self._deps(eng, reads, writes)
        idx = self.count[eng]
        self.count[eng] += 1
        sem, _ = self._esem(eng, idx)
        self.ops[eng].append((waits, fn, (sem, 1)))
        self._commit(("e", eng, idx), reads, writes)

    def dma(self, q, out, in_, reads=(), writes=()):
        waits = self._deps(q, reads, writes)
        si = self.dnext[q]
        self.dnext[q] = (si + 1) % NDMA_SEM
        prev = self.dcount[q][si]
        if prev > 0:
            self._need(q, ("d", q, si, prev), waits)
        val = prev + 16
        self.dcount[q][si] = val
        sem = self.dsems[q][si]
        self.ops[q].append((waits, lambda e, o=out, i=in_: e.dma_start(out=o, in_=i), (sem, 16)))
        tok = ("d", q, si, val)
        self._commit(tok, reads, writes)
        return tok

    def barrier(self):
        for e in self.ENG:
            waits = []
            for s in self.ENG:
                if self.count[s] > 0:
                    self._need(e, ("e", s, self.count[s] - 1), waits, force=True)
            for q in self.dsems:
                for si in range(NDMA_SEM):
                    if self.dcount[q][si] > 0:
                        self._need(e, ("d", q, si, self.dcount[q][si]), waits)
            self.ops[e].append((waits, None, None))

    def emit(self):
        nc = self.nc
        with nc.Block() as block:
            def run(engname):
                def body(e):
                    for waits, fn, inc in self.ops[engname]:
                        for s, v in waits:
                            e.wait_ge(s, v)
                        if fn is not None:
                            ins = fn(e)
                            ins.then_inc(inc[0], inc[1])
                return body
            block.tensor(run("pe"))
            block.scalar(run("act"))
            block.vector(run("dve"))
            block.gpsimd(run("pool"))
            block.sync(run("sp"))


class RR:
    def __init__(self, items):
        self.items = items
        self.i = 0

    def next(self):
        it = self.items[self.i]
        self.i = (self.i + 1) % len(self.items)
        return it


class Phase:
    def __init__(self, K, name):
        self.K = K
        self.name = name
        self.st = contextlib.ExitStack()
        self.n = 0

    def sb(self, shape, dt=F32):
        self.n += 1
        t = self.st.enter_context(self.K.nc.sbuf_tensor(f"{self.name}_{self.n}", list(shape), dt))
        return t, Buf(f"{self.name}_{self.n}")

    def pool(self, n, shape, dt=F32):
        return RR([self.sb(shape, dt) for _ in range(n)])

    def psum(self, n, dt=F32, cols=512):
        out = []
        for _ in range(n):
            self.n += 1
            t = self.st.enter_context(self.K.nc.psum_tensor(f"{self.name}_ps{self.n}", [128, cols], dt))
            out.append((t, Buf(f"{self.name}_ps{self.n}")))
        return RR(out)

    def close(self):
        self.K.S.barrier()
        self.st.close()


def bc(ap, shape, axis):
    return ap.unsqueeze(axis).to_broadcast(list(shape))


NVEC = 89
V_BMOD, V_N1, V_N2, V_CDB, V_LNG, V_LNB, V_DNG = 0, 48, 56, 64, 72, 80, 88
CM_ID, CM_PERM, CM_UF, CM_UB, CM_MNF, CM_MNB, CM_MSF, CM_MSB, CM_NUF, CM_NUB = range(10)
NCM = 10


class K:
    def __init__(self, debug=(), stop_after=None, layers=DEPTH):
        self.debug = set(debug)
        self.stop_after = stop_after
        self.layers = layers
        self.nc = bass.Bass("TRN2", target_bir_lowering=False)
        self.st = contextlib.ExitStack()
        self.S = Sched(self.nc, self.st)
        self.dram = {}

    def mm(self, out, lhsT, rhs, start, stop, reads, writes):
        self.S.op("pe", lambda e: e.matmul(out, lhsT=lhsT, rhs=rhs, start=start, stop=stop), reads, writes)

    def tr(self, out, in_, ident, reads, writes):
        self.S.op("pe", lambda e: e.transpose(out, in_, ident), reads, writes)

    def act(self, out, in_, func, reads, writes, bias=None, scale=None):
        kw = {}
        if bias is not None:
            kw["bias"] = bias
        if scale is not None:
            kw["scale"] = scale
        self.S.op("act", lambda e: e.activation(out=out, in_=in_, func=func, **kw), reads, writes)

    def tt(self, eng, out, in0, in1, op, reads, writes):
        self.S.op(eng, lambda e: e.tensor_tensor(out=out, in0=in0, in1=in1, op=op), reads, writes)

    def ts(self, eng, out, in0, s1, s2, op0, op1, reads, writes):
        if op1 is None:
            self.S.op(eng, lambda e: e.tensor_scalar(out=out, in0=in0, scalar1=s1, scalar2=None, op0=op0), reads, writes)
        else:
            self.S.op(eng, lambda e: e.tensor_scalar(out=out, in0=in0, scalar1=s1, scalar2=s2, op0=op0, op1=op1), reads, writes)

    def stt(self, out, in0, scalar, in1, op0, op1, reads, writes):
        self.S.op("dve", lambda e: e.scalar_tensor_tensor(out=out, in0=in0, scalar=scalar, in1=in1, op0=op0, op1=op1),
                  reads, writes)

    def cp(self, eng, out, in_, reads, writes):
        if eng == "act":
            self.act(out, in_, AF.Copy, reads, writes)
        else:
            self.S.op(eng, lambda e: e.tensor_copy(out=out, in_=in_), reads, writes)

    def ms(self, eng, out, val, writes):
        self.S.op(eng, lambda e: e.memset(out, val), (), writes)

    def rsqrt_(self, out, in_, reads, writes, eps_ap):
        self.act(out, in_, AF.Ln, reads, writes, bias=eps_ap)
        self.act(out, out, AF.Exp, writes, writes, scale=-0.5)

    def din(self, name, shape, dt=F32):
        t = self.nc.dram_tensor(name, list(shape), dt, kind="ExternalInput").ap()
        self.dram[name] = t
        return t

    def dscr(self, name, shape, dt):
        kind = "ExternalOutput" if name in self.debug else "Internal"
        t = self.nc.dram_tensor(name, list(shape), dt, kind=kind).ap()
        self.dram[name] = t
        return t

    def build(self):
        nc, S = self.nc, self.S
        g = self.din
        self.xT0 = g("xT0", [D, NT])
        self.cvec = g("cvec", [128, 8, 2])
        self.w_mod = g("w_mod", [DEPTH, D, 6 * D])
        self.w_in = g("w_in", [DEPTH, D, IN_DIM])
        self.w_sq = {n: g(n, [DEPTH, D, D]) for n in ("w_conv_out", "w_dn_out", "w_na_out", "w_out")}
        self.w_gu = g("w_gu", [DEPTH, D, 2 * DFF])
        self.w_down = g("w_down", [DEPTH, DFF, D])
        self.vecs = g("vecs", [DEPTH, 128, NVEC])
        self.conv_dw = g("conv_dw", [DEPTH, 128, 8, 31])
        self.dn_conv = g("dn_conv", [DEPTH, 128, 24, 5])
        self.dnp = g("dnp", [DEPTH, 128, 32])
        self.rpbT = g("rpbT", [DEPTH, 64, 16, 15, 64])
        self.fin_g = g("fin_g", [128, 8])
        self.ropeC = g("ropeC", [128, SEQ])
        self.ropeS = g("ropeS", [128, SEQ])
        self.cmat = g("cmat", [128, NCM, 128])
        self.yT = nc.dram_tensor("yT", [D, SEQ], F32, kind="ExternalOutput").ap()
        s = self.dscr
        self.wb_in = s("wb_in", [DEPTH, D, IN_DIM], BF16)
        self.wb_sq = {n: s("wb_" + n, [DEPTH, D, D], BF16) for n in self.w_sq}
        self.wb_gu = s("wb_gu", [DEPTH, D, 2 * DFF], BF16)
        self.wb_down = s("wb_down", [DEPTH, DFF, D], BF16)
        self.x1 = s("x1", [D, NT], F32)
        self.hconv = s("hconv", [D, NT], BF16)
        self.dnpre = s("dnpre", [3 * D, NT], BF16)
        self.zs = s("zs", [D, NT], BF16)
        self.bg = s("bg", [NT, 32], F32)
        self.naq = s("naq", [D, NT], BF16)
        self.nak = s("nak", [D, NT], BF16)
        self.nav = s("nav", [NT, D], BF16)
        self.gates = s("gates", [3 * D, NT], BF16)
        self.brA = s("brA", [D, NT], BF16)
        self.brB = s("brB", [D, NT], BF16)
        self.brC = s("brC", [D, NT], BF16)
        self.dq = s("dq", [D, NT], BF16)
        self.dk = s("dk", [D, NT], BF16)
        self.dkt = s("dkt", [NT, D], BF16)
        self.dvt = s("dvt", [NT, D], BF16)
        self.ofw = s("ofw", [D, NT], F32)
        self.obw = s("obw", [D, NT], F32)
        self.modv_d = s("modv_d", [128, 96], F32)

        self.consts()
        self.phase0()
        done = False
        for l in range(self.layers):
            last = (l == DEPTH - 1)
            xsrc = self.xT0 if l == 0 else self.x1
            steps = [("mod", lambda: self.phase_mod(l)),
                     ("p1", lambda: self.phase1(l, xsrc)),
                     ("p2", lambda: self.phase2(l, last)),
                     ("p3a", lambda: self.phase3a(l)),
                     ("p3c", lambda: self.phase3c(l)),
                     ("p3d", lambda: self.phase3d(l, last)),
                     ("p4", lambda: self.phase4(l, last)),
                     ("p5", lambda: self.phase5(l, xsrc, last))]
            for name, fn in steps:
                fn()
                if self.stop_after == (l, name):
                    done = True
                    break
            if done:
                break
        S.barrier()
        S.emit()
        self.st.close()
        return nc

    def consts(self):
        nc, S, st = self.nc, self.S, self.st
        sb = lambda name, shape, dt=F32: st.enter_context(nc.sbuf_tensor(name, list(shape), dt))
        self.cbuf = Buf("consts")
        cb = [self.cbuf]
        self.cm32 = sb("cm32", [128, NCM, 128])
        S.dma("sp", self.cm32[:], self.cmat, writes=cb)
        self.cmb = sb("cmb", [128, 2, 128], BF16)
        self.cp("dve", self.cmb[:], self.cm32[:, 0:2, :], cb, cb)
        self.ones_b = sb("ones_b", [128, 4, 128], BF16)
        for i, v in enumerate((1.0 / 1024, 1.0 / 128, 128.0, 1.0)):
            self.ms("pool", self.ones_b[:, i, :], v, cb)
        self.ones_f = sb("ones_f", [128, 128], F32)
        self.ms("pool", self.ones_f[:], 1.0, cb)
        self.cst = sb("cst", [128, 4], F32)
        for i, v in enumerate((EPS, 128 * EPS, 1.0, 0.0)):
            self.ms("pool", self.cst[:, i:i + 1], v, cb)
        self.eps_t = self.cst[:, 0:1]
        self.eps128_t = self.cst[:, 1:2]
        self.one_t = self.cst[:, 2:3]
        self.vec_t = sb("vec_t", [128, DEPTH, NVEC])
        S.dma("sp", self.vec_t[:], self.vecs.rearrange("l p n -> p l n"), writes=cb)
        self.fing_t = sb("fing_t", [128, 8])
        S.dma("sp", self.fing_t[:], self.fin_g, writes=cb)
        self.modv = sb("modv", [128, 48, 2])
        self.modA = sb("modA", [128, 2, 8, 2])
        self.mbuf = Buf("mod")
        S.barrier()

    def vcol(self, l, off, c):
        return self.vec_t[:, l, off + c:off + c + 1]

    def phase0(self):
        S = self.S
        for l in range(self.layers):
            pairs = [(self.w_in[l], self.wb_in[l], D), (self.w_gu[l], self.wb_gu[l], D),
                     (self.w_down[l], self.wb_down[l], DFF)]
            pairs += [(self.w_sq[n][l], self.wb_sq[n][l], D) for n in self.w_sq]
            for src, dst, rows in pairs:
                for r in range(0, rows, 128):
                    S.dma("pool", dst[r:r + 128, :], src[r:r + 128, :])
        S.barrier()

    def phase_mod(self, l):
        S = self.S
        ph = Phase(self, f"mod{l}")
        cb = [self.cbuf]
        cv, bcv = ph.sb([128, 8, 2])
        S.dma("sp", cv[:], self.cvec, writes=[bcv])
        self.act(cv[:], cv[:], AF.Silu, [bcv], [bcv])
        wp = ph.pool(2, [128, 8, 768])
        ps, bps = ph.psum(1).next()
        wsrc = self.w_mod[l].rearrange("(k p) n -> p k n", p=128)
        for gi in range(8):
            w, bw = wp.next()
            S.dma("sp", w[:], wsrc[:, :, gi * 768:(gi + 1) * 768], writes=[bw])
            for j in range(6):
                jj = gi * 6 + j
                for k in range(8):
                    self.mm(ps[:, jj * 2:jj * 2 + 2], w[:, k, j * 128:(j + 1) * 128], cv[:, k, :],
                            k == 0, k == 7, [bw, bcv], [bps])
        mb = [self.mbuf]
        self.tt("dve", self.modv[:], ps[:, 0:96].rearrange("p (j c) -> p j c", c=2),
                bc(self.vec_t[:, l, V_BMOD:V_BMOD + 48], [128, 48, 2], 2), ALU.add, [bps] + cb, mb)
        for n, (voff, sc) in enumerate(((V_N1, 8), (V_N2, 32))):
            self.ts("dve", self.modA[:, n], self.modv[:, sc:sc + 8, :], 1.0, None, ALU.add, None, mb, mb)
            self.tt("dve", self.modA[:, n], self.modA[:, n],
                    bc(self.vec_t[:, l, voff:voff + 8], [128, 8, 2], 2), ALU.mult, mb + cb, mb)
        if "modv_d" in self.debug:
            S.dma("sp", self.modv_d, self.modv[:].rearrange("p j c -> p (j c)"), reads=mb)
        ph.close()

    def norm_mod(self, ph, x, bx, T, n, col, xn, bxn, sq, bsq, h, bh, rs, brs, psb):
        mb = [self.mbuf]
        cb = [self.cbuf]
        ps, bps = psb
        self.act(sq[:, :, :T], x[:, :, :T], AF.Square, [bx], [bsq])
        for c in range(8):
            self.mm(ps[:, :T], self.ones_b[:, 0, :], sq[:, c, :T], c == 0, c == 7, [bsq] + cb, [bps])
        self.rsqrt_(rs[:, :T], ps[:, :T], [bps] + cb, [brs], self.eps_t)
        self.tt("dve", xn[:, :, :T], x[:, :, :T], bc(rs[:, :T], [128, 8, T], 1), ALU.mult, [bx, brs], [bxn])
        shift = 0 if n == 0 else 24
        for c in range(8):
            if c % 2 == 0:
                self.act(h[:, c, :T], xn[:, c, :T], AF.Identity, [bxn] + mb, [bh],
                         bias=self.modv[:, shift + c, col:col + 1], scale=self.modA[:, n, c, col:col + 1])
            else:
                self.ts("dve", h[:, c, :T], xn[:, c, :T], self.modA[:, n, c, col:col + 1],
                        self.modv[:, shift + c, col:col + 1], ALU.mult, ALU.add, [bxn] + mb, [bh])

    def phase1(self, l, xsrc):
        S = self.S
        ph = Phase(self, f"p1_{l}")
        cb = [self.cbuf]
        xp = ph.pool(2, [128, 8, 512])
        xnp = ph.pool(1, [128, 8, 512])
        sqp = ph.pool(1, [128, 8, 512], BF16)
        hp = ph.pool(2, [128, 8, 512], BF16)
        rsp = ph.pool(1, [128, 512])
        wp = ph.pool(3, [128, 8, 512], BF16)
        stp = ph.pool(3, [128, 4, 512], BF16)
        tmpp = ph.pool(2, [128, 512])
        bgp = ph.pool(2, [128, 4, 32])
        banks = ph.psum(7)
        psn = ph.psum(1).next()
        dnp_t, bdnp = ph.sb([128, 32])
        S.dma("sp", dnp_t[:], self.dnp[l], writes=[bdnp])
        self.act(dnp_t[:, 0:16], dnp_t[:, 0:16], AF.Exp, [bdnp], [bdnp])
        self.ts("dve", dnp_t[:, 0:16], dnp_t[:, 0:16], -1.0, None, ALU.mult, None, [bdnp], [bdnp])

        wsrc = self.wb_in[l].rearrange("(k p) n -> p k n", p=128)
        xs = xsrc.rearrange("(c p) t -> p c t", p=128)
        groups = []
        for j in range(0, 8, 2):
            groups.append(("glu", [(j * 128, 256), (1024 + j * 128, 256)], self.hconv, j * 128))
        for i in range(6):
            groups.append(("copy", [(2048 + i * 512, 512)], self.dnpre, i * 512))
        for i in range(2):
            groups.append(("silu", [(OFF["dn_z"] + i * 512, 512)], self.zs, i * 512))
        groups.append(("ba", [(OFF["beta"], 32)], self.bg, 0))
        for i in range(2):
            groups.append(("copy", [(OFF["na_q"] + i * 512, 512)], self.naq, i * 512))
        for i in range(2):
            groups.append(("copy", [(OFF["na_k"] + i * 512, 512)], self.nak, i * 512))
        for i in range(2):
            groups.append(("tok", [(OFF["na_v"] + i * 512, 512)], self.nav, i * 512))
        for i in range(6):
            groups.append(("sigmoid", [(OFF["gate"] + i * 512, 512)], self.gates, i * 512))

        for (t0, T, isctx) in TILES:
            col = 1 if isctx else 0
            x, bx = xp.next()
            S.dma("sp", x[:, :, :T], xs[:, :, t0:t0 + T], writes=[bx])
            xn, bxn = xnp.next()
            sq, bsq = sqp.next()
            h, bh = hp.next()
            rs, brs = rsp.next()
            self.norm_mod(ph, x, bx, T, 0, col, xn, bxn, sq, bsq, h, bh, rs, brs, psn)
            nsub = T // 128
            for gi, (kind, ranges, dst, drow) in enumerate(groups):
                w, bw = wp.next()
                o = 0
                for (c0, wd) in ranges:
                    S.dma("sp", w[:, :, o:o + wd], wsrc[:, :, c0:c0 + wd], writes=[bw])
                    o += wd
                if kind in ("glu", "copy", "silu", "sigmoid"):
                    pss = []
                    for j in range(4):
                        ps, bps = banks.next()
                        for k in range(8):
                            self.mm(ps[:, :T], w[:, k, j * 128:(j + 1) * 128], h[:, k, :T], k == 0, k == 7,
                                    [bw, bh], [bps])
                        pss.append((ps, bps))
                    stg, bst = stp.next()
                    if kind == "glu":
                        for j in range(2):
                            tmp, btmp = tmpp.next()
                            self.act(tmp[:, :T], pss[2 + j][0][:, :T], AF.Sigmoid, [pss[2 + j][1]], [btmp])
                            self.tt("dve", stg[:, j, :T], pss[j][0][:, :T], tmp[:, :T], ALU.mult,
                                    [pss[j][1], btmp], [bst])
                        nout = 2
                    else:
                        for j in range(4):
                            if kind == "copy":
                                self.cp("dve", stg[:, j, :T], pss[j][0][:, :T], [pss[j][1]], [bst])
                            else:
                                self.act(stg[:, j, :T], pss[j][0][:, :T], AF.Silu if kind == "silu" else AF.Sigmoid,
                                         [pss[j][1]], [bst])
                        nout = 4
                    S.dma("pool", dst[drow:drow + nout * 128, t0:t0 + T].rearrange("(j p) t -> p j t", p=128),
                          stg[:, 0:nout, :T], reads=[bst])
                elif kind == "tok":
                    for s in range(nsub):
                        ps, bps = banks.next()
                        for k in range(8):
                            self.mm(ps[:, :], h[:, k, s * 128:(s + 1) * 128], w[:, k, :], k == 0, k == 7,
                                    [bw, bh], [bps])
                        stg, bst = stp.next()
                        self.cp("act" if s % 2 == 0 else "dve", stg[:, 0, :], ps[:, :], [bps], [bst])
                        S.dma("pool", dst[t0 + s * 128:t0 + (s + 1) * 128, drow:drow + 512], stg[:, 0, :], reads=[bst])
                else:
                    ps, bps = banks.next()
                    for s in range(nsub):
                        for k in range(8):
                            self.mm(ps[:, s * 32:(s + 1) * 32], h[:, k, s * 128:(s + 1) * 128], w[:, k, 0:32],
                                    k == 0, k == 7, [bw, bh], [bps])
                    bgt, bbg = bgp.next()
                    pv = ps[:, 0:nsub * 32].rearrange("p (s c) -> p s c", c=32)
                    self.act(bgt[:, :nsub, 0:16], pv[:, :, 0:16], AF.Sigmoid, [bps], [bbg])
                    self.tt("dve", bgt[:, :nsub, 16:32], pv[:, :, 16:32], bc(dnp_t[:, 16:32], [128, nsub, 16], 1),
                            ALU.add, [bps, bdnp], [bbg])
                    self.ts("dve", bgt[:, :nsub, 16:32], bgt[:, :nsub, 16:32], 60.0, None, ALU.min, None, [bbg], [bbg])
                    self.act(bgt[:, :nsub, 16:32], bgt[:, :nsub, 16:32], AF.Exp, [bbg], [bbg])
                    self.act(bgt[:, :nsub, 16:32], bgt[:, :nsub, 16:32], AF.Ln, [bbg] + cb, [bbg], bias=self.one_t)
                    self.tt("dve", bgt[:, :nsub, 16:32], bgt[:, :nsub, 16:32], bc(dnp_t[:, 0:16], [128, nsub, 16], 1),
                            ALU.mult, [bbg, bdnp], [bbg])
                    S.dma("pool", dst[t0:t0 + T, :].rearrange("(s p) c -> p s c", p=128), bgt[:, :nsub, :], reads=[bbg])
        ph.close()

    def load_w_sq(self, ph, name, l):
        w, bw = ph.sb([128, 8, D], BF16)
        self.S.dma("sp", w[:], self.wb_sq[name][l].rearrange("(k p) n -> p k n", p=128), writes=[bw])
        return w, bw

    def proj_store(self, y, by, T, w, bw, banks, stp, dst, t0):
        for half in range(2):
            stg, bst = stp.next()
            for j in range(4):
                jj = half * 4 + j
                ps, bps = banks.next()
                for k in range(8):
                    self.mm(ps[:, :T], w[:, k, jj * 128:(jj + 1) * 128], y[:, k, :T], k == 0, k == 7, [bw, by], [bps])
                self.cp("act" if j % 2 == 0 else "dve", stg[:, j, :T], ps[:, :T], [bps], [bst])
            self.S.dma("pool", dst[half * 512:(half + 1) * 512, t0:t0 + T].rearrange("(j p) t -> p j t", p=128),
                       stg[:, :, :T], reads=[bst])

    def phase2(self, l, last):
        S = self.S
        ph = Phase(self, f"p2_{l}")
        cb = [self.cbuf]
        w, bw = self.load_w_sq(ph, "w_conv_out", l)
        dw, bdw = ph.sb([128, 8, 31])
        S.dma("sp", dw[:], self.conv_dw[l], writes=[bdw])
        diag, bdiag = ph.sb([128, 248, 128], BF16)
        for c in range(8):
            for k in range(31):
                self.ts("dve" if (c * 31 + k) % 2 == 0 else "pool", diag[:, c * 31 + k, :], self.cm32[:, CM_ID, :],
                        dw[:, c, k:k + 1], None, ALU.mult, None, [bdw] + cb, [bdiag])
        hinp = ph.pool(2, [128, 8, 512 + 30], BF16)
        accp = ph.pool(2, [128, 8, 512])
        hbp = ph.pool(1, [128, 8, 512], BF16)
        sqp = ph.pool(1, [128, 8, 512], BF16)
        yp = ph.pool(2, [128, 8, 512], BF16)
        stp = ph.pool(2, [128, 4, 512], BF16)
        mp = ph.pool(1, [128, 512])
        m2p = ph.pool(1, [128, 512])
        rsp = ph.pool(1, [128, 512])
        banks = ph.psum(6)
        pstat = ph.psum(2)
        hsrc = self.hconv.rearrange("(c p) t -> p c t", p=128)
        for (t0, T, isctx) in TILES:
            if isctx and last:
                continue
            s0, s1 = (0, LC) if isctx else (LC, NT)
            lo, hi = max(t0 - 15, s0), min(t0 + T + 15, s1)
            hin, bhin = hinp.next()
            if lo > t0 - 15:
                self.ms("pool", hin[:, :, 0:15], 0.0, [bhin])
            if hi < t0 + T + 15:
                self.ms("pool", hin[:, :, T + 15:T + 30], 0.0, [bhin])
            S.dma("sp", hin[:, :, lo - (t0 - 15):hi - (t0 - 15)], hsrc[:, :, lo:hi], writes=[bhin])
            acc, bacc = accp.next()
            for c in range(8):
                ps, bps = banks.next()
                for k in range(31):
                    self.mm(ps[:, :T], diag[:, c * 31 + k, :], hin[:, c, k:k + T], k == 0, k == 30, [bhin, bdiag], [bps])
                if c % 2 == 0:
                    self.act(acc[:, c, :T], ps[:, :T], AF.Identity, [bps] + cb, [bacc], bias=self.vcol(l, V_CDB, c))
                else:
                    self.ts("dve", acc[:, c, :T], ps[:, :T], self.vcol(l, V_CDB, c), None, ALU.add, None, [bps] + cb, [bacc])
            hb, bhb = hbp.next()
            sq, bsq = sqp.next()
            self.cp("act", hb[:, :, :T], acc[:, :, :T], [bacc], [bhb])
            self.act(sq[:, :, :T], acc[:, :, :T], AF.Square, [bacc], [bsq])
            pm, bpm = pstat.next()
            pq, bpq = pstat.next()
            for c in range(8):
                self.mm(pm[:, :T], self.ones_b[:, 0, :], hb[:, c, :T], c == 0, c == 7, [bhb] + cb, [bpm])
            for c in range(8):
                self.mm(pq[:, :T], self.ones_b[:, 0, :], sq[:, c, :T], c == 0, c == 7, [bsq] + cb, [bpq])
            mean, bmean = mp.next()
            m2, bm2 = m2p.next()
            rs, brs = rsp.next()
            self.cp("act", mean[:, :T], pm[:, :T], [bpm], [bmean])
            self.tt("dve", m2[:, :T], mean[:, :T], mean[:, :T], ALU.mult, [bmean], [bm2])
            self.tt("dve", m2[:, :T], pq[:, :T], m2[:, :T], ALU.subtract, [bpq, bm2], [bm2])
            self.ts("dve", m2[:, :T], m2[:, :T], 0.0, None, ALU.max, None, [bm2], [bm2])
            self.rsqrt_(rs[:, :T], m2[:, :T], [bm2] + cb, [brs], self.eps_t)
            self.tt("dve", acc[:, :, :T], acc[:, :, :T], bc(mean[:, :T], [128, 8, T], 1), ALU.subtract,
                    [bacc, bmean], [bacc])
            self.tt("dve", acc[:, :, :T], acc[:, :, :T], bc(rs[:, :T], [128, 8, T], 1), ALU.mult, [bacc, brs], [bacc])
            y, by = yp.next()
            for c in range(8):
                self.act(y[:, c, :T], acc[:, c, :T], AF.Silu, [bacc] + cb, [by],
                         bias=self.vcol(l, V_LNB, c), scale=self.vcol(l, V_LNG, c))
            self.proj_store(y, by, T, w, bw, banks, stp, self.brA, t0)
        ph.close()

    def phase5(self, l, xsrc, last):
        S = self.S
        ph = Phase(self, f"p5_{l}")
        cb = [self.cbuf]
        mb = [self.mbuf]
        w, bw = self.load_w_sq(ph, "w_out", l)
        xp = ph.pool(1, [128, 8, 512])
        xnp = ph.pool(1, [128, 8, 512])
        sqp = ph.pool(1, [128, 8, 512], BF16)
        hp = ph.pool(1, [128, 8, 512], BF16)
        rsp = ph.pool(1, [128, 512])
        brp = ph.pool(3, [128, 8, 512], BF16)
        gp = ph.pool(1, [128, 24, 512], BF16)
        t1p = ph.pool(2, [128, 512])
        t2p = ph.pool(2, [128, 512])
        mgp = ph.pool(1, [128, 8, 512], BF16)
        wgp = ph.pool(2, [128, 8, 512], BF16)
        wdp = ph.pool(1, [128, 22, 512], BF16)
        fp = ph.pool(1, [128, 22, 512], BF16)
        tmpp = ph.pool(2, [128, 512])
        banks = ph.psum(7)
        psn = ph.psum(1).next()
        xs = xsrc.rearrange("(c p) t -> p c t", p=128)
        wgs = self.wb_gu[l].rearrange("(k p) n -> p k n", p=128)
        wds = self.wb_down[l].rearrange("(k p) n -> p k n", p=128)
        fm3 = lambda t: t.rearrange("(c p) t -> p c t", p=128)
        for (t0, T, isctx) in TILES:
            if isctx and last:
                continue
            col = 1 if isctx else 0
            x, bx = xp.next()
            S.dma("sp", x[:, :, :T], xs[:, :, t0:t0 + T], writes=[bx])
            g, bg_ = gp.next()
            S.dma("sp", g[:, :, :T], fm3(self.gates)[:, :, t0:t0 + T], writes=[bg_])
            brs_ = []
            for src in (self.brA, self.brB, self.brC):
                b_, bb_ = brp.next()
                S.dma("sp", b_[:, :, :T], fm3(src)[:, :, t0:t0 + T], writes=[bb_])
                brs_.append((b_, bb_))
            mg, bmg = mgp.next()
            for c in range(8):
                t1, bt1 = t1p.next()
                t2, bt2 = t2p.next()
                self.tt("dve", t1[:, :T], g[:, c, :T], brs_[0][0][:, c, :T], ALU.mult, [bg_, brs_[0][1]], [bt1])
                self.tt("pool", t2[:, :T], g[:, 8 + c, :T], brs_[1][0][:, c, :T], ALU.mult, [bg_, brs_[1][1]], [bt2])
                self.tt("dve", t1[:, :T], t1[:, :T], t2[:, :T], ALU.add, [bt1, bt2], [bt1])
                self.tt("pool", t2[:, :T], g[:, 16 + c, :T], brs_[2][0][:, c, :T], ALU.mult, [bg_, brs_[2][1], bt1], [bt2])
                self.tt("dve", mg[:, c, :T], t1[:, :T], t2[:, :T], ALU.add, [bt1, bt2], [bmg])
            for j in range(8):
                ps, bps = banks.next()
                for k in range(8):
                    self.mm(ps[:, :T], w[:, k, j * 128:(j + 1) * 128], mg[:, k, :T], k == 0, k == 7, [bw, bmg], [bps])
                self.stt(x[:, j, :T], ps[:, :T], self.modv[:, 16 + j, col:col + 1], x[:, j, :T], ALU.mult, ALU.add,
                         [bps, bx] + mb, [bx])
            xn, bxn = xnp.next()
            sq, bsq = sqp.next()
            h, bh = hp.next()
            rs, brs = rsp.next()
            self.norm_mod(ph, x, bx, T, 1, col, xn, bxn, sq, bsq, h, bh, rs, brs, psn)
            f, bf = fp.next()
            for gi in range(11):
                wg, bwg = wgp.next()
                S.dma("sp", wg[:, :, 0:256], wgs[:, :, gi * 256:(gi + 1) * 256], writes=[bwg])
                S.dma("sp", wg[:, :, 256:512], wgs[:, :, DFF + gi * 256:DFF + (gi + 1) * 256], writes=[bwg])
                pss = []
                for j in range(4):
                    ps, bps = banks.next()
                    for k in range(8):
                        self.mm(ps[:, :T], wg[:, k, j * 128:(j + 1) * 128], h[:, k, :T], k == 0, k == 7, [bwg, bh], [bps])
                    pss.append((ps, bps))
                for j in range(2):
                    tmp, btmp = tmpp.next()
                    self.act(tmp[:, :T], pss[j][0][:, :T], AF.Silu, [pss[j][1]], [btmp])
                    self.tt("dve", f[:, gi * 2 + j, :T], tmp[:, :T], pss[2 + j][0][:, :T], ALU.mult,
                            [btmp, pss[2 + j][1]], [bf])
            for half in range(2):
                wd, bwd = wdp.next()
                S.dma("sp", wd[:], wds[:, :, half * 512:(half + 1) * 512], writes=[bwd])
                for j in range(4):
                    jj = half * 4 + j
                    ps, bps = banks.next()
                    for k in range(22):
                        self.mm(ps[:, :T], wd[:, k, j * 128:(j + 1) * 128], f[:, k, :T], k == 0, k == 21, [bwd, bf], [bps])
                    self.stt(x[:, jj, :T], ps[:, :T], self.modv[:, 40 + jj, col:col + 1], x[:, jj, :T], ALU.mult, ALU.add,
                             [bps, bx] + mb, [bx])
            if not last:
                S.dma("pool", fm3(self.x1)[:, :, t0:t0 + T], x[:, :, :T], reads=[bx])
            else:
                sq2, bsq2 = sqp.next()
                self.act(sq2[:, :, :T], x[:, :, :T], AF.Square, [bx], [bsq2])
                ps, bps = psn
                for c in range(8):
                    self.mm(ps[:, :T], self.ones_b[:, 0, :], sq2[:, c, :T], c == 0, c == 7, [bsq2] + cb, [bps])
                rs2, brs2 = rsp.next()
                self.rsqrt_(rs2[:, :T], ps[:, :T], [bps] + cb, [brs2], self.eps_t)
                xn2, bxn2 = xnp.next()
                self.tt("dve", xn2[:, :, :T], x[:, :, :T], bc(rs2[:, :T], [128, 8, T], 1), ALU.mult, [bx, brs2], [bxn2])
                self.tt("pool", xn2[:, :, :T], xn2[:, :, :T], bc(self.fing_t[:], [128, 8, T], 2), ALU.mult,
                        [bxn2] + cb, [bxn2])
                S.dma("pool", fm3(self.yT)[:, :, t0 - LC:t0 - LC + T], xn2[:, :, :T], reads=[bxn2])
        ph.close()

    def phase4(self, l, last):
        S = self.S
        ph = Phase(self, f"p4_{l}")
        cb = [self.cbuf]
        wna, bwna = ph.sb([64, 16, D], BF16)
        S.dma("sp", wna[:], self.wb_sq["w_na_out"][l].rearrange("(h p) n -> p h n", p=64), writes=[bwna])
        E, bE = ph.sb([64, 16, 15, 64], BF16)
        ep = ph.pool(2, [64, 2, 15, 64])
        for i in range(8):
            e32, be32 = ep.next()
            S.dma("sp", e32[:], self.rpbT[l][:, i * 2:(i + 1) * 2], writes=[be32])
            self.act(E[:, i * 2:(i + 1) * 2], e32[:], AF.Exp, [be32], [bE])
        kTc, bkTc = ph.sb([128, 8, LC], BF16)
        S.dma("sp", kTc[:], self.nak.rearrange("(c p) t -> p c t", p=128)[:, :, 0:LC], writes=[bkTc])
        Vc, bVc = ph.sb([128, 2, D], BF16)
        S.dma("sp", Vc[:], self.nav[0:LC, :].rearrange("(j p) f -> p j f", p=128), writes=[bVc])
        kTp = ph.pool(1, [128, 8, 960], BF16)
        Vp = ph.pool(1, [64, 15, D], BF16)
        qp = ph.pool(2, [128, 8, 512], BF16)
        oTp = ph.pool(1, [64, 16, 512], BF16)
        Pp = ph.pool(3, [128, 512], BF16)
        Pfp = ph.pool(3, [64, 512])
        recp = ph.pool(2, [64, 512])
        stp = ph.pool(2, [128, 4, 512], BF16)
        sbanks = ph.psum(3)
        accs = ph.psum(4)
        pbank = ph.psum(1)
        naqs = self.naq.rearrange("(c p) t -> p c t", p=128)
        naks = self.nak.rearrange("(c p) t -> p c t", p=128)
        r0f = lambda qr: min(max(qr - 4, 0), ROWS - 8)
        for (t0, T, isctx) in TILES:
            if isctx and last:
                continue
            q, bq = qp.next()
            S.dma("sp", q[:, :, :T], naqs[:, :, t0:t0 + T], writes=[bq])
            rows = []
            if not isctx:
                R0 = (t0 - LC) // GRID_W
                kmin, kmax = r0f(R0), r0f(R0 + 7) + 7
                nr = kmax - kmin + 1
                kT, bkT = kTp.next()
                S.dma("sp", kT[:, :, 0:nr * 64], naks[:, :, LC + kmin * 64:LC + (kmax + 1) * 64], writes=[bkT])
                V, bV = Vp.next()
                S.dma("sp", V[:, 0:nr, :], self.nav[LC + kmin * 64:LC + (kmax + 1) * 64, :].rearrange("(r p) f -> p r f", p=64),
                      writes=[bV])
                for kr in range(kmin, kmax + 1):
                    qs = [qr for qr in range(R0, R0 + 8) if r0f(qr) <= kr <= r0f(qr) + 7]
                    if qs:
                        rows.append((kr, qs[0], qs[-1] + 1))
            oT, boT = oTp.next()
            steps = []
            for h in range(16):
                steps.append(("ctx", h, 0, None))
                for ri, row in enumerate(rows):
                    steps.append(("row", h, ri, row))
                steps.append(("ctx", h, 1, None))
            state = {}

            def emit_qk(st_):
                kind, h, a, row = st_
                c, po = h // 2, (h % 2) * 64
                ps, bps = sbanks.next()
                P, bP = Pp.next()
                if kind == "ctx":
                    self.mm(ps[:, :T], kTc[po:po + 64, c, a * 128:(a + 1) * 128], q[po:po + 64, c, :T], True, True,
                            [bkTc, bq], [bps])
                    self.act(P[:, :T], ps[:, :T], AF.Exp, [bps], [bP], scale=0.125)
                else:
                    kr, qlo, qhi = row
                    cs, nq = (qlo - R0) * 64, qhi - qlo
                    n = nq * 64
                    self.mm(ps[0:64, 0:n], kT[po:po + 64, c, (kr - kmin) * 64:(kr - kmin + 1) * 64],
                            q[po:po + 64, c, cs:cs + n], True, True, [bkT, bq], [bps])
                    Pf, bPf = Pfp.next()
                    self.act(Pf[:, 0:n], ps[0:64, 0:n], AF.Exp, [bps], [bPf], scale=0.125)
                    elo = qlo - kr + 7
                    self.tt("dve" if a % 2 == 0 else "pool", P[0:64, 0:n].rearrange("p (a b) -> p a b", b=64),
                            Pf[:, 0:n].rearrange("p (a b) -> p a b", b=64), E[:, h, elo:elo + nq, :], ALU.mult,
                            [bPf, bE], [bP])
                return (P, bP)

            def emit_pv(st_, pt):
                kind, h, a, row = st_
                P, bP = pt
                if kind == "ctx" and a == 0:
                    state["acc"] = accs.next()
                    state["accS"] = accs.next()
                acc, bacc = state["acc"]
                accS, baccS = state["accS"]
                if kind == "ctx":
                    first = (a == 0)
                    self.mm(acc[0:64, :T], Vc[:, a, h * 64:(h + 1) * 64], P[:, :T], first, not first, [bVc, bP], [bacc])
                    self.mm(accS[0:64, :T], self.ones_b[:, 3, 0:64], P[:, :T], first, not first, [bP] + cb, [baccS])
                    if not first:
                        rec, brec = recp.next()
                        self.S.op("dve", lambda e, o=rec[:, :T], i=accS[0:64, :T]: e.reciprocal(out=o, in_=i), [baccS], [brec])
                        self.tt("dve", oT[:, h, :T], acc[0:64, :T], rec[:, :T], ALU.mult, [bacc, brec], [boT])
                else:
                    kr, qlo, qhi = row
                    cs, n = (qlo - R0) * 64, (qhi - qlo) * 64
                    self.mm(acc[0:64, cs:cs + n], V[:, kr - kmin, h * 64:(h + 1) * 64], P[0:64, 0:n], False, False,
                            [bV, bP], [bacc])
                    self.mm(accS[0:64, cs:cs + n], self.ones_b[0:64, 3, 0:64], P[0:64, 0:n], False, False,
                            [bP] + cb, [baccS])

            LA = 2
            pts = {}
            for i in range(len(steps) + LA):
                if i < len(steps):
                    pts[i] = emit_qk(steps[i])
                if i - LA >= 0:
                    emit_pv(steps[i - LA], pts.pop(i - LA))
            for half in range(2):
                stg, bst = stp.next()
                for j in range(4):
                    jj = half * 4 + j
                    ps, bps = pbank.next()
                    for h in range(16):
                        self.mm(ps[:, :T], wna[:, h, jj * 128:(jj + 1) * 128], oT[:, h, :T], h == 0, h == 15,
                                [bwna, boT], [bps])
                    self.cp("act" if j % 2 == 0 else "dve", stg[:, j, :T], ps[:, :T], [bps], [bst])
                S.dma("pool", self.brC[half * 512:(half + 1) * 512, t0:t0 + T].rearrange("(j p) t -> p j t", p=128),
                      stg[:, :, :T], reads=[bst])
        ph.close()

    def phase3a(self, l):
        S = self.S
        ph = Phase(self, f"p3a_{l}")
        cb = [self.cbuf]
        dcw, bdcw = ph.sb([128, 24, 5])
        S.dma("sp", dcw[:], self.dn_conv[l], writes=[bdcw])
        diag, bdiag = ph.sb([128, 120, 128], BF16)
        for cc in range(24):
            for k in range(5):
                self.ts("dve" if (cc * 5 + k) % 2 == 0 else "pool", diag[:, cc * 5 + k, :], self.cm32[:, CM_ID, :],
                        dcw[:, cc, k:k + 1], None, ALU.mult, None, [bdcw] + cb, [bdiag])
        pinp = ph.pool(2, [128, 24, 512 + 4], BF16)
        accp = ph.pool(2, [128, 8, 512])
        sqp = ph.pool(1, [128, 8, 512], BF16)
        rsp = ph.pool(1, [128, 8, 512])
        sbp = ph.pool(1, [128, 8, 512], BF16)
        outp = ph.pool(2, [128, 8, 512], BF16)
        ropep = ph.pool(2, [128, 2, 512])
        t1p = ph.pool(2, [128, 512])
        t2p = ph.pool(2, [128, 512])
        stp = ph.pool(2, [128, 1024], BF16)
        pst = ph.psum(2, BF16, 1024)
        pso = ph.psum(3)
        psp = ph.psum(3)
        src = self.dnpre.rearrange("(c p) t -> p c t", p=128)
        ident = self.cmb[:, 0, :]
        perm = self.cmb[:, 1, :]

        def transposes(t, bt, T, dst, t0):
            for sidx in range(T // 128):
                ps, bps = pst.next()
                for hh in range(8):
                    self.tr(ps[:, hh * 128:(hh + 1) * 128], t[:, hh, sidx * 128:(sidx + 1) * 128], ident, [bt] + cb, [bps])
                stg, bst = stp.next()
                self.cp("act" if sidx % 2 == 0 else "dve", stg[:], ps[:], [bps], [bst])
                S.dma("pool", dst[t0 + sidx * 128:t0 + (sidx + 1) * 128, :], stg[:], reads=[bst])

        for (t0, T, isctx) in TILES:
            s0, s1 = (0, LC) if isctx else (LC, NT)
            lo, hi = max(t0 - 2, s0), min(t0 + T + 2, s1)
            pin, bpin = pinp.next()
            if lo > t0 - 2:
                self.ms("pool", pin[:, :, 0:2], 0.0, [bpin])
            if hi < t0 + T + 2:
                self.ms("pool", pin[:, :, T + 2:T + 4], 0.0, [bpin])
            S.dma("sp", pin[:, :, lo - (t0 - 2):hi - (t0 - 2)], src[:, :, lo:hi], writes=[bpin])
            if not isctx:
                rope, brope = ropep.next()
                S.dma("sp", rope[:, 0, :T], self.ropeC[:, t0 - LC:t0 - LC + T], writes=[brope])
                S.dma("sp", rope[:, 1, :T], self.ropeS[:, t0 - LC:t0 - LC + T], writes=[brope])
            for part in range(3):
                acc, bacc = accp.next()
                out, bout = outp.next()
                for c in range(8):
                    cc = part * 8 + c
                    ps, bps = pso.next()
                    for k in range(5):
                        self.mm(ps[:, :T], diag[:, cc * 5 + k, :], pin[:, cc, k:k + T], k == 0, k == 4, [bpin, bdiag], [bps])
                    if part == 2:
                        self.act(out[:, c, :T], ps[:, :T], AF.Silu, [bps], [bout])
                    else:
                        self.act(acc[:, c, :T], ps[:, :T], AF.Silu, [bps], [bacc])
                if part == 2:
                    transposes(out, bout, T, self.dvt, t0)
                    continue
                sq, bsq = sqp.next()
                self.act(sq[:, :, :T], acc[:, :, :T], AF.Square, [bacc], [bsq])
                rs, brs = rsp.next()
                for c in range(8):
                    ps, bps = pso.next()
                    self.mm(ps[:, :T], self.ones_b[:, 2 if part == 0 else 3, :], sq[:, c, :T], True, True, [bsq] + cb, [bps])
                    self.rsqrt_(rs[:, c, :T], ps[:, :T], [bps] + cb, [brs], self.eps128_t if part == 0 else self.eps_t)
                if isctx:
                    self.tt("dve", out[:, :, :T], acc[:, :, :T], rs[:, :, :T], ALU.mult, [bacc, brs], [bout])
                else:
                    sb_, bsb = sbp.next()
                    self.cp("pool", sb_[:, :, :T], acc[:, :, :T], [bacc], [bsb])
                    for c in range(8):
                        ps, bps = psp.next()
                        self.mm(ps[:, :T], perm, sb_[:, c, :T], True, True, [bsb] + cb, [bps])
                        t1, bt1 = t1p.next()
                        t2, bt2 = t2p.next()
                        self.tt("pool", t1[:, :T], acc[:, c, :T], rope[:, 0, :T], ALU.mult, [bacc, brope], [bt1])
                        self.tt("dve", t2[:, :T], ps[:, :T], rope[:, 1, :T], ALU.mult, [bps, brope], [bt2])
                        self.tt("dve", t1[:, :T], t1[:, :T], t2[:, :T], ALU.add, [bt1, bt2], [bt1])
                        self.tt("dve", out[:, c, :T], t1[:, :T], rs[:, c, :T], ALU.mult, [bt1, brs], [bout])
                dst = self.dq if part == 0 else self.dk
                S.dma("pool", dst.rearrange("(c p) t -> p c t", p=128)[:, :, t0:t0 + T], out[:, :, :T], reads=[bout])
                if part == 1:
                    transposes(out, bout, T, self.dkt, t0)
        ph.close()

    def phase3d(self, l, last):
        S = self.S
        ph = Phase(self, f"p3d_{l}")
        cb = [self.cbuf]
        w, bw = self.load_w_sq(ph, "w_dn_out", l)
        ofp = ph.pool(2, [128, 8, 512])
        obp = ph.pool(2, [128, 8, 512])
        zp = ph.pool(2, [128, 8, 512], BF16)
        sqp = ph.pool(1, [128, 8, 512], BF16)
        rsp = ph.pool(2, [128, 512])
        tp = ph.pool(2, [128, 512])
        yp = ph.pool(2, [128, 8, 512], BF16)
        stp = ph.pool(2, [128, 4, 512], BF16)
        banks = ph.psum(5)
        pso = ph.psum(3)
        fm3 = lambda t: t.rearrange("(c p) t -> p c t", p=128)
        for (t0, T, isctx) in TILES:
            if isctx and last:
                continue
            of, bof = ofp.next()
            ob, bob = obp.next()
            z, bz = zp.next()
            S.dma("sp", of[:, :, :T], fm3(self.ofw)[:, :, t0:t0 + T], writes=[bof])
            S.dma("sp", ob[:, :, :T], fm3(self.obw)[:, :, t0:t0 + T], writes=[bob])
            S.dma("sp", z[:, :, :T], fm3(self.zs)[:, :, t0:t0 + T], writes=[bz])
            self.tt("pool", of[:, :, :T], of[:, :, :T], ob[:, :, :T], ALU.add, [bof, bob], [bof])
            sq, bsq = sqp.next()
            self.act(sq[:, :, :T], of[:, :, :T], AF.Square, [bof], [bsq])
            y, by = yp.next()
            for c in range(8):
                ps, bps = pso.next()
                self.mm(ps[:, :T], self.ones_b[:, 1, :], sq[:, c, :T], True, True, [bsq] + cb, [bps])
                rs, brs = rsp.next()
                self.rsqrt_(rs[:, :T], ps[:, :T], [bps] + cb, [brs], self.eps_t)
                t, bt = tp.next()
                self.tt("dve", t[:, :T], of[:, c, :T], rs[:, :T], ALU.mult, [bof, brs], [bt])
                self.stt(y[:, c, :T], t[:, :T], self.vcol(l, V_DNG, 0), z[:, c, :T], ALU.mult, ALU.mult,
                         [bt, bz] + cb, [by])
            self.proj_store(y, by, T, w, bw, banks, stp, self.brB, t0)
        ph.close()

    def phase3c(self, l):
        S = self.S
        ph = Phase(self, f"p3c_{l}")
        cb = [self.cbuf]
        cm = self.cm32
        I64b = self.cmb[0:64, 0, 0:64]
        psSm, bpsSm = ph.psum(1).next()
        psT, bpsT = ph.psum(1, BF16, 1024).next()
        dks = self.dk.rearrange("(c p) t -> p c t", p=128)
        dqs = self.dq.rearrange("(c p) t -> p c t", p=128)

        def chain(d):
            U = cm[0:64, CM_UF if d == 0 else CM_UB, 0:64]
            negU = cm[0:64, CM_NUF if d == 0 else CM_NUB, 0:64]
            Mneg = cm[0:64, CM_MNF if d == 0 else CM_MNB, 0:64]
            MnegT = cm[0:64, CM_MNB if d == 0 else CM_MNF, 0:64]
            Ms = cm[0:64, CM_MSF if d == 0 else CM_MSB, 0:64]
            lastc = 63 if d == 0 else 0
            odst = self.ofw if d == 0 else self.obw
            rr = ph.psum(2)
            psX, bpsX = ph.psum(1).next()
            kTbp = ph.pool(2, [128, 8, 256], BF16)
            qTbp = ph.pool(2, [128, 8, 256], BF16)
            ktbp = ph.pool(2, [64, 4, D], BF16)
            vtbp = ph.pool(2, [64, 4, D], BF16)
            bgbp = ph.pool(2, [64, 4, 32])
            f3 = [64, 8, 64]
            Gm, bGm = ph.sb(f3)
            gB, bgB = ph.sb(f3)
            Dm, bDm = ph.sb(f3)
            DT, bDT = ph.sb(f3)
            Eg, bEg = ph.sb([128, 8, 64])
            sm, bsm = ph.sb([128, 16])
            ed, bed = ph.sb([64, 8])
            be, bbe = ph.sb([64, 8])
            L0, bL0 = ph.sb(f3)
            Lp = ph.pool(2, f3, BF16)
            Np = ph.pool(2, f3, BF16)
            ImN, bImN = ph.sb(f3, BF16)
            Xbp = ph.pool(2, f3, BF16)
            Aqk, bAqk = ph.sb(f3, BF16)
            M1T, bM1T = ph.sb(f3, BF16)
            M2T, bM2T = ph.sb(f3, BF16)
            u, bu = ph.sb([64, 8, 128])
            wT, bwT = ph.sb([128, 8, 64], BF16)
            kdec, bkdec = ph.sb([64, 8, 128], BF16)
            qdT, bqdT = ph.sb([128, 8, 64], BF16)
            vnew, bvnew = ph.sb([64, 8, 128], BF16)
            ost, bost = ph.sb([128, 8, 64])
            St, bSt = ph.sb([128, 8, 128])
            Sb, bSb = ph.sb([128, 8, 128], BF16)
            self.ms("pool", St[:], 0.0, [bSt])
            self.ms("pool", Sb[:], 0.0, [bSb])
            order = list(range(4)) + list(range(4, NT // 64)) if d == 0 else \
                list(range(3, -1, -1)) + list(range(NT // 64 - 1, 3, -1))
            curb = None
            for n in order:
                b, nn = n // 4, n % 4
                if b != curb:
                    curb = b
                    kTb, bkTb = kTbp.next()
                    qTb, bqTb = qTbp.next()
                    ktb, bktb = ktbp.next()
                    vtb, bvtb = vtbp.next()
                    bgb, bbgb = bgbp.next()
                    tk = slice(b * 256, (b + 1) * 256)
                    S.dma("sp", kTb[:], dks[:, :, tk], writes=[bkTb])
                    S.dma("sp", qTb[:], dqs[:, :, tk], writes=[bqTb])
                    S.dma("sp", ktb[:], self.dkt[tk, :].rearrange("(n p) f -> p n f", p=64), writes=[bktb])
                    S.dma("sp", vtb[:], self.dvt[tk, :].rearrange("(n p) f -> p n f", p=64), writes=[bvtb])
                    S.dma("sp", bgb[:], self.bg[tk, :].rearrange("(n p) c -> p n c", p=64), writes=[bbgb])
                kT_c = kTb[:, :, nn * 64:(nn + 1) * 64]
                qT_c = qTb[:, :, nn * 64:(nn + 1) * 64]
                kt_c = ktb[:, nn, :].rearrange("p (h f) -> p h f", f=128)
                vt_c = vtb[:, nn, :].rearrange("p (h f) -> p h f", f=128)
                beta = bgb[:, nn, 8 * d:8 * d + 8]
                g = bgb[:, nn, 16 + 8 * d:24 + 8 * d]
                flat = lambda t: t.rearrange("p h f -> p (h f)")
                self.tt("pool", Gm[:], bc(U, f3, 1), bc(g, f3, 2), ALU.mult, cb + [bbgb], [bGm])
                self.cp("pool", gB[:], bc(g, f3, 2), [bbgb], [bgB])
                psR, bpsR = rr.next()
                psE, bpsE = rr.next()
                self.mm(psR[0:64, :], self.ones_f[0:64, 0:64], flat(Gm[:]), True, False, cb + [bGm], [bpsR])
                self.mm(psR[0:64, :], negU, flat(gB[:]), False, True, cb + [bgB], [bpsR])
                self.mm(psE[:, :], self.ones_f[0:64, :], flat(Gm[:]), True, True, cb + [bGm], [bpsE])
                self.mm(psSm[:, 0:8], self.ones_f[0:64, :], g, True, True, cb + [bbgb], [bpsSm])
                self.mm(psSm[0:64, 8:16], U, g, True, True, cb + [bbgb], [bpsSm])
                psR3 = psR[0:64, :].rearrange("p (h f) -> p h f", f=64)
                self.tt("dve", Dm[:], bc(Mneg, f3, 1), psR3, ALU.subtract, cb + [bpsR], [bDm])
                self.act(Dm[:], Dm[:], AF.Exp, [bDm], [bDm])
                self.tt("dve", DT[:], psR3, bc(MnegT, f3, 1), ALU.add, cb + [bpsR], [bDT])
                self.act(DT[:], DT[:], AF.Exp, [bDT], [bDT])
                self.act(flat(Eg[:]), psE[:, :], AF.Exp, [bpsE], [bEg])
                self.act(sm[:], psSm[:, 0:16], AF.Exp, [bpsSm], [bsm])
                self.act(ed[:], psR3[:, :, lastc], AF.Exp, [bpsR], [bed])
                self.tt("dve", be[:], beta, sm[0:64, 8:16], ALU.mult, [bbgb, bsm], [bbe])
                yield
                psKK, bpsKK = rr.next()
                psQK, bpsQK = rr.next()
                for h in range(8):
                    self.mm(psKK[0:64, h * 64:(h + 1) * 64], kT_c[:, h, :], kT_c[:, h, :], True, True, [bkTb], [bpsKK])
                for h in range(8):
                    self.mm(psQK[0:64, h * 64:(h + 1) * 64], kT_c[:, h, :], qT_c[:, h, :], True, True, [bkTb, bqTb], [bpsQK])
                self.tt("dve", flat(L0[:]), psKK[0:64, :], flat(Dm[:]), ALU.mult, [bpsKK, bDm], [bL0])
                self.tt("dve", L0[:], L0[:], bc(Ms, f3, 1), ALU.mult, [bL0] + cb, [bL0])
                Lc, bLc = Lp.next()
                self.tt("dve", Lc[:], L0[:], bc(beta, f3, 2), ALU.mult, [bL0, bbgb], [bLc])
                self.tt("dve", flat(Aqk[:]), psQK[0:64, :], flat(DT[:]), ALU.mult, [bpsQK, bDT], [bAqk])
                for h in range(8):
                    self.tr(psT[0:64, h * 64:(h + 1) * 64], Lc[:, h, :], I64b, [bLc] + cb, [bpsT])
                Nc, bNc = Np.next()
                self.cp("act", flat(Nc[:]), psT[0:64, 0:512], [bpsT], [bNc])
                self.tt("pool", ImN[:], bc(I64b, f3, 1), Nc[:], ALU.subtract, cb + [bNc], [bImN])
                yield
                self.mm(psX[0:64, :], I64b, flat(ImN[:]), True, True, cb + [bImN], [bpsX])
                Xb, bXb = Xbp.next()
                self.cp("act", flat(Xb[:]), psX[0:64, :], [bpsX], [bXb])
                for k in range(1, 6):
                    psL, bpsL = rr.next()
                    for h in range(8):
                        self.mm(psL[0:64, h * 64:(h + 1) * 64], Nc[:, h, :], Lc[:, h, :], True, True, [bNc, bLc], [bpsL])
                    if k < 5:
                        psN, bpsN = rr.next()
                        for h in range(8):
                            self.mm(psN[0:64, h * 64:(h + 1) * 64], Lc[:, h, :], Nc[:, h, :], True, True, [bNc, bLc], [bpsN])
                    Lc, bLc = Lp.next()
                    self.cp("act", flat(Lc[:]), psL[0:64, :], [bpsL], [bLc])
                    if k < 5:
                        Nc, bNc = Np.next()
                        self.cp("dve", flat(Nc[:]), psN[0:64, :], [bpsN], [bNc])
                    for h in range(8):
                        self.mm(psX[0:64, h * 64:(h + 1) * 64], Lc[:, h, :], Xb[:, h, :], False, True, [bLc, bXb], [bpsX])
                    if k < 5:
                        Xb, bXb = Xbp.next()
                        self.cp("act", flat(Xb[:]), psX[0:64, :], [bpsX], [bXb])
                    yield
                psX3 = psX[0:64, :].rearrange("p (h f) -> p h f", f=64)
                self.tt("dve", M1T[:], psX3, bc(beta, f3, 2), ALU.mult, [bpsX, bbgb], [bM1T])
                self.tt("dve", M2T[:], psX3, bc(be[:], f3, 2), ALU.mult, [bpsX, bbe], [bM2T])
                pu = [rr.next(), rr.next()]
                for h in range(8):
                    p_, bp_ = pu[h // 4]
                    self.mm(p_[0:64, (h % 4) * 128:(h % 4 + 1) * 128], M1T[:, h, :], vt_c[:, h, :], True, True,
                            [bM1T, bvtb], [bp_])
                for i in range(2):
                    self.cp("act", flat(u[:, i * 4:(i + 1) * 4, :]), pu[i][0][0:64, :], [pu[i][1]], [bu])
                psW, bpsW = rr.next()
                for h in range(8):
                    self.mm(psW[:, h * 64:(h + 1) * 64], kt_c[:, h, :], M2T[:, h, :], True, True, [bktb, bM2T], [bpsW])
                self.cp("dve", flat(wT[:]), psW[:, :], [bpsW], [bwT])
                self.tt("pool", kdec[:], kt_c, bc(ed[:], [64, 8, 128], 2), ALU.mult, [bktb, bed], [bkdec])
                self.tt("pool", qdT[:], qT_c, Eg[:], ALU.mult, [bqTb, bEg], [bqdT])
                yield
                pv = [rr.next(), rr.next()]
                for h in range(8):
                    p_, bp_ = pv[h // 4]
                    self.mm(p_[0:64, (h % 4) * 128:(h % 4 + 1) * 128], wT[:, h, :], Sb[:, h, :], True, True,
                            [bwT, bSb], [bp_])
                for i in range(2):
                    self.tt("dve", flat(vnew[:, i * 4:(i + 1) * 4, :]), flat(u[:, i * 4:(i + 1) * 4, :]), pv[i][0][0:64, :],
                            ALU.subtract, [bu, pv[i][1]], [bvnew])
                psO, bpsO = rr.next()
                for h in range(8):
                    self.mm(psO[:, h * 64:(h + 1) * 64], Sb[:, h, :], qdT[:, h, :], True, False, [bSb, bqdT], [bpsO])
                    self.mm(psO[:, h * 64:(h + 1) * 64], vnew[:, h, :], Aqk[:, h, :], False, True, [bvnew, bAqk], [bpsO])
                self.cp("act", flat(ost[:]), psO[:, :], [bpsO], [bost])
                S.dma("pool", odst[:, n * 64:(n + 1) * 64].rearrange("(h p) t -> p h t", p=128), ost[:], reads=[bost])
                pS = [rr.next(), rr.next()]
                for h in range(8):
                    p_, bp_ = pS[h // 4]
                    self.mm(p_[:, (h % 4) * 128:(h % 4 + 1) * 128], kdec[:, h, :], vnew[:, h, :], True, True,
                            [bkdec, bvnew], [bp_])
                self.tt("dve", St[:], St[:], bc(sm[:, 0:8], [128, 8, 128], 2), ALU.mult, [bSt, bsm], [bSt])
                for i in range(2):
                    self.tt("dve", flat(St[:, i * 4:(i + 1) * 4, :]), flat(St[:, i * 4:(i + 1) * 4, :]), pS[i][0][:, :],
                            ALU.add, [bSt, pS[i][1]], [bSt])
                self.cp("act", Sb[:], St[:], [bSt], [bSb])
                yield

        gens = [chain(0), chain(1)]
        alive = [True, True]
        for _ in range(4):
            next(gens[0])
        while any(alive):
            for i, gnr in enumerate(gens):
                if alive[i]:
                    try:
                        next(gnr)
                    except StopIteration:
                        alive[i] = False
        ph.close()


def host_consts():
    f32 = np.float32
    t = np.arange(SEQ)
    inv = (np.float32(10000.0) ** (-np.arange(0, 64, 2, dtype=f32) / f32(64))).astype(f32)
    ang_r = ((t // GRID_W).astype(f32)[:, None] * inv).astype(f32)
    ang_c = ((t % GRID_W).astype(f32)[:, None] * inv).astype(f32)
    C = np.zeros((128, SEQ), f32)
    Sg = np.zeros((128, SEQ), f32)
    for f in range(128):
        ang = ang_r if f < 64 else ang_c
        fi = f % 32
        C[f] = np.cos(ang[:, fi])
        Sg[f] = np.sin(ang[:, fi]) * (-1.0 if (f % 64) < 32 else 1.0)
    cm = np.zeros((128, NCM, 128), f32)
    cm[:, CM_ID, :] = np.eye(128, dtype=f32)
    for m in range(128):
        partner = m + 32 if (m % 64) < 32 else m - 32
        cm[partner, CM_PERM, m] = 1.0
    i = np.arange(64)
    le = (i[:, None] <= i[None, :]).astype(f32)
    ge = (i[:, None] >= i[None, :]).astype(f32)
    cm[:64, CM_UF, :64] = le
    cm[:64, CM_UB, :64] = ge
    cm[:64, CM_MNF, :64] = np.where(i[:, None] >= i[None, :], 0.0, NEG)
    cm[:64, CM_MNB, :64] = np.where(i[:, None] <= i[None, :], 0.0, NEG)
    cm[:64, CM_MSF, :64] = (i[:, None] > i[None, :]).astype(f32)
    cm[:64, CM_MSB, :64] = (i[:, None] < i[None, :]).astype(f32)
    cm[:64, CM_NUF, :64] = -le
    cm[:64, CM_NUB, :64] = -ge
    return dict(ropeC=C, ropeS=Sg, cmat=cm)


def fm(v, nchunk):
    sh = v.shape[:-1]
    return np.ascontiguousarray(np.moveaxis(v.reshape(sh + (nchunk, 128)), -1, -2))


def host_shared(inp):
    f32 = np.float32
    out = dict(host_consts())
    for n in ("w_mod", "w_in", "w_conv_out", "w_dn_out", "w_na_out", "w_out", "w_gu", "w_down"):
        out[n] = np.ascontiguousarray(inp[n], dtype=f32)
    vecs = np.zeros((DEPTH, 128, NVEC), f32)
    vecs[:, :, V_BMOD:V_BMOD + 48] = fm(inp["b_mod"], 48)
    vecs[:, :, V_N1:V_N1 + 8] = fm(inp["norm1_g"], 8)
    vecs[:, :, V_N2:V_N2 + 8] = fm(inp["norm2_g"], 8)
    vecs[:, :, V_CDB:V_CDB + 8] = fm(inp["conv_db"], 8)
    vecs[:, :, V_LNG:V_LNG + 8] = fm(inp["conv_ln_g"], 8)
    vecs[:, :, V_LNB:V_LNB + 8] = fm(inp["conv_ln_b"], 8)
    vecs[:, :, V_DNG] = inp["dn_norm_g"]
    out["vecs"] = vecs
    out["conv_dw"] = np.ascontiguousarray(np.transpose(inp["conv_dw"].reshape(DEPTH, 31, 8, 128), (0, 3, 2, 1)), dtype=f32)
    out["dn_conv"] = np.ascontiguousarray(np.transpose(inp["dn_conv"].reshape(DEPTH, 5, 24, 128), (0, 3, 2, 1)), dtype=f32)
    dnp = np.concatenate([inp["dn_a_log"].reshape(DEPTH, 16), inp["dn_dt_bias"].reshape(DEPTH, 16)], -1)
    out["dnp"] = np.ascontiguousarray(np.broadcast_to(dnp[:, None, :], (DEPTH, 128, 32)), dtype=f32)
    rpb = inp["na_rpb"]
    kc = np.arange(64)[:, None]
    qc = np.arange(64)[None, :]
    c0 = np.clip(qc - 8, 0, 48)
    valid = (kc >= c0) & (kc < c0 + 16)
    idx = np.clip(kc - qc + 15, 0, 30)
    tb = rpb[:, :, ::-1, :][:, :, :, idx]
    tb = np.where(valid[None, None, None], tb, f32(NEG))
    out["rpbT"] = np.ascontiguousarray(np.transpose(tb, (0, 3, 1, 2, 4)), dtype=f32)
    out["fin_g"] = fm(inp["final_norm_g"], 8).astype(f32)
    return out


def host_core(inp, b):
    f32 = np.float32
    xT0 = np.concatenate([inp["ctx"][b].T, inp["x"][b].T], axis=1)
    cvec = np.stack([fm(inp["c"][b], 8), fm(inp["c_ctx"], 8)], axis=-1)
    return dict(xT0=np.ascontiguousarray(xT0, dtype=f32), cvec=np.ascontiguousarray(cvec, dtype=f32))


_NC_CACHE = {}


def kernel(**inputs):
    inp = {k: np.asarray(v) for k, v in inputs.items()}
    if "nc" not in _NC_CACHE:
        _NC_CACHE["nc"] = K().build()
    nc = _NC_CACHE["nc"]
    shared = host_shared(inp)
    n = 8
    in_maps = []
    for core in range(n):
        m = dict(shared)
        m.update(host_core(inp, core % 4))
        in_maps.append(m)
    res = run_bass_kernel_spmd(nc, in_maps, core_ids=list(range(n)))
    out = np.stack([np.ascontiguousarray(res.results[b]["yT"].T) for b in range(4)], axis=0)
    return out.astype(np.float32)
```

```python
import contextlib
import numpy as np
import concourse.bass as bass
import concourse.mybir as mybir
from concourse.bass_utils import run_bass_kernel_spmd

F32 = mybir.dt.float32
BF16 = mybir.dt.bfloat16
ALU = mybir.AluOpType
AF = mybir.ActivationFunctionType

EPOCH = 12000
NDMA_SEM = 12

D = 1024
SEQ = 8192
LC = 256
NT = SEQ + LC
DEPTH = 2
IN_DIM = 12320
DFF = 2816
GRID_W = 64
ROWS = SEQ // GRID_W
EPS = 1e-6
OFF = dict(conv=0, dn_q=2048, dn_k=3072, dn_v=4096, dn_z=5120, beta=6144, alpha=6160,
           na_q=6176, na_k=7200, na_v=8224, gate=9248)
TILES = [(0, 256, True)] + [(LC + 512 * i, 512, False) for i in range(SEQ // 512)]
NEG = -30000.0


class Buf:
    __slots__ = ("name", "w", "r")

    def __init__(self, name=""):
        self.name = name
        self.w = None
        self.r = []


class Sched:
    ENG = ("pe", "act", "dve", "pool", "sp")

    def __init__(self, nc, stack):
        self.nc = nc
        self.stack = stack
        self.ops = {e: [] for e in self.ENG}
        self.count = {e: 0 for e in self.ENG}
        self.esems = {e: [] for e in self.ENG}
        self.waited = {e: {} for e in self.ENG}
        self.dsems = {}
        self.dcount = {}
        self.dnext = {}
        for q in ("sp", "act", "pool"):
            self.dsems[q] = [stack.enter_context(nc.semaphore(f"d_{q}_{i}")) for i in range(NDMA_SEM)]
            self.dcount[q] = [0] * NDMA_SEM
            self.dnext[q] = 0
        self.same_engine_sync = {"pe": False, "act": True, "dve": True, "pool": True, "sp": False}

    def _esem(self, eng, idx):
        ep = idx // EPOCH
        while len(self.esems[eng]) <= ep:
            self.esems[eng].append(self.stack.enter_context(
                self.nc.semaphore(f"e_{eng}_{len(self.esems[eng])}")))
        return self.esems[eng][ep], idx % EPOCH + 1

    def _need(self, eng, tok, waits, force=False):
        if tok is None:
            return
        if tok[0] == "e":
            _, src, idx = tok
            if src == eng and not self.same_engine_sync[eng] and not force:
                return
            key = ("e", src)
            if self.waited[eng].get(key, -1) >= idx:
                return
            self.waited[eng][key] = idx
            waits.append(self._esem(src, idx))
        else:
            _, q, si, val = tok
            key = ("d", q, si)
            if self.waited[eng].get(key, -1) >= val:
                return
            self.waited[eng][key] = val
            waits.append((self.dsems[q][si], val))

    def _deps(self, eng, reads, writes):
        waits = []
        for b in reads:
            self._need(eng, b.w, waits)
        for b in writes:
            self._need(eng, b.w, waits)
            for t in b.r:
                self._need(eng, t, waits)
        return waits

    def _commit(self, tok, reads, writes):
        for b in reads:
            b.r.append(tok)
        for b in writes:
            b.w = tok
            b.r = []

    def op(self, eng, fn, reads=(), writes=()):
        waits = self._deps(eng, reads, writes)
        idx = self.count[eng]
        self.count[eng] += 1
        sem, _ = self._esem(eng, idx)
        self.ops[eng].append((waits, fn, (sem, 1)))
        self._commit(("e", eng, idx), reads, writes)

    def dma(self, q, out, in_, reads=(), writes=()):
        waits = self._deps(q, reads, writes)
        si = self.dnext[q]
        self.dnext[q] = (si + 1) % NDMA_SEM
        prev = self.dcount[q][si]
        if prev > 0:
            self._need(q, ("d", q, si, prev), waits)
        val = prev + 16
        self.dcount[q][si] = val
        sem = self.dsems[q][si]
        self.ops[q].append((waits, lambda e, o=out, i=in_: e.dma_start(out=o, in_=i), (sem, 16)))
        tok = ("d", q, si, val)
        self._commit(tok, reads, writes)
        return tok

    def barrier(self):
        for e in self.ENG:
            waits = []
            for s in self.ENG:
                if self.count[s] > 0:
                    self._need(e, ("e", s, self.count[s] - 1), waits, force=True)
            for q in self.dsems:
                for si in range(NDMA_SEM):
                    if self.dcount[q][si] > 0:
                        self._need(e, ("d", q, si, self.dcount[q][si]), waits)
            self.ops[e].append((waits, None, None))

    def emit(self):
        nc = self.nc
        with nc.Block() as block:
            def run(engname):
                def body(e):
                    for waits, fn, inc in self.ops[engname]:
                        for s, v in waits:
                            e.wait_ge(s, v)
                        if fn is not None:
                            ins = fn(e)
                            ins.then_inc(inc[0], inc[1])
                return body
            block.tensor(run("pe"))
            block.scalar(run("act"))
            block.vector(run("dve"))
            block.gpsimd(run("pool"))
            block.sync(run("sp"))


class RR:
    def __init__(self, items):
        self.items = items
        self.i = 0

    def next(self):
        it = self.items[self.i]
        self.i = (self.i + 1) % len(self.items)
        return it


class Phase:
    def __init__(self, K, name):
        self.K = K
        self.name = name
        self.st = contextlib.ExitStack()
        self.n = 0

    def sb(self, shape, dt=F32):
        self.n += 1
        t = self.st.enter_context(self.K.nc.sbuf_tensor(f"{self.name}_{self.n}", list(shape), dt))
        return t, Buf(f"{self.name}_{self.n}")

    def pool(self, n, shape, dt=F32):
        return RR([self.sb(shape, dt) for _ in range(n)])

    def psum(self, n, dt=F32, cols=512):
        out = []
        for _ in range(n):
            self.n += 1
            t = self.st.enter_context(self.K.nc.psum_tensor(f"{self.name}_ps{self.n}", [128, cols], dt))
            out.append((t, Buf(f"{self.name}_ps{self.n}")))
        return RR(out)

    def close(self):
        self.K.S.barrier()
        self.st.close()


def bc(ap, shape, axis):
    return ap.unsqueeze(axis).to_broadcast(list(shape))


NVEC = 89
V_BMOD, V_N1, V_N2, V_CDB, V_LNG, V_LNB, V_DNG = 0, 48, 56, 64, 72, 80, 88
CM_ID, CM_PERM, CM_UF, CM_UB, CM_MNF, CM_MNB, CM_MSF, CM_MSB, CM_NUF, CM_NUB = range(10)
NCM = 10


class K:
    def __init__(self, debug=(), stop_after=None, layers=DEPTH):
        self.debug = set(debug)
        self.stop_after = stop_after
        self.layers = layers
        self.nc = bass.Bass("TRN2", target_bir_lowering=False)
        self.st = contextlib.ExitStack()
        self.S = Sched(self.nc, self.st)
        self.dram = {}

    def mm(self, out, lhsT, rhs, start, stop, reads, writes):
        self.S.op("pe", lambda e: e.matmul(out, lhsT=lhsT, rhs=rhs, start=start, stop=stop), reads, writes)

    def tr(self, out, in_, ident, reads, writes):
        self.S.op("pe", lambda e: e.transpose(out, in_, ident), reads, writes)

    def act(self, out, in_, func, reads, writes, bias=None, scale=None):
        kw = {}
        if bias is not None:
            kw["bias"] = bias
        if scale is not None:
            kw["scale"] = scale
        self.S.op("act", lambda e: e.activation(out=out, in_=in_, func=func, **kw), reads, writes)

    def tt(self, eng, out, in0, in1, op, reads, writes):
        self.S.op(eng, lambda e: e.tensor_tensor(out=out, in0=in0, in1=in1, op=op), reads, writes)

    def ts(self, eng, out, in0, s1, s2, op0, op1, reads, writes):
        if op1 is None:
            self.S.op(eng, lambda e: e.tensor_scalar(out=out, in0=in0, scalar1=s1, scalar2=None, op0=op0), reads, writes)
        else:
            self.S.op(eng, lambda e: e.tensor_scalar(out=out, in0=in0, scalar1=s1, scalar2=s2, op0=op0, op1=op1), reads, writes)

    def stt(self, out, in0, scalar, in1, op0, op1, reads, writes):
        self.S.op("dve", lambda e: e.scalar_tensor_tensor(out=out, in0=in0, scalar=scalar, in1=in1, op0=op0, op1=op1),
                  reads, writes)

    def cp(self, eng, out, in_, reads, writes):
        if eng == "act":
            self.act(out, in_, AF.Copy, reads, writes)
        else:
            self.S.op(eng, lambda e: e.tensor_copy(out=out, in_=in_), reads, writes)

    def ms(self, eng, out, val, writes):
        self.S.op(eng, lambda e: e.memset(out, val), (), writes)

    def rsqrt_(self, out, in_, reads, writes, eps_ap):
        self.act(out, in_, AF.Ln, reads, writes, bias=eps_ap)
        self.act(out, out, AF.Exp, writes, writes, scale=-0.5)

    def din(self, name, shape, dt=F32):
        t = self.nc.dram_tensor(name, list(shape), dt, kind="ExternalInput").ap()
        self.dram[name] = t
        return t

    def dscr(self, name, shape, dt):
        kind = "ExternalOutput" if name in self.debug else "Internal"
        t = self.nc.dram_tensor(name, list(shape), dt, kind=kind).ap()
        self.dram[name] = t
        return t

    def build(self):
        nc, S = self.nc, self.S
        g = self.din
        self.xT0 = g("xT0", [D, NT])
        self.cvec = g("cvec", [128, 8, 2])
        self.w_mod = g("w_mod", [DEPTH, D, 6 * D])
        self.w_in = g("w_in", [DEPTH, D, IN_DIM])
        self.w_sq = {n: g(n, [DEPTH, D, D]) for n in ("w_conv_out", "w_dn_out", "w_na_out", "w_out")}
        self.w_gu = g("w_gu", [DEPTH, D, 2 * DFF])
        self.w_down = g("w_down", [DEPTH, DFF, D])
        self.vecs = g("vecs", [DEPTH, 128, NVEC])
        self.conv_dw = g("conv_dw", [DEPTH, 128, 8, 31])
        self.dn_conv = g("dn_conv", [DEPTH, 128, 24, 5])
        self.dnp = g("dnp", [DEPTH, 128, 32])
        self.rpbT = g("rpbT", [DEPTH, 64, 16, 15, 64])
        self.fin_g = g("fin_g", [128, 8])
        self.ropeC = g("ropeC", [128, SEQ])
        self.ropeS = g("ropeS", [128, SEQ])
        self.cmat = g("cmat", [128, NCM, 128])
        self.yT = nc.dram_tensor("yT", [D, SEQ], F32, kind="ExternalOutput").ap()
        s = self.dscr
        self.wb_in = s("wb_in", [DEPTH, D, IN_DIM], BF16)
        self.wb_sq = {n: s("wb_" + n, [DEPTH, D, D], BF16) for n in self.w_sq}
        self.wb_gu = s("wb_gu", [DEPTH, D, 2 * DFF], BF16)
        self.wb_down = s("wb_down", [DEPTH, DFF, D], BF16)
        self.x1 = s("x1", [D, NT], F32)
        self.hconv = s("hconv", [D, NT], BF16)
        self.dnpre = s("dnpre", [3 * D, NT], BF16)
        self.zs = s("zs", [D, NT], BF16)
        self.bg = s("bg", [NT, 32], F32)
        self.naq = s("naq", [D, NT], BF16)
        self.nak = s("nak", [D, NT], BF16)
        self.nav = s("nav", [NT, D], BF16)
        self.gates = s("gates", [3 * D, NT], BF16)
        self.brA = s("brA", [D, NT], BF16)
        self.brB = s("brB", [D, NT], BF16)
        self.brC = s("brC", [D, NT], BF16)
        self.dq = s("dq", [D, NT], BF16)
        self.dk = s("dk", [D, NT], BF16)
        self.dkt = s("dkt", [NT, D], BF16)
        self.dvt = s("dvt", [NT, D], BF16)
        self.ofw = s("ofw", [D, NT], F32)
        self.obw = s("obw", [D, NT], F32)
        self.modv_d = s("modv_d", [128, 96], F32)

        self.consts()
        self.phase0()
        done = False
        for l in range(self.layers):
            last = (l == DEPTH - 1)
            xsrc = self.xT0 if l == 0 else self.x1
            steps = [("mod", lambda: self.phase_mod(l)),
                     ("p1", lambda: self.phase1(l, xsrc)),
                     ("p2", lambda: self.phase2(l, last)),
                     ("p3a", lambda: self.phase3a(l)),
                     ("p3c", lambda: self.phase3c(l)),
                     ("p3d", lambda: self.phase3d(l, last)),
                     ("p4", lambda: self.phase4(l, last)),
                     ("p5", lambda: self.phase5(l, xsrc, last))]
            for name, fn in steps:
                fn()
                if self.stop_after == (l, name):
                    done = True
                    break
            if done:
                break
        S.barrier()
        S.emit()
        self.st.close()
        return nc

    def consts(self):
        nc, S, st = self.nc, self.S, self.st
        sb = lambda name, shape, dt=F32: st.enter_context(nc.sbuf_tensor(name, list(shape), dt))
        self.cbuf = Buf("consts")
        cb = [self.cbuf]
        self.cm32 = sb("cm32", [128, NCM, 128])
        S.dma("sp", self.cm32[:], self.cmat, writes=cb)
        self.cmb = sb("cmb", [128, 2, 128], BF16)
        self.cp("dve", self.cmb[:], self.cm32[:, 0:2, :], cb, cb)
        self.ones_b = sb("ones_b", [128, 4, 128], BF16)
        for i, v in enumerate((1.0 / 1024, 1.0 / 128, 128.0, 1.0)):
            self.ms("pool", self.ones_b[:, i, :], v, cb)
        self.ones_f = sb("ones_f", [128, 128], F32)
        self.ms("pool", self.ones_f[:], 1.0, cb)
        self.cst = sb("cst", [128, 4], F32)
        for i, v in enumerate((EPS, 128 * EPS, 1.0, 0.0)):
            self.ms("pool", self.cst[:, i:i + 1], v, cb)
        self.eps_t = self.cst[:, 0:1]
        self.eps128_t = self.cst[:, 1:2]
        self.one_t = self.cst[:, 2:3]
        self.vec_t = sb("vec_t", [128, DEPTH, NVEC])
        S.dma("sp", self.vec_t[:], self.vecs.rearrange("l p n -> p l n"), writes=cb)
        self.fing_t = sb("fing_t", [128, 8])
        S.dma("sp", self.fing_t[:], self.fin_g, writes=cb)
        self.modv = sb("modv", [128, 48, 2])
        self.modA = sb("modA", [128, 2, 8, 2])
        self.mbuf = Buf("mod")
        S.barrier()

    def vcol(self, l, off, c):
        return self.vec_t[:, l, off + c:off + c + 1]

    def phase0(self):
        S = self.S
        for l in range(self.layers):
            pairs = [(self.w_in[l], self.wb_in[l], D), (self.w_gu[l], self.wb_gu[l], D),
                     (self.w_down[l], self.wb_down[l], DFF)]
            pairs += [(self.w_sq[n][l], self.wb_sq[n][l], D) for n in self.w_sq]
            for src, dst, rows in pairs:
                for r in range(0, rows, 128):
                    S.dma("pool", dst[r:r + 128, :], src[r:r + 128, :])
        S.barrier()

    def phase_mod(self, l):
        S = self.S
        ph = Phase(self, f"mod{l}")
        cb = [self.cbuf]
        cv, bcv = ph.sb([128, 8, 2])
        S.dma("sp", cv[:], self.cvec, writes=[bcv])
        self.act(cv[:], cv[:], AF.Silu, [bcv], [bcv])
        wp = ph.pool(2, [128, 8, 768])
        ps, bps = ph.psum(1).next()
        wsrc = self.w_mod[l].rearrange("(k p) n -> p k n", p=128)
        for gi in range(8):
            w, bw = wp.next()
            S.dma("sp", w[:], wsrc[:, :, gi * 768:(gi + 1) * 768], writes=[bw])
            for j in range(6):
                jj = gi * 6 + j
                for k in range(8):
                    self.mm(ps[:, jj * 2:jj * 2 + 2], w[:, k, j * 128:(j + 1) * 128], cv[:, k, :],
                            k == 0, k == 7, [bw, bcv], [bps])
        mb = [self.mbuf]
        self.tt("dve", self.modv[:], ps[:, 0:96].rearrange("p (j c) -> p j c", c=2),
                bc(self.vec_t[:, l, V_BMOD:V_BMOD + 48], [128, 48, 2], 2), ALU.add, [bps] + cb, mb)
        for n, (voff, sc) in enumerate(((V_N1, 8), (V_N2, 32))):
            self.ts("dve", self.modA[:, n], self.modv[:, sc:sc + 8, :], 1.0, None, ALU.add, None, mb, mb)
            self.tt("dve", self.modA[:, n], self.modA[:, n],
                    bc(self.vec_t[:, l, voff:voff + 8], [128, 8, 2], 2), ALU.mult, mb + cb, mb)
        if "modv_d" in self.debug:
            S.dma("sp", self.modv_d, self.modv[:].rearrange("p j c -> p (j c)"), reads=mb)
        ph.close()

    def norm_mod(self, ph, x, bx, T, n, col, xn, bxn, sq, bsq, h, bh, rs, brs, psb):
        mb = [self.mbuf]
        cb = [self.cbuf]
        ps, bps = psb
        self.act(sq[:, :, :T], x[:, :, :T], AF.Square, [bx], [bsq])
        for c in range(8):
            self.mm(ps[:, :T], self.ones_b[:, 0, :], sq[:, c, :T], c == 0, c == 7, [bsq] + cb, [bps])
        self.rsqrt_(rs[:, :T], ps[:, :T], [bps] + cb, [brs], self.eps_t)
        self.tt("dve", xn[:, :, :T], x[:, :, :T], bc(rs[:, :T], [128, 8, T], 1), ALU.mult, [bx, brs], [bxn])
        shift = 0 if n == 0 else 24
        for c in range(8):
            if c % 2 == 0:
                self.act(h[:, c, :T], xn[:, c, :T], AF.Identity, [bxn] + mb, [bh],
                         bias=self.modv[:, shift + c, col:col + 1], scale=self.modA[:, n, c, col:col + 1])
            else:
                self.ts("dve", h[:, c, :T], xn[:, c, :T], self.modA[:, n, c, col:col + 1],
                        self.modv[:, shift + c, col:col + 1], ALU.mult, ALU.add, [bxn] + mb, [bh])

    def phase1(self, l, xsrc):
        S = self.S
        ph = Phase(self, f"p1_{l}")
        cb = [self.cbuf]
        xp = ph.pool(2, [128, 8, 512])
        xnp = ph.pool(1, [128, 8, 512])
        sqp = ph.pool(1, [128, 8, 512], BF16)
        hp = ph.pool(2, [128, 8, 512], BF16)
        rsp = ph.pool(1, [128, 512])
        wp = ph.pool(3, [128, 8, 512], BF16)
        stp = ph.pool(3, [128, 4, 512], BF16)
        tmpp = ph.pool(2, [128, 512])
        bgp = ph.pool(2, [128, 4, 32])
        banks = ph.psum(7)
        psn = ph.psum(1).next()
        dnp_t, bdnp = ph.sb([128, 32])
        S.dma("sp", dnp_t[:], self.dnp[l], writes=[bdnp])
        self.act(dnp_t[:, 0:16], dnp_t[:, 0:16], AF.Exp, [bdnp], [bdnp])
        self.ts("dve", dnp_t[:, 0:16], dnp_t[:, 0:16], -1.0, None, ALU.mult, None, [bdnp], [bdnp])

        wsrc = self.wb_in[l].rearrange("(k p) n -> p k n", p=128)
        xs = xsrc.rearrange("(c p) t -> p c t", p=128)
        groups = []
        for j in range(0, 8, 2):
            groups.append(("glu", [(j * 128, 256), (1024 + j * 128, 256)], self.hconv, j * 128))
        for i in range(6):
            groups.append(("copy", [(2048 + i * 512, 512)], self.dnpre, i * 512))
        for i in range(2):
            groups.append(("silu", [(OFF["dn_z"] + i * 512, 512)], self.zs, i * 512))
        groups.append(("ba", [(OFF["beta"], 32)], self.bg, 0))
        for i in range(2):
            groups.append(("copy", [(OFF["na_q"] + i * 512, 512)], self.naq, i * 512))
        for i in range(2):
            groups.append(("copy", [(OFF["na_k"] + i * 512, 512)], self.nak, i * 512))
        for i in range(2):
            groups.append(("tok", [(OFF["na_v"] + i * 512, 512)], self.nav, i * 512))
        for i in range(6):
            groups.append(("sigmoid", [(OFF["gate"] + i * 512, 512)], self.gates, i * 512))

        for (t0, T, isctx) in TILES:
            col = 1 if isctx else 0
            x, bx = xp.next()
            S.dma("sp", x[:, :, :T], xs[:, :, t0:t0 + T], writes=[bx])
            xn, bxn = xnp.next()
            sq, bsq = sqp.next()
            h, bh = hp.next()
            rs, brs = rsp.next()
            self.norm_mod(ph, x, bx, T, 0, col, xn, bxn, sq, bsq, h, bh, rs, brs, psn)
            nsub = T // 128
            for gi, (kind, ranges, dst, drow) in enumerate(groups):
                w, bw = wp.next()
                o = 0
                for (c0, wd) in ranges:
                    S.dma("sp", w[:, :, o:o + wd], wsrc[:, :, c0:c0 + wd], writes=[bw])
                    o += wd
                if kind in ("glu", "copy", "silu", "sigmoid"):
                    pss = []
                    for j in range(4):
                        ps, bps = banks.next()
                        for k in range(8):
                            self.mm(ps[:, :T], w[:, k, j * 128:(j + 1) * 128], h[:, k, :T], k == 0, k == 7,
                                    [bw, bh], [bps])
                        pss.append((ps, bps))
                    stg, bst = stp.next()
                    if kind == "glu":
                        for j in range(2):
                            tmp, btmp = tmpp.next()
                            self.act(tmp[:, :T], pss[2 + j][0][:, :T], AF.Sigmoid, [pss[2 + j][1]], [btmp])
                            self.tt("dve", stg[:, j, :T], pss[j][0][:, :T], tmp[:, :T], ALU.mult,
                                    [pss[j][1], btmp], [bst])
                        nout = 2
                    else:
                        for j in range(4):
                            if kind == "copy":
                                self.cp("dve", stg[:, j, :T], pss[j][0][:, :T], [pss[j][1]], [bst])
                            else:
                                self.act(stg[:, j, :T], pss[j][0][:, :T], AF.Silu if kind == "silu" else AF.Sigmoid,
                                         [pss[j][1]], [bst])
                        nout = 4
                    S.dma("pool", dst[drow:drow + nout * 128, t0:t0 + T].rearrange("(j p) t -> p j t", p=128),
                          stg[:, 0:nout, :T], reads=[bst])
                elif kind == "tok":
                    for s in range(nsub):
                        ps, bps = banks.next()
                        for k in range(8):
                            self.mm(ps[:, :], h[:, k, s * 128:(s + 1) * 128], w[:, k, :], k == 0, k == 7,
                                    [bw, bh], [bps])
                        stg, bst = stp.next()
                        self.cp("act" if s % 2 == 0 else "dve", stg[:, 0, :], ps[:, :], [bps], [bst])
                        S.dma("pool", dst[t0 + s * 128:t0 + (s + 1) * 128, drow:drow + 512], stg[:, 0, :], reads=[bst])
                else:
                    ps, bps = banks.next()
                    for s in range(nsub):
                        for k in range(8):
                            self.mm(ps[:, s * 32:(s + 1) * 32], h[:, k, s * 128:(s + 1) * 128], w[:, k, 0:32],
                                    k == 0, k == 7, [bw, bh], [bps])
                    bgt, bbg = bgp.next()
                    pv = ps[:, 0:nsub * 32].rearrange("p (s c) -> p s c", c=32)
                    self.act(bgt[:, :nsub, 0:16], pv[:, :, 0:16], AF.Sigmoid, [bps], [bbg])
                    self.tt("dve", bgt[:, :nsub, 16:32], pv[:, :, 16:32], bc(dnp_t[:, 16:32], [128, nsub, 16], 1),
                            ALU.add, [bps, bdnp], [bbg])
                    self.ts("dve", bgt[:, :nsub, 16:32], bgt[:, :nsub, 16:32], 60.0, None, ALU.min, None, [bbg], [bbg])
                    self.act(bgt[:, :nsub, 16:32], bgt[:, :nsub, 16:32], AF.Exp, [bbg], [bbg])
                    self.act(bgt[:, :nsub, 16:32], bgt[:, :nsub, 16:32], AF.Ln, [bbg] + cb, [bbg], bias=self.one_t)
                    self.tt("dve", bgt[:, :nsub, 16:32], bgt[:, :nsub, 16:32], bc(dnp_t[:, 0:16], [128, nsub, 16], 1),
                            ALU.mult, [bbg, bdnp], [bbg])
                    S.dma("pool", dst[t0:t0 + T, :].rearrange("(s p) c -> p s c", p=128), bgt[:, :nsub, :], reads=[bbg])
        ph.close()

    def load_w_sq(self, ph, name, l):
        w, bw = ph.sb([128, 8, D], BF16)
        self.S.dma("sp", w[:], self.wb_sq[name][l].rearrange("(k p) n -> p k n", p=128), writes=[bw])
        return w, bw

    def proj_store(self, y, by, T, w, bw, banks, stp, dst, t0):
        for half in range(2):
            stg, bst = stp.next()
            for j in range(4):
                jj = half * 4 + j
                ps, bps = banks.next()
                for k in range(8):
                    self.mm(ps[:, :T], w[:, k, jj * 128:(jj + 1) * 128], y[:, k, :T], k == 0, k == 7, [bw, by], [bps])
                self.cp("act" if j % 2 == 0 else "dve", stg[:, j, :T], ps[:, :T], [bps], [bst])
            self.S.dma("pool", dst[half * 512:(half + 1) * 512, t0:t0 + T].rearrange("(j p) t -> p j t", p=128),
                       stg[:, :, :T], reads=[bst])

    def phase2(self, l, last):
        S = self.S
        ph = Phase(self, f"p2_{l}")
        cb = [self.cbuf]
        w, bw = self.load_w_sq(ph, "w_conv_out", l)
        dw, bdw = ph.sb([128, 8, 31])
        S.dma("sp", dw[:], self.conv_dw[l], writes=[bdw])
        diag, bdiag = ph.sb([128, 248, 128], BF16)
        for c in range(8):
            for k in range(31):
                self.ts("dve" if (c * 31 + k) % 2 == 0 else "pool", diag[:, c * 31 + k, :], self.cm32[:, CM_ID, :],
                        dw[:, c, k:k + 1], None, ALU.mult, None, [bdw] + cb, [bdiag])
        hinp = ph.pool(2, [128, 8, 512 + 30], BF16)
        accp = ph.pool(2, [128, 8, 512])
        hbp = ph.pool(1, [128, 8, 512], BF16)
        sqp = ph.pool(1, [128, 8, 512], BF16)
        yp = ph.pool(2, [128, 8, 512], BF16)
        stp = ph.pool(2, [128, 4, 512], BF16)
        mp = ph.pool(1, [128, 512])
        m2p = ph.pool(1, [128, 512])
        rsp = ph.pool(1, [128, 512])
        banks = ph.psum(6)
        pstat = ph.psum(2)
        hsrc = self.hconv.rearrange("(c p) t -> p c t", p=128)
        for (t0, T, isctx) in TILES:
            if isctx and last:
                continue
            s0, s1 = (0, LC) if isctx else (LC, NT)
            lo, hi = max(t0 - 15, s0), min(t0 + T + 15, s1)
            hin, bhin = hinp.next()
            if lo > t0 - 15:
                self.ms("pool", hin[:, :, 0:15], 0.0, [bhin])
            if hi < t0 + T + 15:
                self.ms("pool", hin[:, :, T + 15:T + 30], 0.0, [bhin])
            S.dma("sp", hin[:, :, lo - (t0 - 15):hi - (t0 - 15)], hsrc[:, :, lo:hi], writes=[bhin])
            acc, bacc = accp.next()
            for c in range(8):
                ps, bps = banks.next()
                for k in range(31):
                    self.mm(ps[:, :T], diag[:, c * 31 + k, :], hin[:, c, k:k + T], k == 0, k == 30, [bhin, bdiag], [bps])
                if c % 2 == 0:
                    self.act(acc[:, c, :T], ps[:, :T], AF.Identity, [bps] + cb, [bacc], bias=self.vcol(l, V_CDB, c))
                else:
                    self.ts("dve", acc[:, c, :T], ps[:, :T], self.vcol(l, V_CDB, c), None, ALU.add, None, [bps] + cb, [bacc])
            hb, bhb = hbp.next()
            sq, bsq = sqp.next()
            self.cp("act", hb[:, :, :T], acc[:, :, :T], [bacc], [bhb])
            self.act(sq[:, :, :T], acc[:, :, :T], AF.Square, [bacc], [bsq])
            pm, bpm = pstat.next()
            pq, bpq = pstat.next()
            for c in range(8):
                self.mm(pm[:, :T], self.ones_b[:, 0, :], hb[:, c, :T], c == 0, c == 7, [bhb] + cb, [bpm])
            for c in range(8):
                self.mm(pq[:, :T], self.ones_b[:, 0, :], sq[:, c, :T], c == 0, c == 7, [bsq] + cb, [bpq])
            mean, bmean = mp.next()
            m2, bm2 = m2p.next()
            rs, brs = rsp.next()
            self.cp("act", mean[:, :T], pm[:, :T], [bpm], [bmean])
            self.tt("dve", m2[:, :T], mean[:, :T], mean[:, :T], ALU.mult, [bmean], [bm2])
            self.tt("dve", m2[:, :T], pq[:, :T], m2[:, :T], ALU.subtract, [bpq, bm2], [bm2])
            self.ts("dve", m2[:, :T], m2[:, :T], 0.0, None, ALU.max, None, [bm2], [bm2])
            self.rsqrt_(rs[:, :T], m2[:, :T], [bm2] + cb, [brs], self.eps_t)
            self.tt("dve", acc[:, :, :T], acc[:, :, :T], bc(mean[:, :T], [128, 8, T], 1), ALU.subtract,
                    [bacc, bmean], [bacc])
            self.tt("dve", acc[:, :, :T], acc[:, :, :T], bc(rs[:, :T], [128, 8, T], 1), ALU.mult, [bacc, brs], [bacc])
            y, by = yp.next()
            for c in range(8):
                self.act(y[:, c, :T], acc[:, c, :T], AF.Silu, [bacc] + cb, [by],
                         bias=self.vcol(l, V_LNB, c), scale=self.vcol(l, V_LNG, c))
            self.proj_store(y, by, T, w, bw, banks, stp, self.brA, t0)
        ph.close()

    def phase5(self, l, xsrc, last):
        S = self.S
        ph = Phase(self, f"p5_{l}")
        cb = [self.cbuf]
        mb = [self.mbuf]
        w, bw = self.load_w_sq(ph, "w_out", l)
        xp = ph.pool(1, [128, 8, 512])
        xnp = ph.pool(1, [128, 8, 512])
        sqp = ph.pool(1, [128, 8, 512], BF16)
        hp = ph.pool(1, [128, 8, 512], BF16)
        rsp = ph.pool(1, [128, 512])
        brp = ph.pool(3, [128, 8, 512], BF16)
        gp = ph.pool(1, [128, 24, 512], BF16)
        t1p = ph.pool(2, [128, 512])
        t2p = ph.pool(2, [128, 512])
        mgp = ph.pool(1, [128, 8, 512], BF16)
        wgp = ph.pool(2, [128, 8, 512], BF16)
        wdp = ph.pool(1, [128, 22, 512], BF16)
        fp = ph.pool(1, [128, 22, 512], BF16)
        tmpp = ph.pool(2, [128, 512])
        banks = ph.psum(7)
        psn = ph.psum(1).next()
        xs = xsrc.rearrange("(c p) t -> p c t", p=128)
        wgs = self.wb_gu[l].rearrange("(k p) n -> p k n", p=128)
        wds = self.wb_down[l].rearrange("(k p) n -> p k n", p=128)
        fm3 = lambda t: t.rearrange("(c p) t -> p c t", p=128)
        for (t0, T, isctx) in TILES:
            if isctx and last:
                continue
            col = 1 if isctx else 0
            x, bx = xp.next()
            S.dma("sp", x[:, :, :T], xs[:, :, t0:t0 + T], writes=[bx])
            g, bg_ = gp.next()
            S.dma("sp", g[:, :, :T], fm3(self.gates)[:, :, t0:t0 + T], writes=[bg_])
            brs_ = []
            for src in (self.brA, self.brB, self.brC):
                b_, bb_ = brp.next()
                S.dma("sp", b_[:, :, :T], fm3(src)[:, :, t0:t0 + T], writes=[bb_])
                brs_.append((b_, bb_))
            mg, bmg = mgp.next()
            for c in range(8):
                t1, bt1 = t1p.next()
                t2, bt2 = t2p.next()
                self.tt("dve", t1[:, :T], g[:, c, :T], brs_[0][0][:, c, :T], ALU.mult, [bg_, brs_[0][1]], [bt1])
                self.tt("pool", t2[:, :T], g[:, 8 + c, :T], brs_[1][0][:, c, :T], ALU.mult, [bg_, brs_[1][1]], [bt2])
                self.tt("dve", t1[:, :T], t1[:, :T], t2[:, :T], ALU.add, [bt1, bt2], [bt1])
                self.tt("pool", t2[:, :T], g[:, 16 + c, :T], brs_[2][0][:, c, :T], ALU.mult, [bg_, brs_[2][1], bt1], [bt2])
                self.tt("dve", mg[:, c, :T], t1[:, :T], t2[:, :T], ALU.add, [bt1, bt2], [bmg])
            for j in range(8):
                ps, bps = banks.next()
                for k in range(8):
                    self.mm(ps[:, :T], w[:, k, j * 128:(j + 1) * 128], mg[:, k, :T], k == 0, k == 7, [bw, bmg], [bps])
                self.stt(x[:, j, :T], ps[:, :T], self.modv[:, 16 + j, col:col + 1], x[:, j, :T], ALU.mult, ALU.add,
                         [bps, bx] + mb, [bx])
            xn, bxn = xnp.next()
            sq, bsq = sqp.next()
            h, bh = hp.next()
            rs, brs = rsp.next()
            self.norm_mod(ph, x, bx, T, 1, col, xn, bxn, sq, bsq, h, bh, rs, brs, psn)
            f, bf = fp.next()
            for gi in range(11):
                wg, bwg = wgp.next()
                S.dma("sp", wg[:, :, 0:256], wgs[:, :, gi * 256:(gi + 1) * 256], writes=[bwg])
                S.dma("sp", wg[:, :, 256:512], wgs[:, :, DFF + gi * 256:DFF + (gi + 1) * 256], writes=[bwg])
                pss = []
                for j in range(4):
                    ps, bps = banks.next()
                    for k in range(8):
                        self.mm(ps[:, :T], wg[:, k, j * 128:(j + 1) * 128], h[:, k, :T], k == 0, k == 7, [bwg, bh], [bps])
                    pss.append((ps, bps))
                for j in range(2):
                    tmp, btmp = tmpp.next()
                    self.act(tmp[:, :T], pss[j][0][:, :T], AF.Silu, [pss[j][1]], [btmp])
                    self.tt("dve", f[:, gi * 2 + j, :T], tmp[:, :T], pss[2 + j][0][:, :T], ALU.mult,
                            [btmp, pss[2 + j][1]], [bf])
            for half in range(2):
                wd, bwd = wdp.next()
                S.dma("sp", wd[:], wds[:, :, half * 512:(half + 1) * 512], writes=[bwd])
                for j in range(4):
                    jj = half * 4 + j
                    ps, bps = banks.next()
                    for k in range(22):
                        self.mm(ps[:, :T], wd[:, k, j * 128:(j + 1) * 128], f[:, k, :T], k == 0, k == 21, [bwd, bf], [bps])
                    self.stt(x[:, jj, :T], ps[:, :T], self.modv[:, 40 + jj, col:col + 1], x[:, jj, :T], ALU.mult, ALU.add,
                             [bps, bx] + mb, [bx])
            if not last:
                S.dma("pool", fm3(self.x1)[:, :, t0:t0 + T], x[:, :, :T], reads=[bx])
            else:
                sq2, bsq2 = sqp.next()
                self.act(sq2[:, :, :T], x[:, :, :T], AF.Square, [bx], [bsq2])
                ps, bps = psn
                for c in range(8):
                    self.mm(ps[:, :T], self.ones_b[:, 0, :], sq2[:, c, :T], c == 0, c == 7, [bsq2] + cb, [bps])
                rs2, brs2 = rsp.next()
                self.rsqrt_(rs2[:, :T], ps[:, :T], [bps] + cb, [brs2], self.eps_t)
                xn2, bxn2 = xnp.next()
                self.tt("dve", xn2[:, :, :T], x[:, :, :T], bc(rs2[:, :T], [128, 8, T], 1), ALU.mult, [bx, brs2], [bxn2])
                self.tt("pool", xn2[:, :, :T], xn2[:, :, :T], bc(self.fing_t[:], [128, 8, T], 2), ALU.mult,
                        [bxn2] + cb, [bxn2])
                S.dma("pool", fm3(self.yT)[:, :, t0 - LC:t0 - LC + T], xn2[:, :, :T], reads=[bxn2])
        ph.close()

    def phase4(self, l, last):
        S = self.S
        ph = Phase(self, f"p4_{l}")
        cb = [self.cbuf]
        wna, bwna = ph.sb([64, 16, D], BF16)
        S.dma("sp", wna[:], self.wb_sq["w_na_out"][l].rearrange("(h p) n -> p h n", p=64), writes=[bwna])
        E, bE = ph.sb([64, 16, 15, 64], BF16)
        E2, bE2 = ph.sb([128, 16, 15, 64], BF16)
        tmpph = Phase(self, f"p4e_{l}")
        ep = tmpph.pool(2, [64, 2, 15, 64])
        for i in range(8):
            e32, be32 = ep.next()
            S.dma("sp", e32[:], self.rpbT[l][:, i * 2:(i + 1) * 2], writes=[be32])
            self.act(E[:, i * 2:(i + 1) * 2], e32[:], AF.Exp, [be32], [bE])
        self.ms("pool", E2[:, 0:8], 0.0, [bE2])
        self.ms("pool", E2[:, 8:16], 0.0, [bE2])
        ep2 = tmpph.pool(2, [128, 2, 8, 64])
        for i in range(8):
            e32, be32 = ep2.next()
            S.dma("sp", e32[0:64], self.rpbT[l][:, i * 2:(i + 1) * 2, 4:12, :], writes=[be32])
            S.dma("sp", e32[64:128], self.rpbT[l][:, i * 2:(i + 1) * 2, 4:12, :], writes=[be32])
            self.act(E2[0:64, i * 2:(i + 1) * 2, 4:12, :], e32[0:64], AF.Exp, [be32], [bE2])
            self.act(E2[64:128, i * 2:(i + 1) * 2, 5:13, :], e32[64:128], AF.Exp, [be32], [bE2])
        tmpph.close()
        kTc, bkTc = ph.sb([128, 8, LC], BF16)
        S.dma("sp", kTc[:], self.nak.rearrange("(c p) t -> p c t", p=128)[:, :, 0:LC], writes=[bkTc])
        Vc, bVc = ph.sb([128, 2, D], BF16)
        S.dma("sp", Vc[:], self.nav[0:LC, :].rearrange("(j p) f -> p j f", p=128), writes=[bVc])
        kTp = ph.pool(1, [128, 8, 960], BF16)
        Vp = ph.pool(1, [128, 8, D], BF16)
        Voddp = ph.pool(1, [64, 6, D], BF16)
        qp = ph.pool(2, [128, 8, 512], BF16)
        oTp = ph.pool(1, [64, 16, 512], BF16)
        Pp = ph.pool(3, [128, 512], BF16)
        Pfp = ph.pool(3, [128, 512])
        recp = ph.pool(2, [64, 512])
        stp = ph.pool(2, [128, 4, 512], BF16)
        sbanks = ph.psum(3)
        accs = ph.psum(4)
        pbank = ph.psum(1)
        naqs = self.naq.rearrange("(c p) t -> p c t", p=128)
        naks = self.nak.rearrange("(c p) t -> p c t", p=128)
        r0f = lambda qr: min(max(qr - 4, 0), ROWS - 8)
        for (t0, T, isctx) in TILES:
            if isctx and last:
                continue
            q, bq = qp.next()
            S.dma("sp", q[:, :, :T], naqs[:, :, t0:t0 + T], writes=[bq])
            rows = []
            if not isctx:
                R0 = (t0 - LC) // GRID_W
                kmin, kmax = r0f(R0), r0f(R0 + 7) + 7
                nr = kmax - kmin + 1
                kT, bkT = kTp.next()
                S.dma("sp", kT[:, :, 0:nr * 64], naks[:, :, LC + kmin * 64:LC + (kmax + 1) * 64], writes=[bkT])
                V, bV = Vp.next()
                interior = (nr == 15 and kmin == R0 - 4)
                npair = nr // 2
                S.dma("sp", V[:, 0:npair, :],
                      self.nav[LC + kmin * 64:LC + (kmin + 2 * npair) * 64, :].rearrange("(r p) f -> p r f", p=128), writes=[bV])
                if nr % 2:
                    S.dma("sp", V[0:64, npair, :], self.nav[LC + kmax * 64:LC + (kmax + 1) * 64, :], writes=[bV])
                Vodd, bVodd = Voddp.next()
                if not interior:
                    nodd = nr // 2
                    S.dma("sp", Vodd[:, 0:nodd, :],
                          self.nav[LC + kmin * 64:LC + (kmin + 2 * nodd) * 64, :].rearrange("(r two p) f -> two p r f", two=2, p=64)[1],
                          writes=[bVodd])
                for kr in range(kmin, kmax + 1):
                    qs = [qr for qr in range(R0, R0 + 8) if r0f(qr) <= kr <= r0f(qr) + 7]
                    if not qs:
                        continue
                    if interior and kr < kmin + 14:
                        if (kr - kmin) % 2 == 0:
                            qs2 = [qr for qr in range(R0, R0 + 8) if r0f(qr) <= kr + 1 <= r0f(qr) + 7]
                            allq = sorted(set(qs) | set(qs2))
                            rows.append(("pair", kr, allq[0], allq[-1] + 1))
                    else:
                        rows.append(("row", kr, qs[0], qs[-1] + 1))
            oT, boT = oTp.next()
            steps = []
            for h in range(16):
                steps.append(("ctx", h, 0, None))
                for ri, row in enumerate(rows):
                    steps.append(("row", h, ri, row))
                steps.append(("ctx", h, 1, None))
            state = {}

            def emit_qk(st_):
                kind, h, a, row = st_
                c, po = h // 2, (h % 2) * 64
                ps, bps = sbanks.next()
                P, bP = Pp.next()
                if kind == "ctx":
                    self.mm(ps[:, :T], kTc[po:po + 64, c, a * 128:(a + 1) * 128], q[po:po + 64, c, :T], True, True,
                            [bkTc, bq], [bps])
                    self.act(P[:, :T], ps[:, :T], AF.Exp, [bps], [bP], scale=0.125)
                else:
                    rk, kr, qlo, qhi = row
                    cs, nq = (qlo - R0) * 64, qhi - qlo
                    n = nq * 64
                    np_ = 128 if rk == "pair" else 64
                    self.mm(ps[0:np_, 0:n], kT[po:po + 64, c, (kr - kmin) * 64:(kr - kmin) * 64 + np_],
                            q[po:po + 64, c, cs:cs + n], True, True, [bkT, bq], [bps])
                    Pf, bPf = Pfp.next()
                    self.act(Pf[0:np_, 0:n], ps[0:np_, 0:n], AF.Exp, [bps], [bPf], scale=0.125)
                    elo = qlo - kr + 7
                    tab, btab = (E2, bE2) if rk == "pair" else (E, bE)
                    self.tt("dve" if a % 2 == 0 else "pool", P[0:np_, 0:n].rearrange("p (a b) -> p a b", b=64),
                            Pf[0:np_, 0:n].rearrange("p (a b) -> p a b", b=64), tab[0:np_, h, elo:elo + nq, :], ALU.mult,
                            [bPf, btab], [bP])
                return (P, bP)

            def emit_pv(st_, pt):
                kind, h, a, row = st_
                P, bP = pt
                if kind == "ctx" and a == 0:
                    state["acc"] = accs.next()
                    state["accS"] = accs.next()
                acc, bacc = state["acc"]
                accS, baccS = state["accS"]
                if kind == "ctx":
                    first = (a == 0)
                    self.mm(acc[0:64, :T], Vc[:, a, h * 64:(h + 1) * 64], P[:, :T], first, not first, [bVc, bP], [bacc])
                    self.mm(accS[0:64, :T], self.ones_b[:, 3, 0:64], P[:, :T], first, not first, [bP] + cb, [baccS])
                    if not first:
                        rec, brec = recp.next()
                        self.S.op("dve", lambda e, o=rec[:, :T], i=accS[0:64, :T]: e.reciprocal(out=o, in_=i), [baccS], [brec])
                        self.tt("dve", oT[:, h, :T], acc[0:64, :T], rec[:, :T], ALU.mult, [bacc, brec], [boT])
                else:
                    rk, kr, qlo, qhi = row
                    cs, n = (qlo - R0) * 64, (qhi - qlo) * 64
                    np_ = 128 if rk == "pair" else 64
                    if rk == "row" and (kr - kmin) % 2 == 1:
                        vsrc, bvsrc = Vodd[0:64, (kr - kmin) // 2, h * 64:(h + 1) * 64], bVodd
                    else:
                        vsrc, bvsrc = V[0:np_, (kr - kmin) // 2, h * 64:(h + 1) * 64], bV
                    self.mm(acc[0:64, cs:cs + n], vsrc, P[0:np_, 0:n], False, False, [bvsrc, bP], [bacc])
                    self.mm(accS[0:64, cs:cs + n], self.ones_b[0:np_, 3, 0:64], P[0:np_, 0:n], False, False,
                            [bP] + cb, [baccS])

            LA = 2
            pts = {}
            for i in range(len(steps) + LA):
                if i < len(steps):
                    pts[i] = emit_qk(steps[i])
                if i - LA >= 0:
                    emit_pv(steps[i - LA], pts.pop(i - LA))
            for half in range(2):
                stg, bst = stp.next()
                for j in range(4):
                    jj = half * 4 + j
                    ps, bps = pbank.next()
                    for h in range(16):
                        self.mm(ps[:, :T], wna[:, h, jj * 128:(jj + 1) * 128], oT[:, h, :T], h == 0, h == 15,
                                [bwna, boT], [bps])
                    self.cp("act" if j % 2 == 0 else "dve", stg[:, j, :T], ps[:, :T], [bps], [bst])
                S.dma("pool", self.brC[half * 512:(half + 1) * 512, t0:t0 + T].rearrange("(j p) t -> p j t", p=128),
                      stg[:, :, :T], reads=[bst])
        ph.close()

    def phase3a(self, l):
        S = self.S
        ph = Phase(self, f"p3a_{l}")
        cb = [self.cbuf]
        dcw, bdcw = ph.sb([128, 24, 5])
        S.dma("sp", dcw[:], self.dn_conv[l], writes=[bdcw])
        diag, bdiag = ph.sb([128, 120, 128], BF16)
        for cc in range(24):
            for k in range(5):
                self.ts("dve" if (cc * 5 + k) % 2 == 0 else "pool", diag[:, cc * 5 + k, :], self.cm32[:, CM_ID, :],
                        dcw[:, cc, k:k + 1], None, ALU.mult, None, [bdcw] + cb, [bdiag])
        pinp = ph.pool(2, [128, 24, 512 + 4], BF16)
        accp = ph.pool(2, [128, 8, 512])
        sqp = ph.pool(1, [128, 8, 512], BF16)
        rsp = ph.pool(1, [128, 8, 512])
        sbp = ph.pool(1, [128, 8, 512], BF16)
        outp = ph.pool(2, [128, 8, 512], BF16)
        ropep = ph.pool(2, [128, 2, 512])
        t1p = ph.pool(2, [128, 512])
        t2p = ph.pool(2, [128, 512])
        stp = ph.pool(2, [128, 1024], BF16)
        pst = ph.psum(2, BF16, 1024)
        pso = ph.psum(3)
        psp = ph.psum(3)
        src = self.dnpre.rearrange("(c p) t -> p c t", p=128)
        ident = self.cmb[:, 0, :]
        perm = self.cmb[:, 1, :]

        def transposes(t, bt, T, dst, t0):
            for sidx in range(T // 128):
                ps, bps = pst.next()
                for hh in range(8):
                    self.tr(ps[:, hh * 128:(hh + 1) * 128], t[:, hh, sidx * 128:(sidx + 1) * 128], ident, [bt] + cb, [bps])
                stg, bst = stp.next()
                self.cp("act" if sidx % 2 == 0 else "dve", stg[:], ps[:], [bps], [bst])
                S.dma("pool", dst[t0 + sidx * 128:t0 + (sidx + 1) * 128, :], stg[:], reads=[bst])

        for (t0, T, isctx) in TILES:
            s0, s1 = (0, LC) if isctx else (LC, NT)
            lo, hi = max(t0 - 2, s0), min(t0 + T + 2, s1)
            pin, bpin = pinp.next()
            if lo > t0 - 2:
                self.ms("pool", pin[:, :, 0:2], 0.0, [bpin])
            if hi < t0 + T + 2:
                self.ms("pool", pin[:, :, T + 2:T + 4], 0.0, [bpin])
            S.dma("sp", pin[:, :, lo - (t0 - 2):hi - (t0 - 2)], src[:, :, lo:hi], writes=[bpin])
            if not isctx:
                rope, brope = ropep.next()
                S.dma("sp", rope[:, 0, :T], self.ropeC[:, t0 - LC:t0 - LC + T], writes=[brope])
                S.dma("sp", rope[:, 1, :T], self.ropeS[:, t0 - LC:t0 - LC + T], writes=[brope])
            for part in range(3):
                acc, bacc = accp.next()
                out, bout = outp.next()
                for c in range(8):
                    cc = part * 8 + c
                    ps, bps = pso.next()
                    for k in range(5):
                        self.mm(ps[:, :T], diag[:, cc * 5 + k, :], pin[:, cc, k:k + T], k == 0, k == 4, [bpin, bdiag], [bps])
                    if part == 2:
                        self.act(out[:, c, :T], ps[:, :T], AF.Silu, [bps], [bout])
                    else:
                        self.act(acc[:, c, :T], ps[:, :T], AF.Silu, [bps], [bacc])
                if part == 2:
                    transposes(out, bout, T, self.dvt, t0)
                    continue
                sq, bsq = sqp.next()
                self.act(sq[:, :, :T], acc[:, :, :T], AF.Square, [bacc], [bsq])
                rs, brs = rsp.next()
                for c in range(8):
                    ps, bps = pso.next()
                    self.mm(ps[:, :T], self.ones_b[:, 2 if part == 0 else 3, :], sq[:, c, :T], True, True, [bsq] + cb, [bps])
                    self.rsqrt_(rs[:, c, :T], ps[:, :T], [bps] + cb, [brs], self.eps128_t if part == 0 else self.eps_t)
                if isctx:
                    self.tt("dve", out[:, :, :T], acc[:, :, :T], rs[:, :, :T], ALU.mult, [bacc, brs], [bout])
                else:
                    sb_, bsb = sbp.next()
                    self.cp("pool", sb_[:, :, :T], acc[:, :, :T], [bacc], [bsb])
                    for c in range(8):
                        ps, bps = psp.next()
                        self.mm(ps[:, :T], perm, sb_[:, c, :T], True, True, [bsb] + cb, [bps])
                        t1, bt1 = t1p.next()
                        t2, bt2 = t2p.next()
                        self.tt("pool", t1[:, :T], acc[:, c, :T], rope[:, 0, :T], ALU.mult, [bacc, brope], [bt1])
                        self.tt("dve", t2[:, :T], ps[:, :T], rope[:, 1, :T], ALU.mult, [bps, brope], [bt2])
                        self.tt("dve", t1[:, :T], t1[:, :T], t2[:, :T], ALU.add, [bt1, bt2], [bt1])
                        self.tt("dve", out[:, c, :T], t1[:, :T], rs[:, c, :T], ALU.mult, [bt1, brs], [bout])
                dst = self.dq if part == 0 else self.dk
                S.dma("pool", dst.rearrange("(c p) t -> p c t", p=128)[:, :, t0:t0 + T], out[:, :, :T], reads=[bout])
                if part == 1:
                    transposes(out, bout, T, self.dkt, t0)
        ph.close()

    def phase3d(self, l, last):
        S = self.S
        ph = Phase(self, f"p3d_{l}")
        cb = [self.cbuf]
        w, bw = self.load_w_sq(ph, "w_dn_out", l)
        ofp = ph.pool(2, [128, 8, 512])
        obp = ph.pool(2, [128, 8, 512])
        zp = ph.pool(2, [128, 8, 512], BF16)
        sqp = ph.pool(1, [128, 8, 512], BF16)
        rsp = ph.pool(2, [128, 512])
        tp = ph.pool(2, [128, 512])
        yp = ph.pool(2, [128, 8, 512], BF16)
        stp = ph.pool(2, [128, 4, 512], BF16)
        banks = ph.psum(5)
        pso = ph.psum(3)
        fm3 = lambda t: t.rearrange("(c p) t -> p c t", p=128)
        for (t0, T, isctx) in TILES:
            if isctx and last:
                continue
            of, bof = ofp.next()
            ob, bob = obp.next()
            z, bz = zp.next()
            S.dma("sp", of[:, :, :T], fm3(self.ofw)[:, :, t0:t0 + T], writes=[bof])
            S.dma("sp", ob[:, :, :T], fm3(self.obw)[:, :, t0:t0 + T], writes=[bob])
            S.dma("sp", z[:, :, :T], fm3(self.zs)[:, :, t0:t0 + T], writes=[bz])
            self.tt("pool", of[:, :, :T], of[:, :, :T], ob[:, :, :T], ALU.add, [bof, bob], [bof])
            sq, bsq = sqp.next()
            self.act(sq[:, :, :T], of[:, :, :T], AF.Square, [bof], [bsq])
            y, by = yp.next()
            for c in range(8):
                ps, bps = pso.next()
                self.mm(ps[:, :T], self.ones_b[:, 1, :], sq[:, c, :T], True, True, [bsq] + cb, [bps])
                rs, brs = rsp.next()
                self.rsqrt_(rs[:, :T], ps[:, :T], [bps] + cb, [brs], self.eps_t)
                t, bt = tp.next()
                self.tt("dve", t[:, :T], of[:, c, :T], rs[:, :T], ALU.mult, [bof, brs], [bt])
                self.stt(y[:, c, :T], t[:, :T], self.vcol(l, V_DNG, 0), z[:, c, :T], ALU.mult, ALU.mult,
                         [bt, bz] + cb, [by])
            self.proj_store(y, by, T, w, bw, banks, stp, self.brB, t0)
        ph.close()

    def phase3c(self, l):
        S = self.S
        ph = Phase(self, f"p3c_{l}")
        cb = [self.cbuf]
        cm = self.cm32
        I64b = self.cmb[0:64, 0, 0:64]
        psSm, bpsSm = ph.psum(1).next()
        psT, bpsT = ph.psum(1, BF16, 1024).next()
        dks = self.dk.rearrange("(c p) t -> p c t", p=128)
        dqs = self.dq.rearrange("(c p) t -> p c t", p=128)

        def chain(d):
            U = cm[0:64, CM_UF if d == 0 else CM_UB, 0:64]
            negU = cm[0:64, CM_NUF if d == 0 else CM_NUB, 0:64]
            Mneg = cm[0:64, CM_MNF if d == 0 else CM_MNB, 0:64]
            MnegT = cm[0:64, CM_MNB if d == 0 else CM_MNF, 0:64]
            Ms = cm[0:64, CM_MSF if d == 0 else CM_MSB, 0:64]
            lastc = 63 if d == 0 else 0
            odst = self.ofw if d == 0 else self.obw
            rr = ph.psum(2)
            psX, bpsX = ph.psum(1).next()
            kTbp = ph.pool(2, [128, 8, 256], BF16)
            qTbp = ph.pool(2, [128, 8, 256], BF16)
            ktbp = ph.pool(2, [64, 4, D], BF16)
            vtbp = ph.pool(2, [64, 4, D], BF16)
            bgbp = ph.pool(2, [64, 4, 32])
            f3 = [64, 8, 64]
            Gm, bGm = ph.sb(f3)
            gB, bgB = ph.sb(f3)
            Dm, bDm = ph.sb(f3)
            DT, bDT = ph.sb(f3)
            Eg, bEg = ph.sb([128, 8, 64])
            sm, bsm = ph.sb([128, 16])
            ed, bed = ph.sb([64, 8])
            be, bbe = ph.sb([64, 8])
            L0, bL0 = ph.sb(f3)
            Lp = ph.pool(2, f3, BF16)
            Np = ph.pool(2, f3, BF16)
            ImN, bImN = ph.sb(f3, BF16)
            Xbp = ph.pool(2, f3, BF16)
            Aqk, bAqk = ph.sb(f3, BF16)
            M1T, bM1T = ph.sb(f3, BF16)
            M2T, bM2T = ph.sb(f3, BF16)
            u, bu = ph.sb([64, 8, 128])
            wT, bwT = ph.sb([128, 8, 64], BF16)
            kdec, bkdec = ph.sb([64, 8, 128], BF16)
            qdT, bqdT = ph.sb([128, 8, 64], BF16)
            vnew, bvnew = ph.sb([64, 8, 128], BF16)
            ost, bost = ph.sb([128, 8, 64])
            St, bSt = ph.sb([128, 8, 128])
            Sb, bSb = ph.sb([128, 8, 128], BF16)
            self.ms("pool", St[:], 0.0, [bSt])
            self.ms("pool", Sb[:], 0.0, [bSb])
            order = list(range(4)) + list(range(4, NT // 64)) if d == 0 else \
                list(range(3, -1, -1)) + list(range(NT // 64 - 1, 3, -1))
            curb = None
            for n in order:
                b, nn = n // 4, n % 4
                if b != curb:
                    curb = b
                    kTb, bkTb = kTbp.next()
                    qTb, bqTb = qTbp.next()
                    ktb, bktb = ktbp.next()
                    vtb, bvtb = vtbp.next()
                    bgb, bbgb = bgbp.next()
                    tk = slice(b * 256, (b + 1) * 256)
                    S.dma("sp", kTb[:], dks[:, :, tk], writes=[bkTb])
                    S.dma("sp", qTb[:], dqs[:, :, tk], writes=[bqTb])
                    S.dma("sp", ktb[:], self.dkt[tk, :].rearrange("(n p) f -> p n f", p=64), writes=[bktb])
                    S.dma("sp", vtb[:], self.dvt[tk, :].rearrange("(n p) f -> p n f", p=64), writes=[bvtb])
                    S.dma("sp", bgb[:], self.bg[tk, :].rearrange("(n p) c -> p n c", p=64), writes=[bbgb])
                kT_c = kTb[:, :, nn * 64:(nn + 1) * 64]
                qT_c = qTb[:, :, nn * 64:(nn + 1) * 64]
                kt_c = ktb[:, nn, :].rearrange("p (h f) -> p h f", f=128)
                vt_c = vtb[:, nn, :].rearrange("p (h f) -> p h f", f=128)
                beta = bgb[:, nn, 8 * d:8 * d + 8]
                g = bgb[:, nn, 16 + 8 * d:24 + 8 * d]
                flat = lambda t: t.rearrange("p h f -> p (h f)")
                self.tt("pool", Gm[:], bc(U, f3, 1), bc(g, f3, 2), ALU.mult, cb + [bbgb], [bGm])
                self.cp("pool", gB[:], bc(g, f3, 2), [bbgb], [bgB])
                psR, bpsR = rr.next()
                psE, bpsE = rr.next()
                self.mm(psR[0:64, :], self.ones_f[0:64, 0:64], flat(Gm[:]), True, False, cb + [bGm], [bpsR])
                self.mm(psR[0:64, :], negU, flat(gB[:]), False, True, cb + [bgB], [bpsR])
                self.mm(psE[:, :], self.ones_f[0:64, :], flat(Gm[:]), True, True, cb + [bGm], [bpsE])
                self.mm(psSm[:, 0:8], self.ones_f[0:64, :], g, True, True, cb + [bbgb], [bpsSm])
                self.mm(psSm[0:64, 8:16], U, g, True, True, cb + [bbgb], [bpsSm])
                psR3 = psR[0:64, :].rearrange("p (h f) -> p h f", f=64)
                self.tt("dve", Dm[:], bc(Mneg, f3, 1), psR3, ALU.subtract, cb + [bpsR], [bDm])
                self.act(Dm[:], Dm[:], AF.Exp, [bDm], [bDm])
                self.tt("dve", DT[:], psR3, bc(MnegT, f3, 1), ALU.add, cb + [bpsR], [bDT])
                self.act(DT[:], DT[:], AF.Exp, [bDT], [bDT])
                self.act(flat(Eg[:]), psE[:, :], AF.Exp, [bpsE], [bEg])
                self.act(sm[:], psSm[:, 0:16], AF.Exp, [bpsSm], [bsm])
                self.act(ed[:], psR3[:, :, lastc], AF.Exp, [bpsR], [bed])
                self.tt("dve", be[:], beta, sm[0:64, 8:16], ALU.mult, [bbgb, bsm], [bbe])
                yield
                psKK, bpsKK = rr.next()
                psQK, bpsQK = rr.next()
                for h in range(8):
                    self.mm(psKK[0:64, h * 64:(h + 1) * 64], kT_c[:, h, :], kT_c[:, h, :], True, True, [bkTb], [bpsKK])
                for h in range(8):
                    self.mm(psQK[0:64, h * 64:(h + 1) * 64], kT_c[:, h, :], qT_c[:, h, :], True, True, [bkTb, bqTb], [bpsQK])
                self.tt("dve", flat(L0[:]), psKK[0:64, :], flat(Dm[:]), ALU.mult, [bpsKK, bDm], [bL0])
                self.tt("dve", L0[:], L0[:], bc(Ms, f3, 1), ALU.mult, [bL0] + cb, [bL0])
                Lc, bLc = Lp.next()
                self.tt("dve", Lc[:], L0[:], bc(beta, f3, 2), ALU.mult, [bL0, bbgb], [bLc])
                self.tt("dve", flat(Aqk[:]), psQK[0:64, :], flat(DT[:]), ALU.mult, [bpsQK, bDT], [bAqk])
                for h in range(8):
                    self.tr(psT[0:64, h * 64:(h + 1) * 64], Lc[:, h, :], I64b, [bLc] + cb, [bpsT])
                Nc, bNc = Np.next()
                self.cp("act", flat(Nc[:]), psT[0:64, 0:512], [bpsT], [bNc])
                self.tt("pool", ImN[:], bc(I64b, f3, 1), Nc[:], ALU.subtract, cb + [bNc], [bImN])
                yield
                self.mm(psX[0:64, :], I64b, flat(ImN[:]), True, True, cb + [bImN], [bpsX])
                Xb, bXb = Xbp.next()
                self.cp("act", flat(Xb[:]), psX[0:64, :], [bpsX], [bXb])
                for k in range(1, 6):
                    psL, bpsL = rr.next()
                    for h in range(8):
                        self.mm(psL[0:64, h * 64:(h + 1) * 64], Nc[:, h, :], Lc[:, h, :], True, True, [bNc, bLc], [bpsL])
                    if k < 5:
                        psN, bpsN = rr.next()
                        for h in range(8):
                            self.mm(psN[0:64, h * 64:(h + 1) * 64], Lc[:, h, :], Nc[:, h, :], True, True, [bNc, bLc], [bpsN])
                    Lc, bLc = Lp.next()
                    self.cp("act", flat(Lc[:]), psL[0:64, :], [bpsL], [bLc])
                    if k < 5:
                        Nc, bNc = Np.next()
                        self.cp("dve", flat(Nc[:]), psN[0:64, :], [bpsN], [bNc])
                    for h in range(8):
                        self.mm(psX[0:64, h * 64:(h + 1) * 64], Lc[:, h, :], Xb[:, h, :], False, True, [bLc, bXb], [bpsX])
                    if k < 5:
                        Xb, bXb = Xbp.next()
                        self.cp("act", flat(Xb[:]), psX[0:64, :], [bpsX], [bXb])
                    yield
                psX3 = psX[0:64, :].rearrange("p (h f) -> p h f", f=64)
                self.tt("dve", M1T[:], psX3, bc(beta, f3, 2), ALU.mult, [bpsX, bbgb], [bM1T])
                self.tt("dve", M2T[:], psX3, bc(be[:], f3, 2), ALU.mult, [bpsX, bbe], [bM2T])
                pu = [rr.next(), rr.next()]
                for h in range(8):
                    p_, bp_ = pu[h // 4]
                    self.mm(p_[0:64, (h % 4) * 128:(h % 4 + 1) * 128], M1T[:, h, :], vt_c[:, h, :], True, True,
                            [bM1T, bvtb], [bp_])
                for i in range(2):
                    self.cp("act", flat(u[:, i * 4:(i + 1) * 4, :]), pu[i][0][0:64, :], [pu[i][1]], [bu])
                psW, bpsW = rr.next()
                for h in range(8):
                    self.mm(psW[:, h * 64:(h + 1) * 64], kt_c[:, h, :], M2T[:, h, :], True, True, [bktb, bM2T], [bpsW])
                self.cp("dve", flat(wT[:]), psW[:, :], [bpsW], [bwT])
                self.tt("pool", kdec[:], kt_c, bc(ed[:], [64, 8, 128], 2), ALU.mult, [bktb, bed], [bkdec])
                self.tt("pool", qdT[:], qT_c, Eg[:], ALU.mult, [bqTb, bEg], [bqdT])
                yield
                pv = [rr.next(), rr.next()]
                for h in range(8):
                    p_, bp_ = pv[h // 4]
                    self.mm(p_[0:64, (h % 4) * 128:(h % 4 + 1) * 128], wT[:, h, :], Sb[:, h, :], True, True,
                            [bwT, bSb], [bp_])
                for i in range(2):
                    self.tt("dve", flat(vnew[:, i * 4:(i + 1) * 4, :]), flat(u[:, i * 4:(i + 1) * 4, :]), pv[i][0][0:64, :],
                            ALU.subtract, [bu, pv[i][1]], [bvnew])
                psO, bpsO = rr.next()
                for h in range(8):
                    self.mm(psO[:, h * 64:(h + 1) * 64], Sb[:, h, :], qdT[:, h, :], True, False, [bSb, bqdT], [bpsO])
                    self.mm(psO[:, h * 64:(h + 1) * 64], vnew[:, h, :], Aqk[:, h, :], False, True, [bvnew, bAqk], [bpsO])
                self.cp("act", flat(ost[:]), psO[:, :], [bpsO], [bost])
                S.dma("pool", odst[:, n * 64:(n + 1) * 64].rearrange("(h p) t -> p h t", p=128), ost[:], reads=[bost])
                pS = [rr.next(), rr.next()]
                for h in range(8):
                    p_, bp_ = pS[h // 4]
                    self.mm(p_[:, (h % 4) * 128:(h % 4 + 1) * 128], kdec[:, h, :], vnew[:, h, :], True, True,
                            [bkdec, bvnew], [bp_])
                self.tt("dve", St[:], St[:], bc(sm[:, 0:8], [128, 8, 128], 2), ALU.mult, [bSt, bsm], [bSt])
                for i in range(2):
                    self.tt("dve", flat(St[:, i * 4:(i + 1) * 4, :]), flat(St[:, i * 4:(i + 1) * 4, :]), pS[i][0][:, :],
                            ALU.add, [bSt, pS[i][1]], [bSt])
                self.cp("act", Sb[:], St[:], [bSt], [bSb])
                yield

        gens = [chain(0), chain(1)]
        alive = [True, True]
        for _ in range(4):
            next(gens[0])
        while any(alive):
            for i, gnr in enumerate(gens):
                if alive[i]:
                    try:
                        next(gnr)
                    except StopIteration:
                        alive[i] = False
        ph.close()


def host_consts():
    f32 = np.float32
    t = np.arange(SEQ)
    inv = (np.float32(10000.0) ** (-np.arange(0, 64, 2, dtype=f32) / f32(64))).astype(f32)
    ang_r = ((t // GRID_W).astype(f32)[:, None] * inv).astype(f32)
    ang_c = ((t % GRID_W).astype(f32)[:, None] * inv).astype(f32)
    C = np.zeros((128, SEQ), f32)
    Sg = np.zeros((128, SEQ), f32)
    for f in range(128):
        ang = ang_r if f < 64 else ang_c
        fi = f % 32
        C[f] = np.cos(ang[:, fi])
        Sg[f] = np.sin(ang[:, fi]) * (-1.0 if (f % 64) < 32 else 1.0)
    cm = np.zeros((128, NCM, 128), f32)
    cm[:, CM_ID, :] = np.eye(128, dtype=f32)
    for m in range(128):
        partner = m + 32 if (m % 64) < 32 else m - 32
        cm[partner, CM_PERM, m] = 1.0
    i = np.arange(64)
    le = (i[:, None] <= i[None, :]).astype(f32)
    ge = (i[:, None] >= i[None, :]).astype(f32)
    cm[:64, CM_UF, :64] = le
    cm[:64, CM_UB, :64] = ge
    cm[:64, CM_MNF, :64] = np.where(i[:, None] >= i[None, :], 0.0, NEG)
    cm[:64, CM_MNB, :64] = np.where(i[:, None] <= i[None, :], 0.0, NEG)
    cm[:64, CM_MSF, :64] = (i[:, None] > i[None, :]).astype(f32)
    cm[:64, CM_MSB, :64] = (i[:, None] < i[None, :]).astype(f32)
    cm[:64, CM_NUF, :64] = -le
    cm[:64, CM_NUB, :64] = -ge
    return dict(ropeC=C, ropeS=Sg, cmat=cm)


def fm(v, nchunk):
    sh = v.shape[:-1]
    return np.ascontiguousarray(np.moveaxis(v.reshape(sh + (nchunk, 128)), -1, -2))


def host_shared(inp):
    f32 = np.float32
    out = dict(host_consts())
    for n in ("w_mod", "w_in", "w_conv_out", "w_dn_out", "w_na_out", "w_out", "w_gu", "w_down"):
        out[n] = np.ascontiguousarray(inp[n], dtype=f32)
    vecs = np.zeros((DEPTH, 128, NVEC), f32)
    vecs[:, :, V_BMOD:V_BMOD + 48] = fm(inp["b_mod"], 48)
    vecs[:, :, V_N1:V_N1 + 8] = fm(inp["norm1_g"], 8)
    vecs[:, :, V_N2:V_N2 + 8] = fm(inp["norm2_g"], 8)
    vecs[:, :, V_CDB:V_CDB + 8] = fm(inp["conv_db"], 8)
    vecs[:, :, V_LNG:V_LNG + 8] = fm(inp["conv_ln_g"], 8)
    vecs[:, :, V_LNB:V_LNB + 8] = fm(inp["conv_ln_b"], 8)
    vecs[:, :, V_DNG] = inp["dn_norm_g"]
    out["vecs"] = vecs
    out["conv_dw"] = np.ascontiguousarray(np.transpose(inp["conv_dw"].reshape(DEPTH, 31, 8, 128), (0, 3, 2, 1)), dtype=f32)
    out["dn_conv"] = np.ascontiguousarray(np.transpose(inp["dn_conv"].reshape(DEPTH, 5, 24, 128), (0, 3, 2, 1)), dtype=f32)
    dnp = np.concatenate([inp["dn_a_log"].reshape(DEPTH, 16), inp["dn_dt_bias"].reshape(DEPTH, 16)], -1)
    out["dnp"] = np.ascontiguousarray(np.broadcast_to(dnp[:, None, :], (DEPTH, 128, 32)), dtype=f32)
    rpb = inp["na_rpb"]
    kc = np.arange(64)[:, None]
    qc = np.arange(64)[None, :]
    c0 = np.clip(qc - 8, 0, 48)
    valid = (kc >= c0) & (kc < c0 + 16)
    idx = np.clip(kc - qc + 15, 0, 30)
    tb = rpb[:, :, ::-1, :][:, :, :, idx]
    tb = np.where(valid[None, None, None], tb, f32(NEG))
    out["rpbT"] = np.ascontiguousarray(np.transpose(tb, (0, 3, 1, 2, 4)), dtype=f32)
    out["fin_g"] = fm(inp["final_norm_g"], 8).astype(f32)
    return out


def host_core(inp, b):
    f32 = np.float32
    xT0 = np.concatenate([inp["ctx"][b].T, inp["x"][b].T], axis=1)
    cvec = np.stack([fm(inp["c"][b], 8), fm(inp["c_ctx"], 8)], axis=-1)
    return dict(xT0=np.ascontiguousarray(xT0, dtype=f32), cvec=np.ascontiguousarray(cvec, dtype=f32))


_NC_CACHE = {}


def kernel(**inputs):
    inp = {k: np.asarray(v) for k, v in inputs.items()}
    if "nc" not in _NC_CACHE:
        _NC_CACHE["nc"] = K().build()
    nc = _NC_CACHE["nc"]
    shared = host_shared(inp)
    n = 8
    in_maps = []
    for core in range(n):
        m = dict(shared)
        m.update(host_core(inp, core % 4))
        in_maps.append(m)
    res = run_bass_kernel_spmd(nc, in_maps, core_ids=list(range(n)))
    out = np.stack([np.ascontiguousarray(res.results[b]["yT"].T) for b in range(4)], axis=0)
    return out.astype(np.float32)
```

```python
import contextlib
import numpy as np
import concourse.bass as bass
import concourse.mybir as mybir
from concourse.bass_utils import run_bass_kernel_spmd

F32 = mybir.dt.float32
BF16 = mybir.dt.bfloat16
ALU = mybir.AluOpType
AF = mybir.ActivationFunctionType

EPOCH = 12000
NDMA_SEM = 12

D = 1024
SEQ = 8192
LC = 256
NT = SEQ + LC
DEPTH = 2
IN_DIM = 12320
DFF = 2816
GRID_W = 64
ROWS = SEQ // GRID_W
EPS = 1e-6
OFF = dict(conv=0, dn_q=2048, dn_k=3072, dn_v=4096, dn_z=5120, beta=6144, alpha=6160,
           na_q=6176, na_k=7200, na_v=8224, gate=9248)
TILES = [(0, 256, True)] + [(LC + 512 * i, 512, False) for i in range(SEQ // 512)]
NEG = -30000.0


class Buf:
    __slots__ = ("name", "w", "r")

    def __init__(self, name=""):
        self.name = name
        self.w = None
        self.r = []


class Sched:
    ENG = ("pe", "act", "dve", "pool", "sp")

    def __init__(self, nc, stack):
        self.nc = nc
        self.stack = stack
        self.ops = {e: [] for e in self.ENG}
        self.count = {e: 0 for e in self.ENG}
        self.esems = {e: [] for e in self.ENG}
        self.waited = {e: {} for e in self.ENG}
        self.dsems = {}
        self.dcount = {}
        self.dnext = {}
        for q in ("sp", "act", "pool"):
            self.dsems[q] = [stack.enter_context(nc.semaphore(f"d_{q}_{i}")) for i in range(NDMA_SEM)]
            self.dcount[q] = [0] * NDMA_SEM
            self.dnext[q] = 0
        self.same_engine_sync = {"pe": False, "act": True, "dve": True, "pool": True, "sp": False}

    def _esem(self, eng, idx):
        ep = idx // EPOCH
        while len(self.esems[eng]) <= ep:
            self.esems[eng].append(self.stack.enter_context(
                self.nc.semaphore(f"e_{eng}_{len(self.esems[eng])}")))
        return self.esems[eng][ep], idx % EPOCH + 1

    def _need(self, eng, tok, waits, force=False):
        if tok is None:
            return
        if tok[0] == "e":
            _, src, idx = tok
            if src == eng and not self.same_engine_sync[eng] and not force:
                return
            key = ("e", src)
            if self.waited[eng].get(key, -1) >= idx:
                return
            self.waited[eng][key] = idx
            waits.append(self._esem(src, idx))
        else:
            _, q, si, val = tok
            key = ("d", q, si)
            if self.waited[eng].get(key, -1) >= val:
                return
            self.waited[eng][key] = val
            waits.append((self.dsems[q][si], val))

    def _deps(self, eng, reads, writes):
        waits = []
        for b in reads:
            self._need(eng, b.w, waits)
        for b in writes:
            self._need(eng, b.w, waits)
            for t in b.r:
                self._need(eng, t, waits)
        return waits

    def _commit(self, tok, reads, writes):
        for b in reads:
            b.r.append(tok)
        for b in writes:
            b.w = tok
            b.r = []

    def op(self, eng, fn, reads=(), writes=()):
        waits = self._deps(eng, reads, writes)
        idx = self.count[eng]
        self.count[eng] += 1
        sem, _ = self._esem(eng, idx)
        self.ops[eng].append((waits, fn, (sem, 1)))
        self._commit(("e", eng, idx), reads, writes)

    def dma(self, q, out, in_, reads=(), writes=()):
        waits = self._deps(q, reads, writes)
        si = self.dnext[q]
        self.dnext[q] = (si + 1) % NDMA_SEM
        prev = self.dcount[q][si]
        if prev > 0:
            self._need(q, ("d", q, si, prev), waits)
        val = prev + 16
        self.dcount[q][si] = val
        sem = self.dsems[q][si]
        self.ops[q].append((waits, lambda e, o=out, i=in_: e.dma_start(out=o, in_=i), (sem, 16)))
        tok = ("d", q, si, val)
        self._commit(tok, reads, writes)
        return tok

    def barrier(self):
        for e in self.ENG:
            waits = []
            for s in self.ENG:
                if self.count[s] > 0:
                    self._need(e, ("e", s, self.count[s] - 1), waits, force=True)
            for q in self.dsems:
                for si in range(NDMA_SEM):
                    if self.dcount[q][si] > 0:
                        self._need(e, ("d", q, si, self.dcount[q][si]), waits)
            self.ops[e].append((waits, None, None))

    def emit(self):
        nc = self.nc
        with nc.Block() as block:
            def run(engname):
                def body(e):
                    for waits, fn, inc in self.ops[engname]:
                        for s, v in waits:
                            e.wait_ge(s, v)
                        if fn is not None:
                            ins = fn(e)
                            ins.then_inc(inc[0], inc[1])
                return body
            block.tensor(run("pe"))
            block.scalar(run("act"))
            block.vector(run("dve"))
            block.gpsimd(run("pool"))
            block.sync(run("sp"))


class RR:
    def __init__(self, items):
        self.items = items
        self.i = 0

    def next(self):
        it = self.items[self.i]
        self.i = (self.i + 1) % len(self.items)
        return it


class Phase:
    def __init__(self, K, name):
        self.K = K
        self.name = name
        self.st = contextlib.ExitStack()
        self.n = 0

    def sb(self, shape, dt=F32):
        self.n += 1
        t = self.st.enter_context(self.K.nc.sbuf_tensor(f"{self.name}_{self.n}", list(shape), dt))
        return t, Buf(f"{self.name}_{self.n}")

    def pool(self, n, shape, dt=F32):
        return RR([self.sb(shape, dt) for _ in range(n)])

    def psum(self, n, dt=F32, cols=512):
        out = []
        for _ in range(n):
            self.n += 1
            t = self.st.enter_context(self.K.nc.psum_tensor(f"{self.name}_ps{self.n}", [128, cols], dt))
            out.append((t, Buf(f"{self.name}_ps{self.n}")))
        return RR(out)

    def close(self):
        self.K.S.barrier()
        self.st.close()


def bc(ap, shape, axis):
    return ap.unsqueeze(axis).to_broadcast(list(shape))


NVEC = 89
V_BMOD, V_N1, V_N2, V_CDB, V_LNG, V_LNB, V_DNG = 0, 48, 56, 64, 72, 80, 88
CM_ID, CM_PERM, CM_UF, CM_UB, CM_MNF, CM_MNB, CM_MSF, CM_MSB, CM_NUF, CM_NUB, CM_BDF, CM_OFF_F, CM_BDB, CM_OFF_B = range(14)
NCM = 14


class K:
    def __init__(self, debug=(), stop_after=None, layers=DEPTH):
        self.debug = set(debug)
        self.stop_after = stop_after
        self.layers = layers
        self.nc = bass.Bass("TRN2", target_bir_lowering=False)
        self.st = contextlib.ExitStack()
        self.S = Sched(self.nc, self.st)
        self.dram = {}

    def mm(self, out, lhsT, rhs, start, stop, reads, writes):
        self.S.op("pe", lambda e: e.matmul(out, lhsT=lhsT, rhs=rhs, start=start, stop=stop), reads, writes)

    def tr(self, out, in_, ident, reads, writes):
        self.S.op("pe", lambda e: e.transpose(out, in_, ident), reads, writes)

    def act(self, out, in_, func, reads, writes, bias=None, scale=None):
        kw = {}
        if bias is not None:
            kw["bias"] = bias
        if scale is not None:
            kw["scale"] = scale
        self.S.op("act", lambda e: e.activation(out=out, in_=in_, func=func, **kw), reads, writes)

    def tt(self, eng, out, in0, in1, op, reads, writes):
        self.S.op(eng, lambda e: e.tensor_tensor(out=out, in0=in0, in1=in1, op=op), reads, writes)

    def ts(self, eng, out, in0, s1, s2, op0, op1, reads, writes):
        if op1 is None:
            self.S.op(eng, lambda e: e.tensor_scalar(out=out, in0=in0, scalar1=s1, scalar2=None, op0=op0), reads, writes)
        else:
            self.S.op(eng, lambda e: e.tensor_scalar(out=out, in0=in0, scalar1=s1, scalar2=s2, op0=op0, op1=op1), reads, writes)

    def stt(self, out, in0, scalar, in1, op0, op1, reads, writes):
        self.S.op("dve", lambda e: e.scalar_tensor_tensor(out=out, in0=in0, scalar=scalar, in1=in1, op0=op0, op1=op1),
                  reads, writes)

    def cp(self, eng, out, in_, reads, writes):
        if eng == "act":
            self.act(out, in_, AF.Copy, reads, writes)
        else:
            self.S.op(eng, lambda e: e.tensor_copy(out=out, in_=in_), reads, writes)

    def ms(self, eng, out, val, writes):
        self.S.op(eng, lambda e: e.memset(out, val), (), writes)

    def rsqrt_(self, out, in_, reads, writes, eps_ap):
        self.act(out, in_, AF.Ln, reads, writes, bias=eps_ap)
        self.act(out, out, AF.Exp, writes, writes, scale=-0.5)

    def din(self, name, shape, dt=F32):
        t = self.nc.dram_tensor(name, list(shape), dt, kind="ExternalInput").ap()
        self.dram[name] = t
        return t

    def dscr(self, name, shape, dt):
        kind = "ExternalOutput" if name in self.debug else "Internal"
        t = self.nc.dram_tensor(name, list(shape), dt, kind=kind).ap()
        self.dram[name] = t
        return t

    def build(self):
        nc, S = self.nc, self.S
        g = self.din
        self.xT0 = g("xT0", [D, NT])
        self.cvec = g("cvec", [128, 8, 2])
        self.w_mod = g("w_mod", [DEPTH, D, 6 * D])
        self.w_in = g("w_in", [DEPTH, D, IN_DIM])
        self.w_sq = {n: g(n, [DEPTH, D, D]) for n in ("w_conv_out", "w_dn_out", "w_na_out", "w_out")}
        self.w_gu = g("w_gu", [DEPTH, D, 2 * DFF])
        self.w_down = g("w_down", [DEPTH, DFF, D])
        self.vecs = g("vecs", [DEPTH, 128, NVEC])
        self.conv_dw = g("conv_dw", [DEPTH, 128, 8, 31])
        self.dn_conv = g("dn_conv", [DEPTH, 128, 24, 5])
        self.dnp = g("dnp", [DEPTH, 128, 32])
        self.rpbT = g("rpbT", [DEPTH, 64, 16, 15, 64])
        self.fin_g = g("fin_g", [128, 8])
        self.ropeC = g("ropeC", [128, SEQ])
        self.ropeS = g("ropeS", [128, SEQ])
        self.cmat = g("cmat", [128, NCM, 128])
        self.yT = nc.dram_tensor("yT", [D, SEQ], F32, kind="ExternalOutput").ap()
        s = self.dscr
        self.wb_in = s("wb_in", [DEPTH, D, IN_DIM], BF16)
        self.wb_sq = {n: s("wb_" + n, [DEPTH, D, D], BF16) for n in self.w_sq}
        self.wb_gu = s("wb_gu", [DEPTH, D, 2 * DFF], BF16)
        self.wb_down = s("wb_down", [DEPTH, DFF, D], BF16)
        self.x1 = s("x1", [D, NT], F32)
        self.hconv = s("hconv", [D, NT], BF16)
        self.dnpre = s("dnpre", [3 * D, NT], BF16)
        self.zs = s("zs", [D, NT], BF16)
        self.bg = s("bg", [NT, 32], F32)
        self.naq = s("naq", [D, NT], BF16)
        self.nak = s("nak", [D, NT], BF16)
        self.nav = s("nav", [NT, D], BF16)
        self.gates = s("gates", [3 * D, NT], BF16)
        self.brA = s("brA", [D, NT], BF16)
        self.brB = s("brB", [D, NT], BF16)
        self.brC = s("brC", [D, NT], BF16)
        self.dq = s("dq", [D, NT], BF16)
        self.dk = s("dk", [D, NT], BF16)
        self.dkt = s("dkt", [NT, D], BF16)
        self.dvt = s("dvt", [NT, D], BF16)
        self.ofw = s("ofw", [D, NT], F32)
        self.obw = s("obw", [D, NT], F32)
        self.modv_d = s("modv_d", [128, 96], F32)

        self.consts()
        self.phase0()
        done = False
        for l in range(self.layers):
            last = (l == DEPTH - 1)
            xsrc = self.xT0 if l == 0 else self.x1
            steps = [("mod", lambda: self.phase_mod(l)),
                     ("p1", lambda: self.phase1(l, xsrc)),
                     ("p2", lambda: self.phase2(l, last)),
                     ("p3a", lambda: self.phase3a(l)),
                     ("p3c", lambda: self.phase3c(l)),
                     ("p3d", lambda: self.phase3d(l, last)),
                     ("p4", lambda: self.phase4(l, last)),
                     ("p5", lambda: self.phase5(l, xsrc, last))]
            for name, fn in steps:
                fn()
                if self.stop_after == (l, name):
                    done = True
                    break
            if done:
                break
        S.barrier()
        S.emit()
        self.st.close()
        return nc

    def consts(self):
        nc, S, st = self.nc, self.S, self.st
        sb = lambda name, shape, dt=F32: st.enter_context(nc.sbuf_tensor(name, list(shape), dt))
        self.cbuf = Buf("consts")
        cb = [self.cbuf]
        self.cm32 = sb("cm32", [128, NCM, 128])
        S.dma("sp", self.cm32[:], self.cmat, writes=cb)
        self.cmb = sb("cmb", [128, 2, 128], BF16)
        self.cp("dve", self.cmb[:], self.cm32[:, 0:2, :], cb, cb)
        self.ones_b = sb("ones_b", [128, 4, 128], BF16)
        for i, v in enumerate((1.0 / 1024, 1.0 / 128, 128.0, 1.0)):
            self.ms("pool", self.ones_b[:, i, :], v, cb)
        self.ones_f = sb("ones_f", [128, 128], F32)
        self.ms("pool", self.ones_f[:], 1.0, cb)
        self.cst = sb("cst", [128, 4], F32)
        for i, v in enumerate((EPS, 128 * EPS, 1.0, 0.0)):
            self.ms("pool", self.cst[:, i:i + 1], v, cb)
        self.eps_t = self.cst[:, 0:1]
        self.eps128_t = self.cst[:, 1:2]
        self.one_t = self.cst[:, 2:3]
        self.vec_t = sb("vec_t", [128, DEPTH, NVEC])
        S.dma("sp", self.vec_t[:], self.vecs.rearrange("l p n -> p l n"), writes=cb)
        self.fing_t = sb("fing_t", [128, 8])
        S.dma("sp", self.fing_t[:], self.fin_g, writes=cb)
        self.modv = sb("modv", [128, 48, 2])
        self.modA = sb("modA", [128, 2, 8, 2])
        self.mbuf = Buf("mod")
        S.barrier()

    def vcol(self, l, off, c):
        return self.vec_t[:, l, off + c:off + c + 1]

    def phase0(self):
        S = self.S
        for l in range(self.layers):
            pairs = [(self.w_in[l], self.wb_in[l], D), (self.w_gu[l], self.wb_gu[l], D),
                     (self.w_down[l], self.wb_down[l], DFF)]
            pairs += [(self.w_sq[n][l], self.wb_sq[n][l], D) for n in self.w_sq]
            for src, dst, rows in pairs:
                for r in range(0, rows, 128):
                    S.dma("pool", dst[r:r + 128, :], src[r:r + 128, :])
        S.barrier()

    def phase_mod(self, l):
        S = self.S
        ph = Phase(self, f"mod{l}")
        cb = [self.cbuf]
        cv, bcv = ph.sb([128, 8, 2])
        S.dma("sp", cv[:], self.cvec, writes=[bcv])
        self.act(cv[:], cv[:], AF.Silu, [bcv], [bcv])
        wp = ph.pool(2, [128, 8, 768])
        ps, bps = ph.psum(1).next()
        wsrc = self.w_mod[l].rearrange("(k p) n -> p k n", p=128)
        for gi in range(8):
            w, bw = wp.next()
            S.dma("sp", w[:], wsrc[:, :, gi * 768:(gi + 1) * 768], writes=[bw])
            for j in range(6):
                jj = gi * 6 + j
                for k in range(8):
                    self.mm(ps[:, jj * 2:jj * 2 + 2], w[:, k, j * 128:(j + 1) * 128], cv[:, k, :],
                            k == 0, k == 7, [bw, bcv], [bps])
        mb = [self.mbuf]
        self.tt("dve", self.modv[:], ps[:, 0:96].rearrange("p (j c) -> p j c", c=2),
                bc(self.vec_t[:, l, V_BMOD:V_BMOD + 48], [128, 48, 2], 2), ALU.add, [bps] + cb, mb)
        for n, (voff, sc) in enumerate(((V_N1, 8), (V_N2, 32))):
            self.ts("dve", self.modA[:, n], self.modv[:, sc:sc + 8, :], 1.0, None, ALU.add, None, mb, mb)
            self.tt("dve", self.modA[:, n], self.modA[:, n],
                    bc(self.vec_t[:, l, voff:voff + 8], [128, 8, 2], 2), ALU.mult, mb + cb, mb)
        if "modv_d" in self.debug:
            S.dma("sp", self.modv_d, self.modv[:].rearrange("p j c -> p (j c)"), reads=mb)
        ph.close()

    def norm_mod(self, ph, x, bx, T, n, col, xn, bxn, sq, bsq, h, bh, rs, brs, psb):
        mb = [self.mbuf]
        cb = [self.cbuf]
        ps, bps = psb
        self.act(sq[:, :, :T], x[:, :, :T], AF.Square, [bx], [bsq])
        for c in range(8):
            self.mm(ps[:, :T], self.ones_b[:, 0, :], sq[:, c, :T], c == 0, c == 7, [bsq] + cb, [bps])
        self.rsqrt_(rs[:, :T], ps[:, :T], [bps] + cb, [brs], self.eps_t)
        self.tt("dve", xn[:, :, :T], x[:, :, :T], bc(rs[:, :T], [128, 8, T], 1), ALU.mult, [bx, brs], [bxn])
        shift = 0 if n == 0 else 24
        for c in range(8):
            if c % 2 == 0:
                self.act(h[:, c, :T], xn[:, c, :T], AF.Identity, [bxn] + mb, [bh],
                         bias=self.modv[:, shift + c, col:col + 1], scale=self.modA[:, n, c, col:col + 1])
            else:
                self.ts("dve", h[:, c, :T], xn[:, c, :T], self.modA[:, n, c, col:col + 1],
                        self.modv[:, shift + c, col:col + 1], ALU.mult, ALU.add, [bxn] + mb, [bh])

    def phase1(self, l, xsrc):
        S = self.S
        ph = Phase(self, f"p1_{l}")
        cb = [self.cbuf]
        xp = ph.pool(2, [128, 8, 512])
        xnp = ph.pool(1, [128, 8, 512])
        sqp = ph.pool(1, [128, 8, 512], BF16)
        hp = ph.pool(2, [128, 8, 512], BF16)
        rsp = ph.pool(1, [128, 512])
        wp = ph.pool(3, [128, 8, 512], BF16)
        stp = ph.pool(3, [128, 4, 512], BF16)
        tmpp = ph.pool(2, [128, 512])
        bgp = ph.pool(2, [128, 4, 32])
        banks = ph.psum(7)
        psn = ph.psum(1).next()
        dnp_t, bdnp = ph.sb([128, 32])
        S.dma("sp", dnp_t[:], self.dnp[l], writes=[bdnp])
        self.act(dnp_t[:, 0:16], dnp_t[:, 0:16], AF.Exp, [bdnp], [bdnp])
        self.ts("dve", dnp_t[:, 0:16], dnp_t[:, 0:16], -1.0, None, ALU.mult, None, [bdnp], [bdnp])

        wsrc = self.wb_in[l].rearrange("(k p) n -> p k n", p=128)
        xs = xsrc.rearrange("(c p) t -> p c t", p=128)
        groups = []
        for j in range(0, 8, 2):
            groups.append(("glu", [(j * 128, 256), (1024 + j * 128, 256)], self.hconv, j * 128))
        for i in range(6):
            groups.append(("copy", [(2048 + i * 512, 512)], self.dnpre, i * 512))
        for i in range(2):
            groups.append(("silu", [(OFF["dn_z"] + i * 512, 512)], self.zs, i * 512))
        groups.append(("ba", [(OFF["beta"], 32)], self.bg, 0))
        for i in range(2):
            groups.append(("copy", [(OFF["na_q"] + i * 512, 512)], self.naq, i * 512))
        for i in range(2):
            groups.append(("copy", [(OFF["na_k"] + i * 512, 512)], self.nak, i * 512))
        for i in range(2):
            groups.append(("tok", [(OFF["na_v"] + i * 512, 512)], self.nav, i * 512))
        for i in range(6):
            groups.append(("sigmoid", [(OFF["gate"] + i * 512, 512)], self.gates, i * 512))

        for (t0, T, isctx) in TILES:
            col = 1 if isctx else 0
            x, bx = xp.next()
            S.dma("sp", x[:, :, :T], xs[:, :, t0:t0 + T], writes=[bx])
            xn, bxn = xnp.next()
            sq, bsq = sqp.next()
            h, bh = hp.next()
            rs, brs = rsp.next()
            self.norm_mod(ph, x, bx, T, 0, col, xn, bxn, sq, bsq, h, bh, rs, brs, psn)
            nsub = T // 128
            for gi, (kind, ranges, dst, drow) in enumerate(groups):
                w, bw = wp.next()
                o = 0
                for (c0, wd) in ranges:
                    S.dma("sp", w[:, :, o:o + wd], wsrc[:, :, c0:c0 + wd], writes=[bw])
                    o += wd
                if kind in ("glu", "copy", "silu", "sigmoid"):
                    pss = []
                    for j in range(4):
                        ps, bps = banks.next()
                        for k in range(8):
                            self.mm(ps[:, :T], w[:, k, j * 128:(j + 1) * 128], h[:, k, :T], k == 0, k == 7,
                                    [bw, bh], [bps])
                        pss.append((ps, bps))
                    stg, bst = stp.next()
                    if kind == "glu":
                        for j in range(2):
                            tmp, btmp = tmpp.next()
                            self.act(tmp[:, :T], pss[2 + j][0][:, :T], AF.Sigmoid, [pss[2 + j][1]], [btmp])
                            self.tt("dve", stg[:, j, :T], pss[j][0][:, :T], tmp[:, :T], ALU.mult,
                                    [pss[j][1], btmp], [bst])
                        nout = 2
                    else:
                        for j in range(4):
                            if kind == "copy":
                                self.cp("dve", stg[:, j, :T], pss[j][0][:, :T], [pss[j][1]], [bst])
                            else:
                                self.act(stg[:, j, :T], pss[j][0][:, :T], AF.Silu if kind == "silu" else AF.Sigmoid,
                                         [pss[j][1]], [bst])
                        nout = 4
                    S.dma("pool", dst[drow:drow + nout * 128, t0:t0 + T].rearrange("(j p) t -> p j t", p=128),
                          stg[:, 0:nout, :T], reads=[bst])
                elif kind == "tok":
                    for s in range(nsub):
                        ps, bps = banks.next()
                        for k in range(8):
                            self.mm(ps[:, :], h[:, k, s * 128:(s + 1) * 128], w[:, k, :], k == 0, k == 7,
                                    [bw, bh], [bps])
                        stg, bst = stp.next()
                        self.cp("act" if s % 2 == 0 else "dve", stg[:, 0, :], ps[:, :], [bps], [bst])
                        S.dma("pool", dst[t0 + s * 128:t0 + (s + 1) * 128, drow:drow + 512], stg[:, 0, :], reads=[bst])
                else:
                    ps, bps = banks.next()
                    for s in range(nsub):
                        for k in range(8):
                            self.mm(ps[:, s * 32:(s + 1) * 32], h[:, k, s * 128:(s + 1) * 128], w[:, k, 0:32],
                                    k == 0, k == 7, [bw, bh], [bps])
                    bgt, bbg = bgp.next()
                    pv = ps[:, 0:nsub * 32].rearrange("p (s c) -> p s c", c=32)
                    self.act(bgt[:, :nsub, 0:16], pv[:, :, 0:16], AF.Sigmoid, [bps], [bbg])
                    self.tt("dve", bgt[:, :nsub, 16:32], pv[:, :, 16:32], bc(dnp_t[:, 16:32], [128, nsub, 16], 1),
                            ALU.add, [bps, bdnp], [bbg])
                    self.ts("dve", bgt[:, :nsub, 16:32], bgt[:, :nsub, 16:32], 60.0, None, ALU.min, None, [bbg], [bbg])
                    self.act(bgt[:, :nsub, 16:32], bgt[:, :nsub, 16:32], AF.Exp, [bbg], [bbg])
                    self.act(bgt[:, :nsub, 16:32], bgt[:, :nsub, 16:32], AF.Ln, [bbg] + cb, [bbg], bias=self.one_t)
                    self.tt("dve", bgt[:, :nsub, 16:32], bgt[:, :nsub, 16:32], bc(dnp_t[:, 0:16], [128, nsub, 16], 1),
                            ALU.mult, [bbg, bdnp], [bbg])
                    S.dma("pool", dst[t0:t0 + T, :].rearrange("(s p) c -> p s c", p=128), bgt[:, :nsub, :], reads=[bbg])
        ph.close()

    def load_w_sq(self, ph, name, l):
        w, bw = ph.sb([128, 8, D], BF16)
        self.S.dma("sp", w[:], self.wb_sq[name][l].rearrange("(k p) n -> p k n", p=128), writes=[bw])
        return w, bw

    def proj_store(self, y, by, T, w, bw, banks, stp, dst, t0):
        for half in range(2):
            stg, bst = stp.next()
            for j in range(4):
                jj = half * 4 + j
                ps, bps = banks.next()
                for k in range(8):
                    self.mm(ps[:, :T], w[:, k, jj * 128:(jj + 1) * 128], y[:, k, :T], k == 0, k == 7, [bw, by], [bps])
                self.cp("act" if j % 2 == 0 else "dve", stg[:, j, :T], ps[:, :T], [bps], [bst])
            self.S.dma("pool", dst[half * 512:(half + 1) * 512, t0:t0 + T].rearrange("(j p) t -> p j t", p=128),
                       stg[:, :, :T], reads=[bst])

    def phase2(self, l, last):
        S = self.S
        ph = Phase(self, f"p2_{l}")
        cb = [self.cbuf]
        w, bw = self.load_w_sq(ph, "w_conv_out", l)
        dw, bdw = ph.sb([128, 8, 31])
        S.dma("sp", dw[:], self.conv_dw[l], writes=[bdw])
        diag, bdiag = ph.sb([128, 248, 128], BF16)
        for c in range(8):
            for k in range(31):
                self.ts("dve" if (c * 31 + k) % 2 == 0 else "pool", diag[:, c * 31 + k, :], self.cm32[:, CM_ID, :],
                        dw[:, c, k:k + 1], None, ALU.mult, None, [bdw] + cb, [bdiag])
        hinp = ph.pool(2, [128, 8, 512 + 30], BF16)
        accp = ph.pool(2, [128, 8, 512])
        hbp = ph.pool(1, [128, 8, 512], BF16)
        sqp = ph.pool(1, [128, 8, 512], BF16)
        yp = ph.pool(2, [128, 8, 512], BF16)
        stp = ph.pool(2, [128, 4, 512], BF16)
        mp = ph.pool(1, [128, 512])
        m2p = ph.pool(1, [128, 512])
        rsp = ph.pool(1, [128, 512])
        banks = ph.psum(6)
        pstat = ph.psum(2)
        hsrc = self.hconv.rearrange("(c p) t -> p c t", p=128)
        for (t0, T, isctx) in TILES:
            if isctx and last:
                continue
            s0, s1 = (0, LC) if isctx else (LC, NT)
            lo, hi = max(t0 - 15, s0), min(t0 + T + 15, s1)
            hin, bhin = hinp.next()
            if lo > t0 - 15:
                self.ms("pool", hin[:, :, 0:15], 0.0, [bhin])
            if hi < t0 + T + 15:
                self.ms("pool", hin[:, :, T + 15:T + 30], 0.0, [bhin])
            S.dma("sp", hin[:, :, lo - (t0 - 15):hi - (t0 - 15)], hsrc[:, :, lo:hi], writes=[bhin])
            acc, bacc = accp.next()
            for c in range(8):
                ps, bps = banks.next()
                for k in range(31):
                    self.mm(ps[:, :T], diag[:, c * 31 + k, :], hin[:, c, k:k + T], k == 0, k == 30, [bhin, bdiag], [bps])
                if c % 2 == 0:
                    self.act(acc[:, c, :T], ps[:, :T], AF.Identity, [bps] + cb, [bacc], bias=self.vcol(l, V_CDB, c))
                else:
                    self.ts("dve", acc[:, c, :T], ps[:, :T], self.vcol(l, V_CDB, c), None, ALU.add, None, [bps] + cb, [bacc])
            hb, bhb = hbp.next()
            sq, bsq = sqp.next()
            self.cp("act", hb[:, :, :T], acc[:, :, :T], [bacc], [bhb])
            self.act(sq[:, :, :T], acc[:, :, :T], AF.Square, [bacc], [bsq])
            pm, bpm = pstat.next()
            pq, bpq = pstat.next()
            for c in range(8):
                self.mm(pm[:, :T], self.ones_b[:, 0, :], hb[:, c, :T], c == 0, c == 7, [bhb] + cb, [bpm])
            for c in range(8):
                self.mm(pq[:, :T], self.ones_b[:, 0, :], sq[:, c, :T], c == 0, c == 7, [bsq] + cb, [bpq])
            mean, bmean = mp.next()
            m2, bm2 = m2p.next()
            rs, brs = rsp.next()
            self.cp("act", mean[:, :T], pm[:, :T], [bpm], [bmean])
            self.tt("dve", m2[:, :T], mean[:, :T], mean[:, :T], ALU.mult, [bmean], [bm2])
            self.tt("dve", m2[:, :T], pq[:, :T], m2[:, :T], ALU.subtract, [bpq, bm2], [bm2])
            self.ts("dve", m2[:, :T], m2[:, :T], 0.0, None, ALU.max, None, [bm2], [bm2])
            self.rsqrt_(rs[:, :T], m2[:, :T], [bm2] + cb, [brs], self.eps_t)
            self.tt("dve", acc[:, :, :T], acc[:, :, :T], bc(mean[:, :T], [128, 8, T], 1), ALU.subtract,
                    [bacc, bmean], [bacc])
            self.tt("dve", acc[:, :, :T], acc[:, :, :T], bc(rs[:, :T], [128, 8, T], 1), ALU.mult, [bacc, brs], [bacc])
            y, by = yp.next()
            for c in range(8):
                self.act(y[:, c, :T], acc[:, c, :T], AF.Silu, [bacc] + cb, [by],
                         bias=self.vcol(l, V_LNB, c), scale=self.vcol(l, V_LNG, c))
            self.proj_store(y, by, T, w, bw, banks, stp, self.brA, t0)
        ph.close()

    def phase5(self, l, xsrc, last):
        S = self.S
        ph = Phase(self, f"p5_{l}")
        cb = [self.cbuf]
        mb = [self.mbuf]
        w, bw = self.load_w_sq(ph, "w_out", l)
        xp = ph.pool(1, [128, 8, 512])
        xnp = ph.pool(1, [128, 8, 512])
        sqp = ph.pool(1, [128, 8, 512], BF16)
        hp = ph.pool(1, [128, 8, 512], BF16)
        rsp = ph.pool(1, [128, 512])
        brp = ph.pool(3, [128, 8, 512], BF16)
        gp = ph.pool(1, [128, 24, 512], BF16)
        t1p = ph.pool(2, [128, 512])
        t2p = ph.pool(2, [128, 512])
        mgp = ph.pool(1, [128, 8, 512], BF16)
        wgp = ph.pool(2, [128, 8, 512], BF16)
        wdp = ph.pool(1, [128, 22, 512], BF16)
        fp = ph.pool(1, [128, 22, 512], BF16)
        tmpp = ph.pool(2, [128, 512])
        banks = ph.psum(7)
        psn = ph.psum(1).next()
        xs = xsrc.rearrange("(c p) t -> p c t", p=128)
        wgs = self.wb_gu[l].rearrange("(k p) n -> p k n", p=128)
        wds = self.wb_down[l].rearrange("(k p) n -> p k n", p=128)
        fm3 = lambda t: t.rearrange("(c p) t -> p c t", p=128)
        for (t0, T, isctx) in TILES:
            if isctx and last:
                continue
            col = 1 if isctx else 0
            x, bx = xp.next()
            S.dma("sp", x[:, :, :T], xs[:, :, t0:t0 + T], writes=[bx])
            g, bg_ = gp.next()
            S.dma("sp", g[:, :, :T], fm3(self.gates)[:, :, t0:t0 + T], writes=[bg_])
            brs_ = []
            for src in (self.brA, self.brB, self.brC):
                b_, bb_ = brp.next()
                S.dma("sp", b_[:, :, :T], fm3(src)[:, :, t0:t0 + T], writes=[bb_])
                brs_.append((b_, bb_))
            mg, bmg = mgp.next()
            for c in range(8):
                t1, bt1 = t1p.next()
                t2, bt2 = t2p.next()
                self.tt("dve", t1[:, :T], g[:, c, :T], brs_[0][0][:, c, :T], ALU.mult, [bg_, brs_[0][1]], [bt1])
                self.tt("pool", t2[:, :T], g[:, 8 + c, :T], brs_[1][0][:, c, :T], ALU.mult, [bg_, brs_[1][1]], [bt2])
                self.tt("dve", t1[:, :T], t1[:, :T], t2[:, :T], ALU.add, [bt1, bt2], [bt1])
                self.tt("pool", t2[:, :T], g[:, 16 + c, :T], brs_[2][0][:, c, :T], ALU.mult, [bg_, brs_[2][1], bt1], [bt2])
                self.tt("dve", mg[:, c, :T], t1[:, :T], t2[:, :T], ALU.add, [bt1, bt2], [bmg])
            for j in range(8):
                ps, bps = banks.next()
                for k in range(8):
                    self.mm(ps[:, :T], w[:, k, j * 128:(j + 1) * 128], mg[:, k, :T], k == 0, k == 7, [bw, bmg], [bps])
                self.stt(x[:, j, :T], ps[:, :T], self.modv[:, 16 + j, col:col + 1], x[:, j, :T], ALU.mult, ALU.add,
                         [bps, bx] + mb, [bx])
            xn, bxn = xnp.next()
            sq, bsq = sqp.next()
            h, bh = hp.next()
            rs, brs = rsp.next()
            self.norm_mod(ph, x, bx, T, 1, col, xn, bxn, sq, bsq, h, bh, rs, brs, psn)
            f, bf = fp.next()
            for gi in range(11):
                wg, bwg = wgp.next()
                S.dma("sp", wg[:, :, 0:256], wgs[:, :, gi * 256:(gi + 1) * 256], writes=[bwg])
                S.dma("sp", wg[:, :, 256:512], wgs[:, :, DFF + gi * 256:DFF + (gi + 1) * 256], writes=[bwg])
                pss = []
                for j in range(4):
                    ps, bps = banks.next()
                    for k in range(8):
                        self.mm(ps[:, :T], wg[:, k, j * 128:(j + 1) * 128], h[:, k, :T], k == 0, k == 7, [bwg, bh], [bps])
                    pss.append((ps, bps))
                for j in range(2):
                    tmp, btmp = tmpp.next()
                    self.act(tmp[:, :T], pss[j][0][:, :T], AF.Silu, [pss[j][1]], [btmp])
                    self.tt("dve", f[:, gi * 2 + j, :T], tmp[:, :T], pss[2 + j][0][:, :T], ALU.mult,
                            [btmp, pss[2 + j][1]], [bf])
            for half in range(2):
                wd, bwd = wdp.next()
                S.dma("sp", wd[:], wds[:, :, half * 512:(half + 1) * 512], writes=[bwd])
                for j in range(4):
                    jj = half * 4 + j
                    ps, bps = banks.next()
                    for k in range(22):
                        self.mm(ps[:, :T], wd[:, k, j * 128:(j + 1) * 128], f[:, k, :T], k == 0, k == 21, [bwd, bf], [bps])
                    self.stt(x[:, jj, :T], ps[:, :T], self.modv[:, 40 + jj, col:col + 1], x[:, jj, :T], ALU.mult, ALU.add,
                             [bps, bx] + mb, [bx])
            if not last:
                S.dma("pool", fm3(self.x1)[:, :, t0:t0 + T], x[:, :, :T], reads=[bx])
            else:
                sq2, bsq2 = sqp.next()
                self.act(sq2[:, :, :T], x[:, :, :T], AF.Square, [bx], [bsq2])
                ps, bps = psn
                for c in range(8):
                    self.mm(ps[:, :T], self.ones_b[:, 0, :], sq2[:, c, :T], c == 0, c == 7, [bsq2] + cb, [bps])
                rs2, brs2 = rsp.next()
                self.rsqrt_(rs2[:, :T], ps[:, :T], [bps] + cb, [brs2], self.eps_t)
                xn2, bxn2 = xnp.next()
                self.tt("dve", xn2[:, :, :T], x[:, :, :T], bc(rs2[:, :T], [128, 8, T], 1), ALU.mult, [bx, brs2], [bxn2])
                self.tt("pool", xn2[:, :, :T], xn2[:, :, :T], bc(self.fing_t[:], [128, 8, T], 2), ALU.mult,
                        [bxn2] + cb, [bxn2])
                S.dma("pool", fm3(self.yT)[:, :, t0 - LC:t0 - LC + T], xn2[:, :, :T], reads=[bxn2])
        ph.close()

    def phase4(self, l, last):
        S = self.S
        ph = Phase(self, f"p4_{l}")
        cb = [self.cbuf]
        wna, bwna = self.load_w_sq(ph, "w_na_out", l)
        E, bE = ph.sb([64, 16, 15, 64], BF16)
        E2, bE2 = ph.sb([128, 16, 15, 64], BF16)
        tmpph = Phase(self, f"p4e_{l}")
        ep = tmpph.pool(2, [64, 2, 15, 64])
        for i in range(8):
            e32, be32 = ep.next()
            S.dma("sp", e32[:], self.rpbT[l][:, i * 2:(i + 1) * 2], writes=[be32])
            self.act(E[:, i * 2:(i + 1) * 2], e32[:], AF.Exp, [be32], [bE])
        self.ms("pool", E2[:, 0:8], 0.0, [bE2])
        self.ms("pool", E2[:, 8:16], 0.0, [bE2])
        ep2 = tmpph.pool(2, [128, 2, 8, 64])
        for i in range(8):
            e32, be32 = ep2.next()
            S.dma("sp", e32[0:64], self.rpbT[l][:, i * 2:(i + 1) * 2, 4:12, :], writes=[be32])
            S.dma("sp", e32[64:128], self.rpbT[l][:, i * 2:(i + 1) * 2, 4:12, :], writes=[be32])
            self.act(E2[0:64, i * 2:(i + 1) * 2, 4:12, :], e32[0:64], AF.Exp, [be32], [bE2])
            self.act(E2[64:128, i * 2:(i + 1) * 2, 5:13, :], e32[64:128], AF.Exp, [be32], [bE2])
        tmpph.close()
        kTc, bkTc = ph.sb([128, 8, LC], BF16)
        S.dma("sp", kTc[:], self.nak.rearrange("(c p) t -> p c t", p=128)[:, :, 0:LC], writes=[bkTc])
        Vc, bVc = ph.sb([128, 2, D], BF16)
        S.dma("sp", Vc[:], self.nav[0:LC, :].rearrange("(j p) f -> p j f", p=128), writes=[bVc])
        kTp = ph.pool(1, [128, 8, 960], BF16)
        Vp = ph.pool(1, [128, 8, D], BF16)
        Voddp = ph.pool(1, [64, 6, D], BF16)
        qp = ph.pool(2, [128, 8, 512], BF16)
        oTp = ph.pool(1, [128, 8, 512], BF16)
        Pp = ph.pool(3, [128, 512], BF16)
        Pfp = ph.pool(3, [128, 512])
        recp = ph.pool(2, [64, 512])
        stp = ph.pool(2, [128, 4, 512], BF16)
        sbanks = ph.psum(3)
        accs = ph.psum(4)
        pbank = ph.psum(1)
        naqs = self.naq.rearrange("(c p) t -> p c t", p=128)
        naks = self.nak.rearrange("(c p) t -> p c t", p=128)
        r0f = lambda qr: min(max(qr - 4, 0), ROWS - 8)
        for (t0, T, isctx) in TILES:
            if isctx and last:
                continue
            q, bq = qp.next()
            S.dma("sp", q[:, :, :T], naqs[:, :, t0:t0 + T], writes=[bq])
            rows = []
            if not isctx:
                R0 = (t0 - LC) // GRID_W
                kmin, kmax = r0f(R0), r0f(R0 + 7) + 7
                nr = kmax - kmin + 1
                kT, bkT = kTp.next()
                S.dma("sp", kT[:, :, 0:nr * 64], naks[:, :, LC + kmin * 64:LC + (kmax + 1) * 64], writes=[bkT])
                V, bV = Vp.next()
                interior = (nr == 15 and kmin == R0 - 4)
                npair = nr // 2
                S.dma("sp", V[:, 0:npair, :],
                      self.nav[LC + kmin * 64:LC + (kmin + 2 * npair) * 64, :].rearrange("(r p) f -> p r f", p=128), writes=[bV])
                if nr % 2:
                    S.dma("sp", V[0:64, npair, :], self.nav[LC + kmax * 64:LC + (kmax + 1) * 64, :], writes=[bV])
                Vodd, bVodd = Voddp.next()
                if not interior:
                    nodd = nr // 2
                    S.dma("sp", Vodd[:, 0:nodd, :],
                          self.nav[LC + kmin * 64:LC + (kmin + 2 * nodd) * 64, :].rearrange("(r two p) f -> two p r f", two=2, p=64)[1],
                          writes=[bVodd])
                for kr in range(kmin, kmax + 1):
                    qs = [qr for qr in range(R0, R0 + 8) if r0f(qr) <= kr <= r0f(qr) + 7]
                    if not qs:
                        continue
                    if interior and kr < kmin + 14:
                        if (kr - kmin) % 2 == 0:
                            qs2 = [qr for qr in range(R0, R0 + 8) if r0f(qr) <= kr + 1 <= r0f(qr) + 7]
                            allq = sorted(set(qs) | set(qs2))
                            rows.append(("pair", kr, allq[0], allq[-1] + 1))
                    else:
                        rows.append(("row", kr, qs[0], qs[-1] + 1))
            oT, boT = oTp.next()
            steps = []
            for h in range(16):
                steps.append(("ctx", h, 0, None))
                for ri, row in enumerate(rows):
                    steps.append(("row", h, ri, row))
                steps.append(("ctx", h, 1, None))
            state = {}

            def emit_qk(st_):
                kind, h, a, row = st_
                c, po = h // 2, (h % 2) * 64
                ps, bps = sbanks.next()
                P, bP = Pp.next()
                if kind == "ctx":
                    self.mm(ps[:, :T], kTc[po:po + 64, c, a * 128:(a + 1) * 128], q[po:po + 64, c, :T], True, True,
                            [bkTc, bq], [bps])
                    self.act(P[:, :T], ps[:, :T], AF.Exp, [bps], [bP], scale=0.125)
                else:
                    rk, kr, qlo, qhi = row
                    cs, nq = (qlo - R0) * 64, qhi - qlo
                    n = nq * 64
                    np_ = 128 if rk == "pair" else 64
                    self.mm(ps[0:np_, 0:n], kT[po:po + 64, c, (kr - kmin) * 64:(kr - kmin) * 64 + np_],
                            q[po:po + 64, c, cs:cs + n], True, True, [bkT, bq], [bps])
                    Pf, bPf = Pfp.next()
                    self.act(Pf[0:np_, 0:n], ps[0:np_, 0:n], AF.Exp, [bps], [bPf], scale=0.125)
                    elo = qlo - kr + 7
                    tab, btab = (E2, bE2) if rk == "pair" else (E, bE)
                    self.tt("dve" if a % 2 == 0 else "pool", P[0:np_, 0:n].rearrange("p (a b) -> p a b", b=64),
                            Pf[0:np_, 0:n].rearrange("p (a b) -> p a b", b=64), tab[0:np_, h, elo:elo + nq, :], ALU.mult,
                            [bPf, btab], [bP])
                return (P, bP)

            def emit_pv(st_, pt):
                kind, h, a, row = st_
                P, bP = pt
                if kind == "ctx" and a == 0:
                    state["acc"] = accs.next()
                    state["accS"] = accs.next()
                acc, bacc = state["acc"]
                accS, baccS = state["accS"]
                if kind == "ctx":
                    first = (a == 0)
                    self.mm(acc[0:64, :T], Vc[:, a, h * 64:(h + 1) * 64], P[:, :T], first, not first, [bVc, bP], [bacc])
                    self.mm(accS[0:64, :T], self.ones_b[:, 3, 0:64], P[:, :T], first, not first, [bP] + cb, [baccS])
                    if not first:
                        rec, brec = recp.next()
                        self.S.op("dve", lambda e, o=rec[:, :T], i=accS[0:64, :T]: e.reciprocal(out=o, in_=i), [baccS], [brec])
                        self.tt("dve", oT[(h % 2) * 64:(h % 2) * 64 + 64, h // 2, :T], acc[0:64, :T], rec[:, :T], ALU.mult,
                                [bacc, brec], [boT])
                else:
                    rk, kr, qlo, qhi = row
                    cs, n = (qlo - R0) * 64, (qhi - qlo) * 64
                    np_ = 128 if rk == "pair" else 64
                    if rk == "row" and (kr - kmin) % 2 == 1:
                        vsrc, bvsrc = Vodd[0:64, (kr - kmin) // 2, h * 64:(h + 1) * 64], bVodd
                    else:
                        vsrc, bvsrc = V[0:np_, (kr - kmin) // 2, h * 64:(h + 1) * 64], bV
                    self.mm(acc[0:64, cs:cs + n], vsrc, P[0:np_, 0:n], False, False, [bvsrc, bP], [bacc])
                    self.mm(accS[0:64, cs:cs + n], self.ones_b[0:np_, 3, 0:64], P[0:np_, 0:n], False, False,
                            [bP] + cb, [baccS])

            LA = 2
            pts = {}
            for i in range(len(steps) + LA):
                if i < len(steps):
                    pts[i] = emit_qk(steps[i])
                if i - LA >= 0:
                    emit_pv(steps[i - LA], pts.pop(i - LA))
            for half in range(2):
                stg, bst = stp.next()
                for j in range(4):
                    jj = half * 4 + j
                    ps, bps = pbank.next()
                    for k in range(8):
                        self.mm(ps[:, :T], wna[:, k, jj * 128:(jj + 1) * 128], oT[:, k, :T], k == 0, k == 7,
                                [bwna, boT], [bps])
                    self.cp("act" if j % 2 == 0 else "dve", stg[:, j, :T], ps[:, :T], [bps], [bst])
                S.dma("pool", self.brC[half * 512:(half + 1) * 512, t0:t0 + T].rearrange("(j p) t -> p j t", p=128),
                      stg[:, :, :T], reads=[bst])
        ph.close()

    def phase3a(self, l):
        S = self.S
        ph = Phase(self, f"p3a_{l}")
        cb = [self.cbuf]
        dcw, bdcw = ph.sb([128, 24, 5])
        S.dma("sp", dcw[:], self.dn_conv[l], writes=[bdcw])
        diag, bdiag = ph.sb([128, 120, 128], BF16)
        for cc in range(24):
            for k in range(5):
                self.ts("dve" if (cc * 5 + k) % 2 == 0 else "pool", diag[:, cc * 5 + k, :], self.cm32[:, CM_ID, :],
                        dcw[:, cc, k:k + 1], None, ALU.mult, None, [bdcw] + cb, [bdiag])
        pinp = ph.pool(2, [128, 24, 512 + 4], BF16)
        accp = ph.pool(2, [128, 8, 512])
        sqp = ph.pool(1, [128, 8, 512], BF16)
        rsp = ph.pool(1, [128, 8, 512])
        sbp = ph.pool(1, [128, 8, 512], BF16)
        outp = ph.pool(2, [128, 8, 512], BF16)
        ropep = ph.pool(2, [128, 2, 512])
        t1p = ph.pool(2, [128, 512])
        t2p = ph.pool(2, [128, 512])
        stp = ph.pool(2, [128, 1024], BF16)
        pst = ph.psum(2, BF16, 1024)
        pso = ph.psum(3)
        psp = ph.psum(3)
        src = self.dnpre.rearrange("(c p) t -> p c t", p=128)
        ident = self.cmb[:, 0, :]
        perm = self.cmb[:, 1, :]

        def transposes(t, bt, T, dst, t0):
            for sidx in range(T // 128):
                ps, bps = pst.next()
                for hh in range(8):
                    self.tr(ps[:, hh * 128:(hh + 1) * 128], t[:, hh, sidx * 128:(sidx + 1) * 128], ident, [bt] + cb, [bps])
                stg, bst = stp.next()
                self.cp("act" if sidx % 2 == 0 else "dve", stg[:], ps[:], [bps], [bst])
                S.dma("pool", dst[t0 + sidx * 128:t0 + (sidx + 1) * 128, :], stg[:], reads=[bst])

        for (t0, T, isctx) in TILES:
            s0, s1 = (0, LC) if isctx else (LC, NT)
            lo, hi = max(t0 - 2, s0), min(t0 + T + 2, s1)
            pin, bpin = pinp.next()
            if lo > t0 - 2:
                self.ms("pool", pin[:, :, 0:2], 0.0, [bpin])
            if hi < t0 + T + 2:
                self.ms("pool", pin[:, :, T + 2:T + 4], 0.0, [bpin])
            S.dma("sp", pin[:, :, lo - (t0 - 2):hi - (t0 - 2)], src[:, :, lo:hi], writes=[bpin])
            if not isctx:
                rope, brope = ropep.next()
                S.dma("sp", rope[:, 0, :T], self.ropeC[:, t0 - LC:t0 - LC + T], writes=[brope])
                S.dma("sp", rope[:, 1, :T], self.ropeS[:, t0 - LC:t0 - LC + T], writes=[brope])
            for part in range(3):
                acc, bacc = accp.next()
                out, bout = outp.next()
                for c in range(8):
                    cc = part * 8 + c
                    ps, bps = pso.next()
                    for k in range(5):
                        self.mm(ps[:, :T], diag[:, cc * 5 + k, :], pin[:, cc, k:k + T], k == 0, k == 4, [bpin, bdiag], [bps])
                    if part == 2:
                        self.act(out[:, c, :T], ps[:, :T], AF.Silu, [bps], [bout])
                    else:
                        self.act(acc[:, c, :T], ps[:, :T], AF.Silu, [bps], [bacc])
                if part == 2:
                    transposes(out, bout, T, self.dvt, t0)
                    continue
                sq, bsq = sqp.next()
                self.act(sq[:, :, :T], acc[:, :, :T], AF.Square, [bacc], [bsq])
                rs, brs = rsp.next()
                for c in range(8):
                    ps, bps = pso.next()
                    self.mm(ps[:, :T], self.ones_b[:, 2 if part == 0 else 3, :], sq[:, c, :T], True, True, [bsq] + cb, [bps])
                    self.rsqrt_(rs[:, c, :T], ps[:, :T], [bps] + cb, [brs], self.eps128_t if part == 0 else self.eps_t)
                if isctx:
                    self.tt("dve", out[:, :, :T], acc[:, :, :T], rs[:, :, :T], ALU.mult, [bacc, brs], [bout])
                else:
                    sb_, bsb = sbp.next()
                    self.cp("pool", sb_[:, :, :T], acc[:, :, :T], [bacc], [bsb])
                    for c in range(8):
                        ps, bps = psp.next()
                        self.mm(ps[:, :T], perm, sb_[:, c, :T], True, True, [bsb] + cb, [bps])
                        t1, bt1 = t1p.next()
                        t2, bt2 = t2p.next()
                        self.tt("pool", t1[:, :T], acc[:, c, :T], rope[:, 0, :T], ALU.mult, [bacc, brope], [bt1])
                        self.tt("dve", t2[:, :T], ps[:, :T], rope[:, 1, :T], ALU.mult, [bps, brope], [bt2])
                        self.tt("dve", t1[:, :T], t1[:, :T], t2[:, :T], ALU.add, [bt1, bt2], [bt1])
                        self.tt("dve", out[:, c, :T], t1[:, :T], rs[:, c, :T], ALU.mult, [bt1, brs], [bout])
                dst = self.dq if part == 0 else self.dk
                S.dma("pool", dst.rearrange("(c p) t -> p c t", p=128)[:, :, t0:t0 + T], out[:, :, :T], reads=[bout])
                if part == 1:
                    transposes(out, bout, T, self.dkt, t0)
        ph.close()

    def phase3d(self, l, last):
        S = self.S
        ph = Phase(self, f"p3d_{l}")
        cb = [self.cbuf]
        w, bw = self.load_w_sq(ph, "w_dn_out", l)
        ofp = ph.pool(2, [128, 8, 512])
        obp = ph.pool(2, [128, 8, 512])
        zp = ph.pool(2, [128, 8, 512], BF16)
        sqp = ph.pool(1, [128, 8, 512], BF16)
        rsp = ph.pool(2, [128, 512])
        tp = ph.pool(2, [128, 512])
        yp = ph.pool(2, [128, 8, 512], BF16)
        stp = ph.pool(2, [128, 4, 512], BF16)
        banks = ph.psum(5)
        pso = ph.psum(3)
        fm3 = lambda t: t.rearrange("(c p) t -> p c t", p=128)
        for (t0, T, isctx) in TILES:
            if isctx and last:
                continue
            of, bof = ofp.next()
            ob, bob = obp.next()
            z, bz = zp.next()
            S.dma("sp", of[:, :, :T], fm3(self.ofw)[:, :, t0:t0 + T], writes=[bof])
            S.dma("sp", ob[:, :, :T], fm3(self.obw)[:, :, t0:t0 + T], writes=[bob])
            S.dma("sp", z[:, :, :T], fm3(self.zs)[:, :, t0:t0 + T], writes=[bz])
            self.tt("pool", of[:, :, :T], of[:, :, :T], ob[:, :, :T], ALU.add, [bof, bob], [bof])
            sq, bsq = sqp.next()
            self.act(sq[:, :, :T], of[:, :, :T], AF.Square, [bof], [bsq])
            y, by = yp.next()
            for c in range(8):
                ps, bps = pso.next()
                self.mm(ps[:, :T], self.ones_b[:, 1, :], sq[:, c, :T], True, True, [bsq] + cb, [bps])
                rs, brs = rsp.next()
                self.rsqrt_(rs[:, :T], ps[:, :T], [bps] + cb, [brs], self.eps_t)
                t, bt = tp.next()
                self.tt("dve", t[:, :T], of[:, c, :T], rs[:, :T], ALU.mult, [bof, brs], [bt])
                self.stt(y[:, c, :T], t[:, :T], self.vcol(l, V_DNG, 0), z[:, c, :T], ALU.mult, ALU.mult,
                         [bt, bz] + cb, [by])
            self.proj_store(y, by, T, w, bw, banks, stp, self.brB, t0)
        ph.close()

    def phase3c(self, l):
        S = self.S
        ph = Phase(self, f"p3c_{l}")
        cb = [self.cbuf]
        cm = self.cm32
        I64b = self.cmb[0:64, 0, 0:64]
        psSm, bpsSm = ph.psum(1).next()
        psT, bpsT = ph.psum(1, BF16, 1024).next()
        dks = self.dk.rearrange("(c p) t -> p c t", p=128)
        dqs = self.dq.rearrange("(c p) t -> p c t", p=128)

        def chain(d):
            U = cm[0:64, CM_UF if d == 0 else CM_UB, 0:64]
            negU = cm[0:64, CM_NUF if d == 0 else CM_NUB, 0:64]
            Mneg = cm[0:64, CM_MNF if d == 0 else CM_MNB, 0:64]
            MnegT = cm[0:64, CM_MNB if d == 0 else CM_MNF, 0:64]
            MsBD = cm[0:64, CM_BDF if d == 0 else CM_BDB, 0:64]
            MsOFF = cm[0:64, CM_OFF_F if d == 0 else CM_OFF_B, 0:64]
            lastc = 63 if d == 0 else 0
            odst = self.ofw if d == 0 else self.obw
            rr = ph.psum(2)
            psX, bpsX = ph.psum(1).next()
            kTbp = ph.pool(2, [128, 8, 256], BF16)
            qTbp = ph.pool(2, [128, 8, 256], BF16)
            ktbp = ph.pool(2, [64, 4, D], BF16)
            vtbp = ph.pool(2, [64, 4, D], BF16)
            bgbp = ph.pool(2, [64, 4, 32])
            f3 = [64, 8, 64]
            Gm, bGm = ph.sb(f3)
            gB, bgB = ph.sb(f3)
            Dm, bDm = ph.sb(f3)
            DT, bDT = ph.sb(f3)
            Eg, bEg = ph.sb([128, 8, 64])
            sm, bsm = ph.sb([128, 16])
            ed, bed = ph.sb([64, 8])
            be, bbe = ph.sb([64, 8])
            L0, bL0 = ph.sb(f3)
            Lp = ph.pool(2, f3, BF16)
            Np = ph.pool(2, f3, BF16)
            ImN, bImN = ph.sb(f3, BF16)
            Coff, bCoff = ph.sb(f3, BF16)
            Xn, bXn = ph.sb(f3, BF16)
            Yb, bYb = ph.sb(f3, BF16)
            Xbp = ph.pool(2, f3, BF16)
            Aqk, bAqk = ph.sb(f3, BF16)
            M1T, bM1T = ph.sb(f3, BF16)
            M2T, bM2T = ph.sb(f3, BF16)
            u, bu = ph.sb([64, 8, 128])
            wT, bwT = ph.sb([128, 8, 64], BF16)
            kdec, bkdec = ph.sb([64, 8, 128], BF16)
            qdT, bqdT = ph.sb([128, 8, 64], BF16)
            vnew, bvnew = ph.sb([64, 8, 128], BF16)
            ost, bost = ph.sb([128, 8, 64])
            St, bSt = ph.sb([128, 8, 128])
            Sb, bSb = ph.sb([128, 8, 128], BF16)
            self.ms("pool", St[:], 0.0, [bSt])
            self.ms("pool", Sb[:], 0.0, [bSb])
            order = list(range(4)) + list(range(4, NT // 64)) if d == 0 else \
                list(range(3, -1, -1)) + list(range(NT // 64 - 1, 3, -1))
            curb = None
            for n in order:
                b, nn = n // 4, n % 4
                if b != curb:
                    curb = b
                    kTb, bkTb = kTbp.next()
                    qTb, bqTb = qTbp.next()
                    ktb, bktb = ktbp.next()
                    vtb, bvtb = vtbp.next()
                    bgb, bbgb = bgbp.next()
                    tk = slice(b * 256, (b + 1) * 256)
                    S.dma("sp", kTb[:], dks[:, :, tk], writes=[bkTb])
                    S.dma("sp", qTb[:], dqs[:, :, tk], writes=[bqTb])
                    S.dma("sp", ktb[:], self.dkt[tk, :].rearrange("(n p) f -> p n f", p=64), writes=[bktb])
                    S.dma("sp", vtb[:], self.dvt[tk, :].rearrange("(n p) f -> p n f", p=64), writes=[bvtb])
                    S.dma("sp", bgb[:], self.bg[tk, :].rearrange("(n p) c -> p n c", p=64), writes=[bbgb])
                kT_c = kTb[:, :, nn * 64:(nn + 1) * 64]
                qT_c = qTb[:, :, nn * 64:(nn + 1) * 64]
                kt_c = ktb[:, nn, :].rearrange("p (h f) -> p h f", f=128)
                vt_c = vtb[:, nn, :].rearrange("p (h f) -> p h f", f=128)
                beta = bgb[:, nn, 8 * d:8 * d + 8]
                g = bgb[:, nn, 16 + 8 * d:24 + 8 * d]
                flat = lambda t: t.rearrange("p h f -> p (h f)")
                self.tt("pool", Gm[:], bc(U, f3, 1), bc(g, f3, 2), ALU.mult, cb + [bbgb], [bGm])
                self.cp("pool", gB[:], bc(g, f3, 2), [bbgb], [bgB])
                psR, bpsR = rr.next()
                psE, bpsE = rr.next()
                self.mm(psR[0:64, :], self.ones_f[0:64, 0:64], flat(Gm[:]), True, False, cb + [bGm], [bpsR])
                self.mm(psR[0:64, :], negU, flat(gB[:]), False, True, cb + [bgB], [bpsR])
                self.mm(psE[:, :], self.ones_f[0:64, :], flat(Gm[:]), True, True, cb + [bGm], [bpsE])
                self.mm(psSm[:, 0:8], self.ones_f[0:64, :], g, True, True, cb + [bbgb], [bpsSm])
                self.mm(psSm[0:64, 8:16], U, g, True, True, cb + [bbgb], [bpsSm])
                psR3 = psR[0:64, :].rearrange("p (h f) -> p h f", f=64)
                self.tt("dve", Dm[:], bc(Mneg, f3, 1), psR3, ALU.subtract, cb + [bpsR], [bDm])
                self.act(Dm[:], Dm[:], AF.Exp, [bDm], [bDm])
                self.tt("dve", DT[:], psR3, bc(MnegT, f3, 1), ALU.add, cb + [bpsR], [bDT])
                self.act(DT[:], DT[:], AF.Exp, [bDT], [bDT])
                self.act(flat(Eg[:]), psE[:, :], AF.Exp, [bpsE], [bEg])
                self.act(sm[:], psSm[:, 0:16], AF.Exp, [bpsSm], [bsm])
                self.act(ed[:], psR3[:, :, lastc], AF.Exp, [bpsR], [bed])
                self.tt("dve", be[:], beta, sm[0:64, 8:16], ALU.mult, [bbgb, bsm], [bbe])
                yield
                psKK, bpsKK = rr.next()
                psQK, bpsQK = rr.next()
                for h in range(8):
                    self.mm(psKK[0:64, h * 64:(h + 1) * 64], kT_c[:, h, :], kT_c[:, h, :], True, True, [bkTb], [bpsKK])
                for h in range(8):
                    self.mm(psQK[0:64, h * 64:(h + 1) * 64], kT_c[:, h, :], qT_c[:, h, :], True, True, [bkTb, bqTb], [bpsQK])
                self.tt("dve", flat(L0[:]), psKK[0:64, :], flat(Dm[:]), ALU.mult, [bpsKK, bDm], [bL0])
                self.tt("dve", L0[:], L0[:], bc(beta, f3, 2), ALU.mult, [bL0, bbgb], [bL0])
                Lc, bLc = Lp.next()
                self.tt("dve", Lc[:], L0[:], bc(MsBD, f3, 1), ALU.mult, [bL0] + cb, [bLc])
                self.tt("pool", Coff[:], L0[:], bc(MsOFF, f3, 1), ALU.mult, [bL0] + cb, [bCoff])
                self.tt("dve", flat(Aqk[:]), psQK[0:64, :], flat(DT[:]), ALU.mult, [bpsQK, bDT], [bAqk])
                for h in range(8):
                    self.tr(psT[0:64, h * 64:(h + 1) * 64], Lc[:, h, :], I64b, [bLc] + cb, [bpsT])
                Nc, bNc = Np.next()
                self.cp("act", flat(Nc[:]), psT[0:64, 0:512], [bpsT], [bNc])
                self.tt("pool", ImN[:], bc(I64b, f3, 1), Nc[:], ALU.subtract, cb + [bNc], [bImN])
                yield
                self.mm(psX[0:64, :], I64b, flat(ImN[:]), True, True, cb + [bImN], [bpsX])
                Xb, bXb = Xbp.next()
                self.cp("act", flat(Xb[:]), psX[0:64, :], [bpsX], [bXb])
                NST = 4
                for k in range(1, NST + 1):
                    psL, bpsL = rr.next()
                    for h in range(8):
                        self.mm(psL[0:64, h * 64:(h + 1) * 64], Nc[:, h, :], Lc[:, h, :], True, True, [bNc, bLc], [bpsL])
                    if k < NST:
                        psN, bpsN = rr.next()
                        for h in range(8):
                            self.mm(psN[0:64, h * 64:(h + 1) * 64], Lc[:, h, :], Nc[:, h, :], True, True, [bNc, bLc], [bpsN])
                    Lc, bLc = Lp.next()
                    self.cp("act", flat(Lc[:]), psL[0:64, :], [bpsL], [bLc])
                    if k < NST:
                        Nc, bNc = Np.next()
                        self.cp("dve", flat(Nc[:]), psN[0:64, :], [bpsN], [bNc])
                    for h in range(8):
                        self.mm(psX[0:64, h * 64:(h + 1) * 64], Lc[:, h, :], Xb[:, h, :], False, True, [bLc, bXb], [bpsX])
                    Xb, bXb = Xbp.next()
                    self.cp("act", flat(Xb[:]), psX[0:64, :], [bpsX], [bXb])
                    yield
                for h in range(8):
                    self.tr(psT[0:64, h * 64:(h + 1) * 64], Xb[:, h, :], I64b, [bXb] + cb, [bpsT])
                self.act(flat(Xn[:]), psT[0:64, 0:512], AF.Identity, [bpsT], [bXn], scale=-1.0)
                psY, bpsY = rr.next()
                for h in range(8):
                    self.mm(psY[0:64, h * 64:(h + 1) * 64], Coff[:, h, :], Xb[:, h, :], True, True, [bCoff, bXb], [bpsY])
                self.cp("dve", flat(Yb[:]), psY[0:64, :], [bpsY], [bYb])
                for h in range(8):
                    self.mm(psX[0:64, h * 64:(h + 1) * 64], Xn[:, h, :], Yb[:, h, :], False, True, [bXn, bYb], [bpsX])
                yield
                psX3 = psX[0:64, :].rearrange("p (h f) -> p h f", f=64)
                self.tt("dve", M1T[:], psX3, bc(beta, f3, 2), ALU.mult, [bpsX, bbgb], [bM1T])
                self.tt("dve", M2T[:], psX3, bc(be[:], f3, 2), ALU.mult, [bpsX, bbe], [bM2T])
                pu = [rr.next(), rr.next()]
                for h in range(8):
                    p_, bp_ = pu[h // 4]
                    self.mm(p_[0:64, (h % 4) * 128:(h % 4 + 1) * 128], M1T[:, h, :], vt_c[:, h, :], True, True,
                            [bM1T, bvtb], [bp_])
                for i in range(2):
                    self.cp("act", flat(u[:, i * 4:(i + 1) * 4, :]), pu[i][0][0:64, :], [pu[i][1]], [bu])
                psW, bpsW = rr.next()
                for h in range(8):
                    self.mm(psW[:, h * 64:(h + 1) * 64], kt_c[:, h, :], M2T[:, h, :], True, True, [bktb, bM2T], [bpsW])
                self.cp("dve", flat(wT[:]), psW[:, :], [bpsW], [bwT])
                self.tt("pool", kdec[:], kt_c, bc(ed[:], [64, 8, 128], 2), ALU.mult, [bktb, bed], [bkdec])
                self.tt("pool", qdT[:], qT_c, Eg[:], ALU.mult, [bqTb, bEg], [bqdT])
                yield
                pv = [rr.next(), rr.next()]
                for h in range(8):
                    p_, bp_ = pv[h // 4]
                    self.mm(p_[0:64, (h % 4) * 128:(h % 4 + 1) * 128], wT[:, h, :], Sb[:, h, :], True, True,
                            [bwT, bSb], [bp_])
                for i in range(2):
                    self.tt("dve", flat(vnew[:, i * 4:(i + 1) * 4, :]), flat(u[:, i * 4:(i + 1) * 4, :]), pv[i][0][0:64, :],
                            ALU.subtract, [bu, pv[i][1]], [bvnew])
                psO, bpsO = rr.next()
                for h in range(8):
                    self.mm(psO[:, h * 64:(h + 1) * 64], Sb[:, h, :], qdT[:, h, :], True, False, [bSb, bqdT], [bpsO])
                    self.mm(psO[:, h * 64:(h + 1) * 64], vnew[:, h, :], Aqk[:, h, :], False, True, [bvnew, bAqk], [bpsO])
                self.cp("act", flat(ost[:]), psO[:, :], [bpsO], [bost])
                S.dma("pool", odst[:, n * 64:(n + 1) * 64].rearrange("(h p) t -> p h t", p=128), ost[:], reads=[bost])
                pS = [rr.next(), rr.next()]
                for h in range(8):
                    p_, bp_ = pS[h // 4]
                    self.mm(p_[:, (h % 4) * 128:(h % 4 + 1) * 128], kdec[:, h, :], vnew[:, h, :], True, True,
                            [bkdec, bvnew], [bp_])
                self.tt("dve", St[:], St[:], bc(sm[:, 0:8], [128, 8, 128], 2), ALU.mult, [bSt, bsm], [bSt])
                for i in range(2):
                    self.tt("dve", flat(St[:, i * 4:(i + 1) * 4, :]), flat(St[:, i * 4:(i + 1) * 4, :]), pS[i][0][:, :],
                            ALU.add, [bSt, pS[i][1]], [bSt])
                self.cp("act", Sb[:], St[:], [bSt], [bSb])
                yield

        gens = [chain(0), chain(1)]
        alive = [True, True]
        for _ in range(4):
            next(gens[0])
        while any(alive):
            for i, gnr in enumerate(gens):
                if alive[i]:
                    try:
                        next(gnr)
                    except StopIteration:
                        alive[i] = False
        ph.close()


def host_consts():
    f32 = np.float32
    t = np.arange(SEQ)
    inv = (np.float32(10000.0) ** (-np.arange(0, 64, 2, dtype=f32) / f32(64))).astype(f32)
    ang_r = ((t // GRID_W).astype(f32)[:, None] * inv).astype(f32)
    ang_c = ((t % GRID_W).astype(f32)[:, None] * inv).astype(f32)
    C = np.zeros((128, SEQ), f32)
    Sg = np.zeros((128, SEQ), f32)
    for f in range(128):
        ang = ang_r if f < 64 else ang_c
        fi = f % 32
        C[f] = np.cos(ang[:, fi])
        Sg[f] = np.sin(ang[:, fi]) * (-1.0 if (f % 64) < 32 else 1.0)
    cm = np.zeros((128, NCM, 128), f32)
    cm[:, CM_ID, :] = np.eye(128, dtype=f32)
    for m in range(128):
        partner = m + 32 if (m % 64) < 32 else m - 32
        cm[partner, CM_PERM, m] = 1.0
    i = np.arange(64)
    le = (i[:, None] <= i[None, :]).astype(f32)
    ge = (i[:, None] >= i[None, :]).astype(f32)
    cm[:64, CM_UF, :64] = le
    cm[:64, CM_UB, :64] = ge
    cm[:64, CM_MNF, :64] = np.where(i[:, None] >= i[None, :], 0.0, NEG)
    cm[:64, CM_MNB, :64] = np.where(i[:, None] <= i[None, :], 0.0, NEG)
    cm[:64, CM_MSF, :64] = (i[:, None] > i[None, :]).astype(f32)
    cm[:64, CM_MSB, :64] = (i[:, None] < i[None, :]).astype(f32)
    cm[:64, CM_NUF, :64] = -le
    cm[:64, CM_NUB, :64] = -ge
    bd = ((i[:, None] // 32) == (i[None, :] // 32)).astype(f32)
    cm[:64, CM_BDF, :64] = cm[:64, CM_MSF, :64] * bd
    cm[:64, CM_OFF_F, :64] = cm[:64, CM_MSF, :64] * (1 - bd)
    cm[:64, CM_BDB, :64] = cm[:64, CM_MSB, :64] * bd
    cm[:64, CM_OFF_B, :64] = cm[:64, CM_MSB, :64] * (1 - bd)
    return dict(ropeC=C, ropeS=Sg, cmat=cm)


def fm(v, nchunk):
    sh = v.shape[:-1]
    return np.ascontiguousarray(np.moveaxis(v.reshape(sh + (nchunk, 128)), -1, -2))


def host_shared(inp):
    f32 = np.float32
    out = dict(host_consts())
    for n in ("w_mod", "w_in", "w_conv_out", "w_dn_out", "w_na_out", "w_out", "w_gu", "w_down"):
        out[n] = np.ascontiguousarray(inp[n], dtype=f32)
    vecs = np.zeros((DEPTH, 128, NVEC), f32)
    vecs[:, :, V_BMOD:V_BMOD + 48] = fm(inp["b_mod"], 48)
    vecs[:, :, V_N1:V_N1 + 8] = fm(inp["norm1_g"], 8)
    vecs[:, :, V_N2:V_N2 + 8] = fm(inp["norm2_g"], 8)
    vecs[:, :, V_CDB:V_CDB + 8] = fm(inp["conv_db"], 8)
    vecs[:, :, V_LNG:V_LNG + 8] = fm(inp["conv_ln_g"], 8)
    vecs[:, :, V_LNB:V_LNB + 8] = fm(inp["conv_ln_b"], 8)
    vecs[:, :, V_DNG] = inp["dn_norm_g"]
    out["vecs"] = vecs
    out["conv_dw"] = np.ascontiguousarray(np.transpose(inp["conv_dw"].reshape(DEPTH, 31, 8, 128), (0, 3, 2, 1)), dtype=f32)
    out["dn_conv"] = np.ascontiguousarray(np.transpose(inp["dn_conv"].reshape(DEPTH, 5, 24, 128), (0, 3, 2, 1)), dtype=f32)
    dnp = np.concatenate([inp["dn_a_log"].reshape(DEPTH, 16), inp["dn_dt_bias"].reshape(DEPTH, 16)], -1)
    out["dnp"] = np.ascontiguousarray(np.broadcast_to(dnp[:, None, :], (DEPTH, 128, 32)), dtype=f32)
    rpb = inp["na_rpb"]
    kc = np.arange(64)[:, None]
    qc = np.arange(64)[None, :]
    c0 = np.clip(qc - 8, 0, 48)
    valid = (kc >= c0) & (kc < c0 + 16)
    idx = np.clip(kc - qc + 15, 0, 30)
    tb = rpb[:, :, ::-1, :][:, :, :, idx]
    tb = np.where(valid[None, None, None], tb, f32(NEG))
    out["rpbT"] = np.ascontiguousarray(np.transpose(tb, (0, 3, 1, 2, 4)), dtype=f32)
    out["fin_g"] = fm(inp["final_norm_g"], 8).astype(f32)
    return out


def host_core(inp, b):
    f32 = np.float32
    xT0 = np.concatenate([inp["ctx"][b].T, inp["x"][b].T], axis=1)
    cvec = np.stack([fm(inp["c"][b], 8), fm(inp["c_ctx"], 8)], axis=-1)
    return dict(xT0=np.ascontiguousarray(xT0, dtype=f32), cvec=np.ascontiguousarray(cvec, dtype=f32))


_NC_CACHE = {}


def kernel(**inputs):
    inp = {k: np.asarray(v) for k, v in inputs.items()}
    if "nc" not in _NC_CACHE:
        _NC_CACHE["nc"] = K().build()
    nc = _NC_CACHE["nc"]
    shared = host_shared(inp)
    n = 8
    in_maps = []
    for core in range(n):
        m = dict(shared)
        m.update(host_core(inp, core % 4))
        in_maps.append(m)
    res = run_bass_kernel_spmd(nc, in_maps, core_ids=list(range(n)))
    out = np.stack([np.ascontiguousarray(res.results[b]["yT"].T) for b in range(4)], axis=0)
    return out.astype(np.float32)
```
